# Optimizing a Trainium2 kernel written in Bass

```python
import jax, jax.numpy as jnp
from jax import lax
import numpy as np

D_MODEL = 1024
BATCH = 8
SEQ = 4096
DEPTH = 1

MEM_LEN = 256
HG_HEADS = 4
HG_KEY_DIM = 128
HG_VAL_DIM = 128
HG_KW = HG_HEADS * HG_KEY_DIM
HG_VW = HG_HEADS * HG_VAL_DIM
HG_CHUNK = 64
ATT_Q_HEADS = 8
ATT_KV_HEADS = 2
ATT_GROUP = ATT_Q_HEADS // ATT_KV_HEADS
ATT_HEAD_DIM = 64
ATT_QW = ATT_Q_HEADS * ATT_HEAD_DIM
ATT_KVW = ATT_KV_HEADS * ATT_HEAD_DIM
WINDOW = 128
ATT_BLOCK = 128
ROPE_THETA = 500000.0
ROT_DIM = ATT_HEAD_DIM // 4
MEM_HEADS = 4
MEM_HEAD_DIM = D_MODEL // MEM_HEADS
D_FF = 2816
CONV_WIDTH = 3
NORM_EPS = 1e-6
IN_SPLITS = (HG_KW, HG_KW, HG_KW, HG_VW, HG_VW, ATT_QW, ATT_KVW, ATT_KVW, D_MODEL, D_MODEL)
IN_COLS = sum(IN_SPLITS)
SPLIT_IDX = [int(c) for c in np.cumsum(IN_SPLITS)[:-1]]

kernel_name = 'hybrid_hgrn2_swa_memxattn_convffn_encoder'


def rms_norm(x, g):
    xf = x.astype(jnp.float32)
    y = xf * lax.rsqrt(jnp.mean(xf * xf, axis=-1, keepdims=True) + NORM_EPS) * g.astype(jnp.float32)
    return y.astype(x.dtype)


def partial_rotary(t, positions):
    half = ROT_DIM // 2
    inv_freq = 1.0 / (ROPE_THETA ** (jnp.arange(0, ROT_DIM, 2, dtype=jnp.float32) / ROT_DIM))
    ang = positions.astype(jnp.float32)[..., None] * inv_freq
    cos, sin = jnp.cos(ang)[:, :, None, :], jnp.sin(ang)[:, :, None, :]
    tf = t.astype(jnp.float32)
    t1, t2 = tf[..., :half], tf[..., half:ROT_DIM]
    out = jnp.concatenate([t1 * cos - t2 * sin, t2 * cos + t1 * sin, tf[..., ROT_DIM:]], axis=-1)
    return out.astype(t.dtype)


def gla_chunkwise(q, k, log_f, v):
    B, S, H, K = q.shape
    V = v.shape[-1]
    n_chunks = S // HG_CHUNK

    def chunks(t):
        return t.reshape(B, n_chunks, HG_CHUNK, H, t.shape[-1]).transpose(1, 0, 3, 2, 4)

    incl = jnp.tril(jnp.ones((HG_CHUNK, HG_CHUNK), dtype=bool))[:, :, None]

    def step(state, inp):
        qc, kc, gc, vc = inp
        b = jnp.cumsum(gc, axis=2)
        o_inter = jnp.einsum('bhtk,bhkv->bhtv', qc * jnp.exp(b), state)
        decay = jnp.exp(jnp.where(incl, b[:, :, :, None, :] - b[:, :, None, :, :], -jnp.inf))
        scores = jnp.einsum('bhtk,bhtsk->bhts', qc, decay * kc[:, :, None, :, :])
        o_intra = jnp.einsum('bhts,bhsv->bhtv', scores, vc)
        b_end = b[:, :, -1, :]
        new_state = (jnp.exp(b_end)[..., None] * state
                     + jnp.einsum('bhsk,bhsv->bhkv', kc * jnp.exp(b_end[:, :, None, :] - b), vc))
        return new_state, o_inter + o_intra

    state0 = jnp.zeros((B, H, K, V), jnp.float32)
    _, o = lax.scan(step, state0, (chunks(q), chunks(k), chunks(log_f), chunks(v)))
    return o.transpose(1, 0, 3, 2, 4).reshape(B, S, H, V)


def window_gqa_sink(q, k, v, sink):
    B, S = q.shape[0], q.shape[1]
    nb = S // ATT_BLOCK
    qb = q.reshape(B, nb, ATT_BLOCK, ATT_KV_HEADS, ATT_GROUP, ATT_HEAD_DIM)

    def band(t):
        tp = jnp.pad(t.reshape(B, nb, ATT_BLOCK, ATT_KV_HEADS, ATT_HEAD_DIM),
                     ((0, 0), (1, 1), (0, 0), (0, 0), (0, 0)))
        return jnp.concatenate([tp[:, :-2], tp[:, 1:-1], tp[:, 2:]], axis=2)

    kb, vb = band(k), band(v)
    s = jnp.einsum('bnqhgd,bnkhd->bnhgqk', qb, kb).astype(jnp.float32) * (ATT_HEAD_DIM ** -0.5)
    blk = jnp.arange(nb)[:, None, None]
    qi = blk * ATT_BLOCK + jnp.arange(ATT_BLOCK)[None, :, None]
    kj = (blk - 1) * ATT_BLOCK + jnp.arange(3 * ATT_BLOCK)[None, None, :]
    valid = (jnp.abs(qi - kj) <= WINDOW) & (kj >= 0) & (kj < S)
    s = jnp.where(valid[None, :, None, None], s, -jnp.inf)
    sk = sink.astype(jnp.float32).reshape(ATT_KV_HEADS, ATT_GROUP)[None, None, :, :, None, None]
    m = jnp.maximum(jnp.max(s, axis=-1, keepdims=True), sk)
    p = jnp.exp(s - m)
    p = p / (jnp.sum(p, axis=-1, keepdims=True) + jnp.exp(sk - m))
    o = jnp.einsum('bnhgqk,bnkhd->bnqhgd', p.astype(v.dtype), vb)
    return o.reshape(B, S, ATT_QW)


def token_mixers(n, positions, w_in, lb_fwd, lb_bwd, hg_norm, sink, w_br_rec, w_br_att, w_out):
    B, S, _ = n.shape
    f32 = jnp.float32
    proj = n @ w_in
    q_r, fz_f, fz_b, i_r, g_r, q_a, k_a, v_a, gate_r, gate_a = jnp.split(proj, SPLIT_IDX, axis=-1)

    qh = jax.nn.silu(q_r.astype(f32)).reshape(B, S, HG_HEADS, HG_KEY_DIM) * (HG_KEY_DIM ** -0.5)
    vh = i_r.astype(f32).reshape(B, S, HG_HEADS, HG_VAL_DIM)

    def forget(z, lb):
        f = lb + (1.0 - lb) * jax.nn.sigmoid(z.astype(f32))
        return ((1.0 - f).reshape(B, S, HG_HEADS, HG_KEY_DIM),
                jnp.log(f).reshape(B, S, HG_HEADS, HG_KEY_DIM))

    k_f, lf_f = forget(fz_f, lb_fwd)
    k_b, lf_b = forget(fz_b, lb_bwd)
    o_f = gla_chunkwise(qh, k_f, lf_f, vh)
    o_b = gla_chunkwise(qh[:, ::-1], k_b[:, ::-1], lf_b[:, ::-1], vh[:, ::-1])[:, ::-1]
    o = o_f + o_b
    o = (o * lax.rsqrt(jnp.mean(o * o, axis=-1, keepdims=True) + NORM_EPS)
         * hg_norm.astype(f32).reshape(HG_HEADS, HG_VAL_DIM))
    o = o.reshape(B, S, HG_VW) * jax.nn.silu(g_r.astype(f32))
    y_rec = o.astype(n.dtype) @ w_br_rec

    qa = partial_rotary(q_a.reshape(B, S, ATT_Q_HEADS, ATT_HEAD_DIM), positions)
    ka = partial_rotary(k_a.reshape(B, S, ATT_KV_HEADS, ATT_HEAD_DIM), positions)
    va = v_a.reshape(B, S, ATT_KV_HEADS, ATT_HEAD_DIM)
    y_att = window_gqa_sink(qa, ka, va, sink) @ w_br_att

    merged = jax.nn.sigmoid(gate_r) * y_rec + jax.nn.sigmoid(gate_a) * y_att
    return merged @ w_out


def memory_cross_attention(n, mem_n, w_q, w_kv, w_o):
    B, S, _ = n.shape
    M = mem_n.shape[1]
    q = (n @ w_q).reshape(B, S, MEM_HEADS, MEM_HEAD_DIM)
    k, v = jnp.split(mem_n @ w_kv, 2, axis=-1)
    k = k.reshape(B, M, MEM_HEADS, MEM_HEAD_DIM)
    v = v.reshape(B, M, MEM_HEADS, MEM_HEAD_DIM)
    s = jnp.einsum('bshd,bmhd->bhsm', q, k).astype(jnp.float32) * (MEM_HEAD_DIM ** -0.5)
    p = jax.nn.softmax(s, axis=-1).astype(v.dtype)
    o = jnp.einsum('bhsm,bmhd->bshd', p, v).reshape(B, S, D_MODEL)
    return o @ w_o


def conv_ffn(n, w_up, conv_w, conv_b, w_down):
    S = n.shape[1]
    u, g = jnp.split(n @ w_up, 2, axis=-1)
    pad = CONV_WIDTH // 2
    gp = jnp.pad(g, ((0, 0), (pad, pad), (0, 0)))
    gc = sum(gp[:, j:j + S] * conv_w[j] for j in range(CONV_WIDTH)) + conv_b
    return (jax.nn.silu(gc) * u) @ w_down


def setup_inputs(seed: int = 0) -> dict:
    key = jax.random.key(seed)
    ks = jax.random.split(key, 24)

    def nrm(k, shape, scale):
        return jax.random.normal(k, shape, jnp.float32) * scale

    def gain(k, shape):
        return 1.0 + 0.02 * jax.random.normal(k, shape, jnp.float32)

    L, D = DEPTH, D_MODEL
    return {
        'x': nrm(ks[0], (BATCH, SEQ, D), 1.0),
        'mem': nrm(ks[1], (BATCH, MEM_LEN, D), 1.0),
        'positions': jnp.broadcast_to(jnp.arange(SEQ, dtype=jnp.int32), (BATCH, SEQ)),
        'norm_mix': gain(ks[2], (L, D)),
        'w_in': nrm(ks[3], (L, D, IN_COLS), D ** -0.5),
        'lower_bounds': nrm(ks[4], (2, L + 1, HG_KW), 0.5),
        'hg_norm': gain(ks[5], (L, HG_VW)),
        'attn_sink': nrm(ks[6], (L, ATT_Q_HEADS), 0.5),
        'w_br_rec': nrm(ks[7], (L, HG_VW, D), HG_VW ** -0.5),
        'w_br_att': nrm(ks[8], (L, ATT_QW, D), ATT_QW ** -0.5),
        'w_mix_out': nrm(ks[9], (L, D, D), D ** -0.5),
        'norm_mem': gain(ks[10], (L, D)),
        'norm_mem_kv': gain(ks[11], (L, D)),
        'w_mem_q': nrm(ks[12], (L, D, D), D ** -0.5),
        'w_mem_kv': nrm(ks[13], (L, D, 2 * D), D ** -0.5),
        'w_mem_o': nrm(ks[14], (L, D, D), D ** -0.5),
        'norm_ffn': gain(ks[15], (L, D)),
        'w_up': nrm(ks[16], (L, D, 2 * D_FF), D ** -0.5),
        'conv_w': nrm(ks[17], (L, CONV_WIDTH, D_FF), CONV_WIDTH ** -0.5),
        'conv_b': nrm(ks[18], (L, D_FF), 0.01),
        'w_down': nrm(ks[19], (L, D_FF, D), D_FF ** -0.5),
        'final_norm': gain(ks[20], (D,)),
    }


def reference(x, mem, positions, norm_mix, w_in, lower_bounds, hg_norm, attn_sink, w_br_rec,
              w_br_att, w_mix_out, norm_mem, norm_mem_kv, w_mem_q, w_mem_kv, w_mem_o, norm_ffn,
              w_up, conv_w, conv_b, w_down, final_norm):
    lb_table = jnp.cumsum(jax.nn.softmax(lower_bounds.astype(jnp.float32), axis=1), axis=1)
    h = x
    for l in range(DEPTH):
        n = rms_norm(h, norm_mix[l])
        h = h + token_mixers(n, positions, w_in[l], lb_table[0, l], lb_table[1, l], hg_norm[l],
                             attn_sink[l], w_br_rec[l], w_br_att[l], w_mix_out[l])
        n = rms_norm(h, norm_mem[l])
        mem_n = rms_norm(mem, norm_mem_kv[l])
        h = h + memory_cross_attention(n, mem_n, w_mem_q[l], w_mem_kv[l], w_mem_o[l])
        n = rms_norm(h, norm_ffn[l])
        h = h + conv_ffn(n, w_up[l], conv_w[l], conv_b[l], w_down[l])
    return rms_norm(h, final_norm)
```

```python
import numpy as np
import ml_dtypes
import concourse.bass as bass
import concourse.mybir as mybir
from concourse.bass_utils import run_bass_kernel_spmd

F32 = mybir.dt.float32
BF16 = mybir.dt.bfloat16
I32 = mybir.dt.int32
AF = mybir.ActivationFunctionType
ALU = mybir.AluOpType
AX = mybir.AxisListType


class Buf:
    __slots__ = ("name", "w", "r", "ld_sem", "ld_cnt", "st_sem", "st_cnt")

    def __init__(self, name):
        self.name = name
        self.w = None
        self.r = {}
        self.ld_sem = None
        self.ld_cnt = 0
        self.st_sem = None
        self.st_cnt = 0


class Sched:
    SEM_LIMIT = 60000

    def __init__(self, nc):
        self.nc = nc
        self.eng = {"pe": nc.tensor, "act": nc.scalar, "dve": nc.vector,
                    "pool": nc.gpsimd, "sp": nc.sync}
        self.sem = {}
        self.cnt = {}
        self.nsem = 0
        for e in ("pe", "act", "dve", "pool"):
            self._new_sem(e)
        self.waited = {e: {} for e in self.eng}
        self.n_ops = {e: 0 for e in self.eng}
        self.n_waits = {e: 0 for e in self.eng}

    def _new_sem(self, e):
        self.nsem += 1
        self.sem[e] = self.nc.alloc_semaphore(f"s_{e}_{self.nsem}")
        self.cnt[e] = 0

    def buf(self, name):
        return Buf(name)

    def _wait(self, e, sem, val):
        w = self.waited[e]
        if w.get(sem, 0) >= val:
            return
        self.eng[e].wait_ge(sem, val)
        w[sem] = val
        self.n_waits[e] += 1

    def _deps(self, e, reads, writes, own):
        strict = e != "pe"
        for b in reads:
            if b.w is not None:
                self._wait(e, *b.w)
        for b in writes:
            if b.w is not None and (strict or b.w[0] is not own):
                self._wait(e, *b.w)
            for s, v in b.r.items():
                if strict or s is not own:
                    self._wait(e, s, v)

    def op(self, e, fn, reads=(), writes=(), inc=True):
        if inc and self.cnt[e] >= self.SEM_LIMIT:
            self._new_sem(e)
        own = self.sem[e]
        self._deps(e, reads, writes, own)
        ins = fn(self.eng[e])
        self.n_ops[e] += 1
        if inc:
            self.cnt[e] += 1
            ins.then_inc(own, 1)
            val = self.cnt[e]
        else:
            val = self.cnt[e] + 1
        for b in reads:
            if b.r.get(own, 0) < val:
                b.r[own] = val
        for b in writes:
            b.w = (own, val)
            b.r = {}
        return ins

    def dma(self, q, out_ap, in_ap, reads=(), writes=(), **kw):
        own = None
        self._deps(q, reads, writes, own)
        ins = self.eng[q].dma_start(out=out_ap, in_=in_ap, **kw)
        self.n_ops[q] += 1
        if writes:
            b = writes[0]
            if b.ld_sem is None or b.ld_cnt >= self.SEM_LIMIT:
                self.nsem += 1
                b.ld_sem = self.nc.alloc_semaphore(f"ld_{b.name}_{self.nsem}")
                b.ld_cnt = 0
            b.ld_cnt += 16
            ins.then_inc(b.ld_sem, 16)
            tok = (b.ld_sem, b.ld_cnt)
            for wb in writes:
                wb.w = tok
                wb.r = {}
            for rb in reads:
                rb.r[tok[0]] = tok[1]
        else:
            b = reads[0]
            if b.st_sem is None or b.st_cnt >= self.SEM_LIMIT:
                self.nsem += 1
                b.st_sem = self.nc.alloc_semaphore(f"st_{b.name}_{self.nsem}")
                b.st_cnt = 0
            b.st_cnt += 16
            ins.then_inc(b.st_sem, 16)
            tok = (b.st_sem, b.st_cnt)
            for rb in reads:
                rb.r[tok[0]] = tok[1]
        return tok

    def wait_tok(self, e, tok):
        self._wait(e, tok[0], tok[1])


D = 1024
KC = 8
T = 512
C = 64
NCH = T // C
MID = 31
H0 = 4
HWID = T + H0
DFF = 2816
JF = DFF // 128
MEM = 256
EPS = 1e-6
SLOT = 4096
NSLOT = 4
CHUNK0 = 4096
ROPE_THETA = 500000.0


def _f_tiles(W):
    Kd, N = W.shape
    return np.ascontiguousarray(W.reshape(Kd // 128, 128, N // 128, 128).transpose(1, 2, 0, 3))


def _t_tiles(W):
    Kd, N = W.shape
    return np.ascontiguousarray(W.reshape(Kd // 128, 128, N).transpose(1, 0, 2))


def _partner(W, nh):
    Wh = W.reshape(W.shape[0], nh, 64)
    P = np.zeros_like(Wh)
    P[:, :, 0:8] = Wh[:, :, 8:16]
    P[:, :, 8:16] = Wh[:, :, 0:8]
    return P.reshape(W.shape)


def build_wall(inp):
    w_in = inp["w_in"][0]
    segs = []

    def add(name, arr):
        segs.append((name, arr.reshape(128, -1)))

    ka = w_in[:, 3072:3200]
    kdup = np.concatenate([ka[:, 0:64], ka[:, 0:64], ka[:, 64:128], ka[:, 64:128]], axis=1)
    kdup_p = _partner(kdup, 4)
    add("qr", _f_tiles(w_in[:, 0:512]))
    add("fzb", _f_tiles(w_in[:, 1024:1536]))
    add("ir", _t_tiles(w_in[:, 1536:2048]))
    add("ka4", _f_tiles(np.concatenate([kdup, kdup_p], axis=1)))
    add("va", _t_tiles(w_in[:, 3200:3328]))
    g0 = len(segs)
    add("fzf", _f_tiles(w_in[:, 512:1024]))
    add("gr", _f_tiles(w_in[:, 2048:2560]))
    qa = w_in[:, 2560:3072]
    qa_t = _f_tiles(qa)
    qp_t = _f_tiles(_partner(qa, 8))
    for jj in range(2):
        add(f"qq{jj}", np.concatenate([qa_t[:, 2 * jj].reshape(128, -1), qp_t[:, 2 * jj].reshape(128, -1),
                                       qa_t[:, 2 * jj + 1].reshape(128, -1), qp_t[:, 2 * jj + 1].reshape(128, -1)], axis=1))
    gr_t = _f_tiles(w_in[:, 3328:4352])
    ga_t = _f_tiles(w_in[:, 4352:5376])
    br_t = _f_tiles(inp["w_br_rec"][0])
    ba_t = _f_tiles(inp["w_br_att"][0])
    for j in range(8):
        add(f"mg{j}", np.concatenate([gr_t[:, j].reshape(128, -1), ga_t[:, j].reshape(128, -1),
                                      br_t[:, j].reshape(128, -1), ba_t[:, j].reshape(128, -1)], axis=1))
    wo = _f_tiles(inp["w_mix_out"][0])
    add("wout0", wo[:, 0:4]); add("wout1", wo[:, 4:8])
    g1 = len(segs)
    wq = _f_tiles(inp["w_mem_q"][0])
    add("wmq0", wq[:, 0:4]); add("wmq1", wq[:, 4:8])
    wk = _f_tiles(inp["w_mem_kv"][0][:, 0:1024])
    add("wmk0", wk[:, 0:4]); add("wmk1", wk[:, 4:8])
    add("wmv0", _t_tiles(inp["w_mem_kv"][0][:, 1024:1536]))
    add("wmv1", _t_tiles(inp["w_mem_kv"][0][:, 1536:2048]))
    wmo = _f_tiles(inp["w_mem_o"][0])
    add("wmo0", wmo[:, 0:4]); add("wmo1", wmo[:, 4:8])
    g2 = len(segs)
    wu = _f_tiles(inp["w_up"][0][:, 0:DFF])
    wg = _f_tiles(inp["w_up"][0][:, DFF:2 * DFF])
    for jj in range(JF // 2):
        add(f"up{jj}", np.concatenate([wu[:, 2 * jj].reshape(128, -1), wg[:, 2 * jj].reshape(128, -1),
                                       wu[:, 2 * jj + 1].reshape(128, -1), wg[:, 2 * jj + 1].reshape(128, -1)], axis=1))
    wd = _f_tiles(inp["w_down"][0])
    for j in range(8):
        add(f"dn{j}", wd[:, j])
    g3 = len(segs)
    index = {}
    off = 0
    for name, arr in segs:
        assert arr.shape[1] <= SLOT, (name, arr.shape)
        index[name] = (off, arr.shape[1])
        off += arr.shape[1]
    wall = np.ascontiguousarray(np.concatenate([a for _, a in segs], axis=1).astype(np.float32))
    bounds = [index[segs[g - 1][0]][0] + index[segs[g - 1][0]][1] for g in (g0, g1, g2, g3)]
    return wall, index, bounds


def wall_index():
    index = {}
    off = 0
    order = ([("qr", 4096), ("fzb", 4096), ("ir", 4096), ("ka4", 4096), ("va", 1024)],
             [("fzf", 4096), ("gr", 4096), ("qq0", 4096), ("qq1", 4096)] + [(f"mg{j}", 3072) for j in range(8)]
             + [("wout0", 4096), ("wout1", 4096)],
             [("wmq0", 4096), ("wmq1", 4096), ("wmk0", 4096), ("wmk1", 4096), ("wmv0", 4096), ("wmv1", 4096),
              ("wmo0", 4096), ("wmo1", 4096)],
             [(f"up{j}", 4096) for j in range(JF // 2)] + [(f"dn{j}", 2816) for j in range(8)])
    bounds = []
    for grp in order:
        for name, L in grp:
            index[name] = (off, L)
            off += L
        bounds.append(off)
    return index, bounds, off


def build_params(inp):
    cols = []
    names = {}

    def add(name, arr):
        arr = np.asarray(arr, np.float32).reshape(128, -1)
        names[name] = (sum(c.shape[1] for c in cols), arr.shape[1])
        cols.append(arr)

    def pk(v):
        v = np.asarray(v, np.float32).reshape(-1, 128)
        return np.ascontiguousarray(v.T)

    add("g_mix", pk(inp["norm_mix"][0]))
    add("g_mem", pk(inp["norm_mem"][0]))
    add("g_memkv", pk(inp["norm_mem_kv"][0]))
    add("g_ffn", pk(inp["norm_ffn"][0]))
    add("g_fin", pk(inp["final_norm"]))
    lbr = inp["lower_bounds"]
    add("lb_raw", np.stack([pk(lbr[d, s]) for d in range(2) for s in range(2)], axis=1))
    add("hgn", pk(inp["hg_norm"][0]))
    add("sink", np.broadcast_to(np.asarray(inp["attn_sink"][0], np.float32)[None, :], (128, 8)))
    add("conv_w", np.stack([pk(inp["conv_w"][0][j]) for j in range(3)], axis=1))
    add("conv_b", pk(inp["conv_b"][0]))
    par = np.ascontiguousarray(np.concatenate(cols, axis=1))
    return par, names


def params_index():
    names = {}
    off = 0
    for n, L in (("g_mix", 8), ("g_mem", 8), ("g_memkv", 8), ("g_ffn", 8), ("g_fin", 8), ("lb_raw", 16),
                 ("hgn", 4), ("sink", 8), ("conv_w", 66), ("conv_b", 22)):
        names[n] = (off, L)
        off += L
    return names, off


def build_consts():
    cols = {}
    p = np.arange(128)
    ident = np.eye(128, dtype=np.float32)
    s = (p % 64)[:, None]
    t = np.arange(64)[None, :]
    mask_f = (s <= t).astype(np.float32)
    mask_b = (s >= t).astype(np.float32)
    key = p[:, None]
    q = np.arange(128)[None, :]
    band_prev = (key >= q).astype(np.float32)
    band_next = (key <= q).astype(np.float32)
    m01 = np.ones((128, T), np.float32)
    m01[:, ::C] = 0.0
    inv_freq = 1.0 / (ROPE_THETA ** (np.arange(0, 16, 2, dtype=np.float32) / 16.0))
    invf = np.zeros((128, 1), np.float32)
    phase_c = np.full((128, 1), 0.5 * np.pi, np.float32)
    phase_s = np.zeros((128, 1), np.float32)
    for pp in range(128):
        r = pp % 64
        if r < 16:
            invf[pp, 0] = inv_freq[r % 8]
            phase_s[pp, 0] = np.pi if r < 8 else 0.0
    ones = np.ones((128, 128), np.float32)
    order = [("ident", ident), ("mask_f", mask_f), ("mask_b", mask_b), ("band_prev", band_prev),
             ("band_next", band_next), ("m01", m01), ("invf", invf), ("phase_c", phase_c), ("phase_s", phase_s), ("ones", ones)]
    names = {}
    off = 0
    arrs = []
    for n, a in order:
        names[n] = (off, a.shape[1])
        off += a.shape[1]
        arrs.append(a.astype(np.float32))
    return np.ascontiguousarray(np.concatenate(arrs, axis=1)), names


def weight_plan(NT):
    plan = []
    for _ in range(NT):
        plan += ["fzb", "qr", "ir", "ka4", "va"]
    plan += ["wmk0", "wmk1", "wmv0", "wmv1"]
    for _ in range(NT):
        plan += ["fzf", "gr", "qq0", "qq1"]
        plan += [f"mg{j}" for j in range(8)]
        plan += ["wout0", "wout1", "wmq0", "wmq1", "wmo0", "wmo1"]
        plan += [f"up{j}" for j in range(JF // 2)]
        plan += [f"dn{j}" for j in range(8)]
    plan += [f"dn{j}" for j in range(8)]
    return plan


def build_program(SEQ, dbg_taps=(), stop_after=None):
    nc = bass.Bass("TRN2", target_bir_lowering=False)
    S = Sched(nc)
    NT = SEQ // T
    NBLK = SEQ // 128
    widx, wbounds, LTOT = wall_index()
    pidx, NPAR = params_index()
    _, cidx = build_consts()
    NCON = sum(v[1] for v in cidx.values())

    x_d = nc.dram_tensor("x", [SEQ, D], F32, kind="ExternalInput").ap()
    mem_d = nc.dram_tensor("mem", [MEM, D], F32, kind="ExternalInput").ap()
    pos_d = nc.dram_tensor("pos", [1, SEQ], I32, kind="ExternalInput").ap()
    wall_d = nc.dram_tensor("wall", [128, LTOT], F32, kind="ExternalInput").ap()
    par_d = nc.dram_tensor("par", [128, NPAR], F32, kind="ExternalInput").ap()
    con_d = nc.dram_tensor("con", [128, NCON], F32, kind="ExternalInput").ap()
    out_d = nc.dram_tensor("out", [SEQ, D], F32, kind="ExternalOutput").ap()
    wscr_d = nc.dram_tensor("wscr", [128, LTOT], BF16).ap()
    ob_d = nc.dram_tensor("obscr", [128, 4, SEQ], F32).ap()
    nts_d = nc.dram_tensor("ntscr", [NT, 128, KC * T], BF16).ap()
    vs_d = nc.dram_tensor("vscr", [NT, 128, 4 * 512], BF16).ap()
    sqs_d = nc.dram_tensor("sqscr", [NT, 128, 4 * T], BF16).ap()
    dbg_out = {}

    def sb(name, shape, dt):
        return nc.alloc_sbuf_tensor("sb_" + name, shape, dt)

    def bufs(name, n):
        return [S.buf(f"{name}{i}") for i in range(n)]

    con = sb("con", [128, NCON], F32); bcon = S.buf("con")
    par = sb("par", [128, NPAR], F32); bpar = S.buf("par")
    identb = sb("identb", [128, 128], BF16)
    onesD = sb("onesD", [128, 128], BF16)
    onesV = sb("onesV", [128, 128], BF16)
    ones1 = sb("ones1", [128, 128], BF16)
    bandb = sb("bandb", [128, 3, 128], BF16)
    bcst = S.buf("cst")
    sm = sb("sm", [128, 64], F32); bsm = S.buf("sm")
    hT = sb("hT", [128, KC, HWID], F32); bh = bufs("h", KC)
    nT = sb("nT", [128, KC, T], BF16); bn = bufs("n", KC)
    B8 = sb("B8", [128, KC, T], BF16); bB = bufs("B", KC)
    X1 = sb("X1", [128, JF, T], BF16); bX = bufs("X", JF)
    NPF = 6
    PF = sb("PF", [128, NPF, T], F32); bPF = bufs("pf", NPF)
    HG = sb("HG", [128, 3, 4, T], F32); bHG = [bufs(f"hg{i}_", 4) for i in range(3)]
    KT = sb("KT", [128, 2, SEQ], BF16); bKT = bufs("kt", NT)
    VW = 66
    Vst = sb("Vst", [128, NBLK, 2, VW], BF16); bV = bufs("v", NT)
    KmT = sb("KmT", [128, 8, MEM], BF16); bKm = S.buf("KmT")
    Vm = sb("Vm", [128, 2, D], BF16); bVm = S.buf("Vm")
    ktok = sb("ktok", [128, 4, 512], BF16); bktok = bufs("ktok", 4)
    vtok = sb("vtok", [128, 4, 512], BF16); bvtok = bufs("vtok", 4)
    ATm = sb("ATm", [128, 2, 256], BF16); bATm = bufs("atm", 2)
    U = sb("U", [128, 4, 128], F32); bU = bufs("U", 4)
    S16 = sb("S16", [128, 2, 4, 128], BF16); bS16 = [bufs(f"s16_{i}_", 4) for i in range(2)]
    la = sb("la", [128, 4, NCH], F32); bla = S.buf("la")
    lbt = sb("lbt", [128, 4, NCH], F32); blbt = S.buf("lbt")
    refc = sb("refc", [128, 4, NCH], F32); brefc = S.buf("refc")
    lg = sb("lg", [128, 4, NCH], F32); blg = S.buf("lg")
    gam = sb("gam", [128, 4, NCH], F32); bgam = S.buf("gam")
    carry = sb("carry", [128, 2, 4], F32); bcarry = bufs("carry", 2)
    NPT = 4
    PT = sb("PT", [128, NPT, 3, 128], BF16); bPT = bufs("PT", NPT)
    oatok = sb("oatok", [128, 2, 512], BF16); boatok = bufs("oatok", 2)
    dsm = sb("dsm", [128, 2, 8], F32); bdsm = bufs("dsm", 2)
    xt = sb("xt", [128, 2, D], F32); bxt = bufs("xt", 2)
    ot = sb("ot", [128, 2, 512], F32); bot = bufs("ot", 2)
    wring = sb("wring", [128, NSLOT, SLOT], BF16); bring = bufs("ring", NSLOT)
    posi = sb("posi", [128, T], I32); bposi = S.buf("posi")
    rot = sb("rot", [128, 2, T], F32); brot = bufs("rot", 2)
    rstd_t = sb("rstd", [128, T], F32); brstd = S.buf("rstd")
    nint = sb("nint", [128, T], I32); bnint = S.buf("nint")
    gcar = sb("gcar", [128, JF, 2], F32); bgcar = bufs("gcar", JF)
    ucar = sb("ucar", [128, JF], F32); bucar = bufs("ucar", JF)
    NGS = 2
    gs = sb("gs", [128, NGS, T + 2], F32); bgs = bufs("gs", NGS)
    small = sb("small", [128, 4, JF], F32); bsmall = bufs("small", 4)
    actl = sb("actl", [128, JF], BF16); bactl = S.buf("actl")

    PS = [nc.alloc_psum_tensor(f"ps{i}", [128, 512], F32) for i in range(8)]
    bPS = bufs("ps", 8)
    ps_state = {"rot": 0, "held": set()}

    def psum_next():
        while True:
            i = ps_state["rot"] % 8
            ps_state["rot"] += 1
            if i not in ps_state["held"]:
                return PS[i], bPS[i]

    def psum_hold(n):
        res = []
        for _ in range(n):
            while True:
                i = ps_state["rot"] % 8
                ps_state["rot"] += 1
                if i not in ps_state["held"]:
                    break
            ps_state["held"].add(i)
            res.append(i)
        return res

    def psum_release(idx):
        for i in idx:
            ps_state["held"].discard(i)

    pf_state = {"rot": 0}

    def tmpf():
        i = pf_state["rot"] % NPF
        pf_state["rot"] += 1
        return PF[:, i, :], bPF[i]

    ew_state = {"rot": 0}

    def ew():
        return "dve"

    def cs(name, a=0, b=None):
        off, L = cidx[name]
        b = L if b is None else b
        return con[:, off + a: off + b]

    def pr(name, a=0, b=None):
        off, L = pidx[name]
        b = L if b is None else b
        return par[:, off + a: off + b]

    def tap(name, ap, bufl, shape):
        if name not in dbg_taps:
            return
        key = name
        n = 0
        while key in dbg_out:
            n += 1
            key = f"{name}_{n}"
        d = nc.dram_tensor("dbg_" + key, list(shape), ap.dtype, kind="ExternalOutput").ap()
        dbg_out[key] = S.dma("sp", d, ap, reads=list(bufl))

    plan = weight_plan(NT)
    ring = {"issued": 0, "consumed": 0}
    bgrp = bufs("wgrp", 4)

    def grp_of(off):
        for g, b in enumerate(wbounds):
            if off < b:
                return g
        raise AssertionError

    def ring_issue(m):
        name = plan[m]
        off, L = widx[name]
        slot = m % NSLOT
        S.dma("sp", wring[:, slot, 0:L], wscr_d[:, off:off + L], reads=[bgrp[grp_of(off)]], writes=[bring[slot]])

    def ring_next(name):
        n = ring["consumed"]
        assert plan[n] == name, (n, plan[n], name)
        while ring["issued"] < min(len(plan), n + NSLOT):
            ring_issue(ring["issued"])
            ring["issued"] += 1
        ring["consumed"] += 1
        slot = n % NSLOT
        return wring[:, slot, :], bring[slot]

    S.dma("sp", con[:], con_d, writes=[bcon])
    S.dma("sp", par[:], par_d, writes=[bpar])
    S.op("dve", lambda e: e.tensor_copy(identb[:], cs("ident")), reads=[bcon], writes=[bcst])
    S.op("dve", lambda e: e.tensor_scalar(onesD[:], cs("ones"), 1.0 / D, None, ALU.mult), reads=[bcon], writes=[bcst])
    S.op("dve", lambda e: e.tensor_scalar(onesV[:], cs("ones"), 1.0 / 128, None, ALU.mult), reads=[bcon], writes=[bcst])
    S.op("dve", lambda e: e.tensor_copy(ones1[:], cs("ones")), reads=[bcon], writes=[bcst])
    S.op("dve", lambda e: e.tensor_copy(bandb[:, 0, :], cs("band_prev")), reads=[bcon], writes=[bcst])
    S.op("dve", lambda e: e.tensor_copy(bandb[:, 1, :], cs("ones")), reads=[bcon], writes=[bcst])
    S.op("dve", lambda e: e.tensor_copy(bandb[:, 2, :], cs("band_next")), reads=[bcon], writes=[bcst])
    lbr = pr("lb_raw").rearrange("p (d s h) -> p d s h", d=2, s=2)
    S.op("dve", lambda e: e.tensor_tensor(sm[:, 32:40].rearrange("p (d h) -> p d h", d=2), lbr[:, :, 0, :], lbr[:, :, 1, :], ALU.subtract),
         reads=[bpar], writes=[bsm])
    S.op("act", lambda e: e.activation(sm[:, 0:8], sm[:, 32:40], AF.Sigmoid), reads=[bsm], writes=[bsm])
    S.op("dve", lambda e: e.tensor_scalar(sm[:, 8:16], sm[:, 0:8], -1.0, 1.0, ALU.mult, ALU.add), reads=[bsm], writes=[bsm])
    S.op("dve", lambda e: e.tensor_scalar(sm[:, 16:24], sm[:, 0:8], 1.0, -1.0, ALU.mult, ALU.add), reads=[bsm], writes=[bsm])
    S.op("act", lambda e: e.activation(sm[:, 24:32], pr("sink"), AF.Exp), reads=[bpar], writes=[bsm])
    S.op("pool", lambda e: e.memset(sm[:, 40:41], EPS), writes=[bsm])
    S.op("pool", lambda e: e.memset(hT[:], 0.0), writes=bh)
    S.op("pool", lambda e: e.memset(gcar[:], 0.0), writes=bgcar)
    S.op("pool", lambda e: e.memset(ucar[:], 0.0), writes=bucar)
    S.op("pool", lambda e: e.memset(carry[:], 0.0), writes=bcarry)
    S.op("pool", lambda e: e.memset(U[:], 0.0), writes=bU)
    S.op("pool", lambda e: e.memset(Vst[:], 1.0), writes=bV)

    ci = 0
    g_start = 0
    for g, g_end in enumerate(wbounds):
        a = g_start
        while a < g_end:
            b = min(a + CHUNK0, g_end)
            L = b - a
            sl = ci % 2
            S.dma("pool", wscr_d[:, a:b], wall_d[:, a:b], writes=[bgrp[g]])
            ci += 1
            a = b
        g_start = g_end

    xpre = {"tile": None}

    def x_dma(ti, blk):
        tok0 = ti * T
        sl = blk % 2
        S.dma("sp", xt[:, sl, :], x_d[tok0 + 128 * blk: tok0 + 128 * (blk + 1), :], writes=[bxt[sl]])

    def prefetch_x(ti):
        x_dma(ti, 0)
        x_dma(ti, 1)
        xpre["tile"] = ti

    def load_x_tile_gen(ti, nxt=None):
        have = xpre["tile"] == ti
        xpre["tile"] = None
        for blk in range(4):
            sl = blk % 2
            if not (have and blk < 2):
                x_dma(ti, blk)
            for half in range(2):
                ps, bps = psum_next()
                for kk in range(4):
                    k = half * 4 + kk
                    S.op("pe", lambda e, ps=ps, kk=kk, k=k, sl=sl: e.transpose(ps[:, kk * 128:(kk + 1) * 128], xt[:, sl, k * 128:(k + 1) * 128], cs("ident")),
                         reads=[bxt[sl], bcon], writes=[bps], inc=(kk == 3))
                S.op("act", lambda e, ps=ps, half=half, blk=blk: e.copy(
                    hT[:, half * 4:half * 4 + 4, H0 + 128 * blk: H0 + 128 * (blk + 1)],
                    ps[:, :].rearrange("p (a b) -> p a b", a=4)), reads=[bps], writes=bh[half * 4:half * 4 + 4])
            if blk >= 2 and nxt is not None:
                x_dma(nxt, blk - 2)
            if blk == 3 and nxt is not None:
                xpre["tile"] = nxt
            yield

    def load_x_tile(ti, nxt=None):
        for _ in load_x_tile_gen(ti, nxt):
            pass

    def rmsnorm(src, bsrc, gname, N, out, bout, nk=KC, ones=None):
        for _ in rmsnorm_gen(src, bsrc, gname, N, out, bout, nk, ones):
            pass

    def rmsnorm_gen(src, bsrc, gname, N, out, bout, nk=KC, ones=None):
        ones = onesD if ones is None else ones
        for k in range(nk):
            S.op("act", lambda e, k=k: e.activation(B8[:, k, 0:N], src(k), AF.Square), reads=[bsrc[k]], writes=[bB[k]])
        yield
        ps, bps = psum_next()
        for k in range(nk):
            S.op("pe", lambda e, k=k: e.matmul(ps[:, 0:N], ones[:], B8[:, k, 0:N], start=(k == 0), stop=(k == nk - 1)),
                 reads=[bB[k], bcst], writes=[bps], inc=(k == nk - 1))
        yield
        rs, brs = (rstd_t[:, :], brstd) if N == T else tmpf()
        S.op("act", lambda e: e.activation(rs[:, 0:N], ps[:, 0:N], AF.Ln, bias=sm[:, 40:41]), reads=[bps, bsm], writes=[brs])
        S.op("act", lambda e: e.activation(rs[:, 0:N], rs[:, 0:N], AF.Exp, scale=-0.5), reads=[brs], writes=[brs])
        yield
        for k in range(nk):
            S.op("dve", lambda e, k=k: e.scalar_tensor_tensor(out(k), src(k), pr(gname, k, k + 1), rs[:, 0:N], ALU.mult, ALU.mult),
                 reads=[bsrc[k], brs, bpar], writes=[bout[k]])
            if k % 4 == 3:
                yield

    def proj_F(wt, bwt, blk, rhs, brhs, N, nk=KC, wstride=None):
        ps, bps = psum_next()
        for k in range(nk):
            o = (blk * nk + k) * 128
            S.op("pe", lambda e, o=o, k=k: e.matmul(ps[:, 0:N], wt[:, o:o + 128], rhs(k), start=(k == 0), stop=(k == nk - 1)),
                 reads=[bwt, brhs[k]], writes=[bps], inc=(k == nk - 1))
        return ps, bps

    nTk = lambda k: nT[:, k, :]

    LF, KK, BQ = 0, 1, 2
    bsqs = bufs("sqs", NT)
    bvs = bufs("vs", NT)
    bnts = bufs("nts", NT)

    def hgrn_tile(ti, d, first, mid_hook=None, bg=None):
        tok0 = ti * T
        wt, bwt = ring_next("fzf" if d == 0 else "fzb")
        sgs = []
        for hd in range(4):
            ps, bps = proj_F(wt, bwt, hd, nTk, bn, T)
            S.op("act", lambda e, hd=hd: e.activation(HG[:, KK, hd, :], ps[:, :], AF.Sigmoid), reads=[bps], writes=[bHG[KK][hd]])
            sgs.append((HG[:, KK, hd, :], bHG[KK][hd]))
        if d == 1:
            wt, bwt = ring_next("qr")
            for hd in range(4):
                ps, bps = proj_F(wt, bwt, hd, nTk, bn, T)
                S.op("act", lambda e, hd=hd: e.activation(X1[:, 8 + hd, :], ps[:, :], AF.Silu), reads=[bps], writes=[bX[8 + hd]])
            S.dma("sp", sqs_d[ti], X1[:, 8:12, :], reads=bX[8:12], writes=[bsqs[ti]])
        else:
            S.dma("sp", X1[:, 0:4, :], sqs_d[ti], reads=[bsqs[ti]], writes=bX[0:4])
        if d == 1:
            wt, bwt = ring_next("ir")
            for blk in range(4):
                ps, bps = psum_next()
                for k in range(KC):
                    S.op("pe", lambda e, k=k, blk=blk: e.matmul(ps[:, :], nT[:, k, blk * 128:(blk + 1) * 128], wt[:, k * 512:(k + 1) * 512],
                                                                start=(k == 0), stop=(k == KC - 1)),
                         reads=[bwt, bn[k]], writes=[bps], inc=(k == KC - 1))
                S.op("act", lambda e, blk=blk: e.copy(vtok[:, blk, :], ps[:, :]), reads=[bps], writes=[bvtok[blk]])
            S.dma("sp", vs_d[ti], vtok[:], reads=bvtok, writes=[bvs[ti]])
        else:
            S.dma("sp", vtok[:], vs_d[ti], reads=[bvs[ti]], writes=bvtok)
        if mid_hook is not None:
            mid_hook()
        for hd in range(4):
            sg, bsg = sgs[hd]
            c = d * 4 + hd
            S.op("act", lambda e, c=c, hd=hd: e.activation(HG[:, LF, hd, :], sg, AF.Ln, scale=sm[:, 8 + c:9 + c], bias=sm[:, c:c + 1]),
                 reads=[bsg, bsm], writes=[bHG[LF][hd]])
            S.op("dve", lambda e, c=c, hd=hd: e.tensor_scalar(HG[:, KK, hd, :], sg, sm[:, 16 + c:17 + c], sm[:, 8 + c:9 + c], ALU.mult, ALU.add),
                 reads=[bsg, bsm], writes=[bHG[KK][hd]])
        tap("lf", HG[:, LF], bHG[LF], [128, 4, T])
        tap("kk", HG[:, KK], bHG[KK], [128, 4, T])
        flat = lambda i: HG[:, i].rearrange("p h t -> p (h t)")
        ch = lambda i: HG[:, i].rearrange("p h (c s) -> p h c s", s=C)
        for hd in range(4):
            S.op("dve", lambda e, hd=hd: e.tensor_tensor_scan(HG[:, BQ, hd, :], cs("m01"), HG[:, LF, hd, :], 0.0, ALU.mult, ALU.add),
                 reads=[bHG[LF][hd], bcon], writes=[bHG[BQ][hd]])
        tap("bq", HG[:, BQ], bHG[BQ], [128, 4, T])
        if d == 0:
            S.op("dve", lambda e: e.tensor_copy(la[:], ch(BQ)[:, :, :, MID]), reads=bHG[BQ], writes=[bla])
            S.op("dve", lambda e: e.tensor_tensor(lbt[:], ch(BQ)[:, :, :, C - 1], ch(BQ)[:, :, :, MID], ALU.subtract), reads=bHG[BQ], writes=[blbt])
            S.op("dve", lambda e: e.tensor_copy(refc[:], ch(BQ)[:, :, :, MID]), reads=bHG[BQ], writes=[brefc])
            S.op("dve", lambda e: e.tensor_tensor(ch(BQ), ch(BQ), refc[:].unsqueeze(3).to_broadcast([128, 4, NCH, C]), ALU.subtract),
                 reads=bHG[BQ] + [brefc], writes=bHG[BQ])
            S.op("act", lambda e: e.activation(flat(LF), flat(BQ), AF.Exp), reads=bHG[BQ], writes=bHG[LF])
            S.op("act", lambda e: e.activation(flat(BQ), flat(BQ), AF.Exp, scale=-1.0), reads=bHG[BQ], writes=bHG[BQ])
        else:
            S.op("dve", lambda e: e.tensor_tensor(flat(LF), flat(BQ), flat(LF), ALU.subtract), reads=bHG[BQ] + bHG[LF], writes=bHG[LF])
            S.op("dve", lambda e: e.tensor_tensor(la[:], ch(BQ)[:, :, :, C - 1], ch(LF)[:, :, :, MID], ALU.subtract), reads=bHG[BQ] + bHG[LF], writes=[bla])
            S.op("dve", lambda e: e.tensor_copy(lbt[:], ch(LF)[:, :, :, MID]), reads=bHG[LF], writes=[blbt])
            S.op("dve", lambda e: e.tensor_copy(refc[:], ch(LF)[:, :, :, MID]), reads=bHG[LF], writes=[brefc])
            S.op("dve", lambda e: e.tensor_tensor(ch(LF), ch(LF), refc[:].unsqueeze(3).to_broadcast([128, 4, NCH, C]), ALU.subtract),
                 reads=bHG[LF] + [brefc], writes=bHG[LF])
            S.op("act", lambda e: e.activation(flat(BQ), flat(LF), AF.Exp), reads=bHG[LF], writes=bHG[BQ])
            S.op("act", lambda e: e.activation(flat(LF), flat(LF), AF.Exp, scale=-1.0), reads=bHG[LF], writes=bHG[LF])
        if d == 0:
            S.op("dve", lambda e: e.tensor_tensor(lg[:, :, 1:NCH], la[:, :, 1:NCH], lbt[:, :, 0:NCH - 1], ALU.add), reads=[bla, blbt], writes=[blg])
            S.op("dve", lambda e: e.tensor_tensor(lg[:, :, 0:1], la[:, :, 0:1], carry[:, 0, :].unsqueeze(2), ALU.add), reads=[bla, bcarry[0]], writes=[blg])
            S.op("dve", lambda e: e.tensor_copy(carry[:, 0, :].unsqueeze(2), lbt[:, :, NCH - 1:NCH]), reads=[blbt], writes=[bcarry[0]])
        else:
            S.op("dve", lambda e: e.tensor_tensor(lg[:, :, 0:NCH - 1], la[:, :, 0:NCH - 1], lbt[:, :, 1:NCH], ALU.add), reads=[bla, blbt], writes=[blg])
            S.op("dve", lambda e: e.tensor_tensor(lg[:, :, NCH - 1:NCH], la[:, :, NCH - 1:NCH], carry[:, 1, :].unsqueeze(2), ALU.add), reads=[bla, bcarry[1]], writes=[blg])
            S.op("dve", lambda e: e.tensor_copy(carry[:, 1, :].unsqueeze(2), lbt[:, :, 0:1]), reads=[blbt], writes=[bcarry[1]])
        S.op("act", lambda e: e.activation(gam[:], lg[:], AF.Exp), reads=[blg], writes=[bgam])
        for hd in range(4):
            S.op(ew(), lambda e, hd=hd: e.tensor_tensor(X1[:, 4 + hd, :], HG[:, KK, hd, :], HG[:, BQ, hd, :], ALU.mult),
                 reads=[bHG[KK][hd], bHG[BQ][hd]], writes=[bX[4 + hd]])
        for hd in range(4):
            src = 8 + hd if d == 1 else hd
            S.op("dve", lambda e, hd=hd, src=src: e.scalar_tensor_tensor(X1[:, hd, :], X1[:, src, :], float(128 ** -0.5), HG[:, LF, hd, :], ALU.mult, ALU.mult),
                 reads=[bX[src], bHG[LF][hd]], writes=[bX[hd]])
        tap("qt", X1[:, 0:4, :], bX[0:4], [128, 4, T])
        tap("kt", X1[:, 4:8, :], bX[4:8], [128, 4, T])
        tap("gam", gam[:], [bgam], [128, 4, NCH])
        for blk in range(4):
            ps, bps = psum_next()
            psb = ps[:, :].bitcast(BF16)
            for hd in range(4):
                S.op("pe", lambda e, hd=hd, blk=blk: e.transpose(psb[:, hd * 128:(hd + 1) * 128], X1[:, 4 + hd, blk * 128:(blk + 1) * 128], identb[:]),
                     reads=[bX[4 + hd], bcst], writes=[bps], inc=(hd == 3))
            S.op("dve", lambda e, blk=blk: e.tensor_copy(ktok[:, blk, :], psb[:, 0:512]), reads=[bps], writes=[bktok[blk]])
        held = psum_hold(4)
        order = range(NCH) if d == 0 else range(NCH - 1, -1, -1)
        mask = cs("mask_f") if d == 0 else cs("mask_b")
        order = list(order)

        def emit_AT(ci_):
            c = order[ci_]
            p0 = 64 * (c % 2)
            cols = slice(c * C, (c + 1) * C)
            ab = ci_ % 2
            psA, bpsA = psum_next()
            for hd in range(4):
                S.op("pe", lambda e, hd=hd: e.matmul(psA[p0:p0 + 64, hd * 64:(hd + 1) * 64], X1[:, 4 + hd, cols], X1[:, hd, cols],
                                                     start=True, stop=True, tile_position=(0, p0)),
                     reads=[bX[4 + hd], bX[hd]], writes=[bpsA], inc=(hd == 3))
            S.op("dve", lambda e: e.tensor_tensor(ATm[p0:p0 + 64, ab, :].rearrange("p (h t) -> p h t", h=4),
                                                  psA[p0:p0 + 64, 0:256].rearrange("p (h t) -> p h t", h=4),
                                                  mask[p0:p0 + 64, :].unsqueeze(1).to_broadcast([64, 4, C]), ALU.mult),
                 reads=[bpsA, bcon], writes=[bATm[ab]])

        emit_AT(0)
        for ci_, c in enumerate(order):
            blk, hf = c // 2, c % 2
            p0 = 64 * hf
            cols = slice(c * C, (c + 1) * C)
            very_first = first and ci_ == 0
            sb_ = ci_ % 2
            ab = ci_ % 2
            if not very_first:
                for hd in range(4):
                    S.op("act", lambda e, hd=hd: e.activation(S16[:, sb_, hd, :], U[:, hd, :], AF.Copy, scale=gam[:, hd, c:c + 1]),
                         reads=[bU[hd], bgam], writes=[bS16[sb_][hd]])
            psU, bpsU = psum_next()
            for hd in range(4):
                S.op("pe", lambda e, hd=hd: e.matmul(psU[:, hd * 128:(hd + 1) * 128], ktok[p0:p0 + 64, blk, hd * 128:(hd + 1) * 128],
                                                     vtok[p0:p0 + 64, blk, hd * 128:(hd + 1) * 128], start=True, stop=True),
                     reads=[bktok[blk], bvtok[blk]], writes=[bpsU], inc=(hd == 3))
            if ci_ + 1 < NCH:
                emit_AT(ci_ + 1)
            for hd in range(4):
                pso, bpso = PS[held[hd]], bPS[held[hd]]
                S.op("pe", lambda e, hd=hd, pso=pso: e.matmul(pso[:, cols], vtok[p0:p0 + 64, blk, hd * 128:(hd + 1) * 128],
                                                              ATm[p0:p0 + 64, ab, hd * 64:(hd + 1) * 64], start=True, stop=very_first),
                     reads=[bvtok[blk], bATm[ab]], writes=[bpso], inc=very_first)
                if not very_first:
                    S.op("pe", lambda e, hd=hd, pso=pso: e.matmul(pso[:, cols], S16[:, sb_, hd, :], X1[:, hd, cols], start=False, stop=True),
                         reads=[bS16[sb_][hd], bX[hd]], writes=[bpso])
            for hd in range(4):
                if very_first:
                    S.op("dve", lambda e, hd=hd: e.tensor_copy(U[:, hd, :], psU[:, hd * 128:(hd + 1) * 128]), reads=[bpsU], writes=[bU[hd]])
                else:
                    S.op("dve", lambda e, hd=hd: e.scalar_tensor_tensor(U[:, hd, :], U[:, hd, :], gam[:, hd, c:c + 1], psU[:, hd * 128:(hd + 1) * 128],
                                                                        ALU.mult, ALU.add),
                         reads=[bU[hd], bgam, bpsU], writes=[bU[hd]])
            if bg is not None:
                next(bg, None)
                next(bg, None)
        if bg is not None:
            for _ in bg:
                pass
        return held

    bob = bufs("ob", NT)
    def p1_prep_gen(ti):
        yield from load_x_tile_gen(ti, nxt=(ti - 1 if ti > 0 else None))
        yield from rmsnorm_gen(lambda k: hT[:, k, H0:H0 + T], bh, "g_mix", T, nTk, bn)
        S.dma("sp", nts_d[ti], nT[:], reads=bn, writes=[bnts[ti]])
        yield

    def p1_prep(ti):
        for _ in p1_prep_gen(ti):
            pass

    for idx, ti in enumerate(range(NT - 1, -1, -1)):
        tok0 = ti * T
        p1_prep(ti)
        S.dma("sp", posi[:], pos_d[:, tok0:tok0 + T].partition_broadcast(128), writes=[bposi])
        posf, bposf = tmpf()
        S.op("dve", lambda e: e.tensor_copy(posf, posi[:]), reads=[bposi], writes=[bposf])
        ct, bct = rot[:, 0, :], brot[0]
        sn, bsn = rot[:, 1, :], brot[1]
        for tb, btb, ph in ((ct, bct, "phase_c"), (sn, bsn, "phase_s")):
            S.op("dve", lambda e, tb=tb, ph=ph: e.tensor_scalar(tb, posf, cs("invf"), cs(ph), ALU.mult, ALU.add), reads=[bposf, bcon], writes=[btb])
            S.op("dve", lambda e, tb=tb: e.tensor_scalar(nint[:], tb, float(1.0 / (2 * np.pi)), None, ALU.mult), reads=[btb], writes=[bnint])
            nf_, bnf_ = tmpf()
            S.op("dve", lambda e: e.tensor_copy(nf_, nint[:]), reads=[bnint], writes=[bnf_])
            S.op("dve", lambda e, tb=tb: e.scalar_tensor_tensor(tb, nf_, float(-2 * np.pi), tb, ALU.mult, ALU.add), reads=[bnf_, btb], writes=[btb])
            S.op("dve", lambda e, tb=tb: e.tensor_scalar(tb, tb, -3.1415925, 3.1415925, ALU.max, ALU.min), reads=[btb], writes=[btb])
            S.op("act", lambda e, tb=tb: e.activation(tb, tb, AF.Sin), reads=[btb], writes=[btb])

        def kv_hook(ti=ti, tok0=tok0):
            wt, bwt = ring_next("ka4")
            for kvh in range(2):
                ps1, bps1 = proj_F(wt, bwt, kvh, nTk, bn, T)
                ps2, bps2 = proj_F(wt, bwt, 2 + kvh, nTk, bn, T)
                t1, bt1 = tmpf()
                t2, bt2 = tmpf()
                S.op("dve", lambda e: e.tensor_tensor(t1, ps1[:, :], ct, ALU.mult), reads=[bps1, bct], writes=[bt1])
                S.op("dve", lambda e: e.tensor_tensor(t2, ps2[:, :], sn, ALU.mult), reads=[bps2, bsn], writes=[bt2])
                S.op("dve", lambda e, kvh=kvh: e.tensor_tensor(KT[:, kvh, tok0:tok0 + T], t1, t2, ALU.add), reads=[bt1, bt2], writes=[bKT[ti]])
            wt, bwt = ring_next("va")
            for blk in range(4):
                ps, bps = psum_next()
                for k in range(KC):
                    S.op("pe", lambda e, k=k, blk=blk: e.matmul(ps[:, 0:128], nT[:, k, blk * 128:(blk + 1) * 128], wt[:, k * 128:(k + 1) * 128],
                                                                start=(k == 0), stop=(k == KC - 1)),
                         reads=[bwt, bn[k]], writes=[bps], inc=(k == KC - 1))
                S.op("act", lambda e, blk=blk: e.copy(Vst[:, ti * 4 + blk, :, 0:64], ps[:, 0:128].rearrange("p (a b) -> p a b", a=2)),
                     reads=[bps], writes=[bV[ti]])
        held = hgrn_tile(ti, 1, idx == 0, mid_hook=kv_hook)
        for hd in range(4):
            S.op("act", lambda e, hd=hd: e.copy(HG[:, KK, hd, :], PS[held[hd]][:, :]), reads=[bPS[held[hd]]], writes=[bHG[KK][hd]])
        psum_release(held)
        S.dma("sp", ob_d[:, :, tok0:tok0 + T], HG[:, KK], reads=bHG[KK], writes=[bob[ti]])
        if idx == 0:
            tap("ob", HG[:, KK], bHG[KK], [128, 4, T])
    if stop_after == "phase1":
        tap("KT", KT[:], bKT, [128, 2, SEQ])
        tap("Vst", Vst[:], bV, [128, NBLK, 2, VW])
        return finish(nc, S, dbg_out)

    for half in range(2):
        S.dma("sp", xt[:, half, :], mem_d[128 * half:128 * (half + 1), :], writes=[bxt[half]])
        for h2 in range(2):
            ps, bps = psum_next()
            for kk in range(4):
                k = h2 * 4 + kk
                S.op("pe", lambda e, kk=kk, k=k: e.transpose(ps[:, kk * 128:(kk + 1) * 128], xt[:, half, k * 128:(k + 1) * 128], cs("ident")),
                     reads=[bxt[half], bcon], writes=[bps], inc=(kk == 3))
            S.op("act", lambda e: e.copy(hT[:, h2 * 4:h2 * 4 + 4, H0 + 128 * half: H0 + 128 * (half + 1)],
                                         ps[:, :].rearrange("p (a b) -> p a b", a=4)), reads=[bps], writes=bh[h2 * 4:h2 * 4 + 4])
    rmsnorm(lambda k: hT[:, k, H0:H0 + MEM], bh, "g_memkv", MEM, lambda k: nT[:, k, 0:MEM], bn)
    for w2 in range(2):
        wt, bwt = ring_next(f"wmk{w2}")
        for jb in range(4):
            j = w2 * 4 + jb
            ps, bps = proj_F(wt, bwt, jb, lambda k: nT[:, k, 0:MEM], bn, MEM)
            S.op("act", lambda e, j=j: e.copy(KmT[:, j, :], ps[:, 0:MEM]), reads=[bps], writes=[bKm])
    for w2 in range(2):
        wt, bwt = ring_next(f"wmv{w2}")
        for mc in range(2):
            ps, bps = psum_next()
            for k in range(KC):
                S.op("pe", lambda e, k=k: e.matmul(ps[:, :], nT[:, k, mc * 128:(mc + 1) * 128], wt[:, k * 512:(k + 1) * 512],
                                                   start=(k == 0), stop=(k == KC - 1)),
                     reads=[bwt, bn[k]], writes=[bps], inc=(k == KC - 1))
            S.op("act", lambda e: e.copy(Vm[:, mc, w2 * 512:(w2 + 1) * 512], ps[:, :]), reads=[bps], writes=[bVm])
    prefetch_x(0)
    tap("KmT", KmT[:], [bKm], [128, 8, MEM])
    tap("Vm", Vm[:], [bVm], [128, 2, D])

    OG, QR, OA = 8, 12, 16
    final_toks = []
    hwin = lambda k, N=T: hT[:, k, H0 - 1:H0 - 1 + N]

    def ffn_finish(ti, N, c0, row0, skip_first):
        rmsnorm(lambda k: hT[:, k, c0:c0 + N], bh, "g_fin", N, lambda k: hT[:, k, c0:c0 + N], bh)
        nb = (N + 127) // 128
        for blk in range(nb):
            w = min(128, N - 128 * blk)
            sl = blk % 2
            r0 = row0 + 128 * blk
            for half in range(2):
                ps, bps = psum_next()
                for kk in range(4):
                    k = half * 4 + kk
                    S.op("pe", lambda e, kk=kk, k=k: e.transpose(ps[0:w, kk * 128:(kk + 1) * 128], hT[:, k, c0 + 128 * blk: c0 + 128 * blk + w], cs("ident")),
                         reads=[bh[k], bcon], writes=[bps], inc=(kk == 3))
                S.op("act" if half == 0 else "dve",
                     (lambda e: e.copy(ot[0:w, half, :], ps[0:w, :])) if half == 0 else
                     (lambda e: e.tensor_copy(ot[0:w, half, :], ps[0:w, :])),
                     reads=[bps], writes=[bot[half]])
                cols = slice(half * 512, (half + 1) * 512)
                if skip_first and blk == 0:
                    tok = S.dma("sp", out_d[r0 + 1:r0 + w, cols], ot[1:w, half, :], reads=[bot[half]])
                else:
                    tok = S.dma("sp", out_d[r0:r0 + w, cols], ot[0:w, half, :], reads=[bot[half]])
                final_toks.append(tok)

    for ti in range(NT):
        tok0 = ti * T
        if ti == 0:
            S.dma("sp", nT[:], nts_d[0], reads=[bnts[0]], writes=bn)
        load_x_tile(ti, nxt=(ti + 1 if ti + 1 < NT else None))
        S.dma("sp", posi[:], pos_d[:, tok0:tok0 + T].partition_broadcast(128), writes=[bposi])
        posf, bposf = tmpf()
        S.op("dve", lambda e: e.tensor_copy(posf, posi[:]), reads=[bposi], writes=[bposf])
        ct, bct = rot[:, 0, :], brot[0]
        sn, bsn = rot[:, 1, :], brot[1]
        for tb, btb, ph in ((ct, bct, "phase_c"), (sn, bsn, "phase_s")):
            S.op("dve", lambda e, tb=tb, ph=ph: e.tensor_scalar(tb, posf, cs("invf"), cs(ph), ALU.mult, ALU.add), reads=[bposf, bcon], writes=[btb])
            S.op("dve", lambda e, tb=tb: e.tensor_scalar(nint[:], tb, float(1.0 / (2 * np.pi)), None, ALU.mult), reads=[btb], writes=[bnint])
            nf_, bnf_ = tmpf()
            S.op("dve", lambda e: e.tensor_copy(nf_, nint[:]), reads=[bnint], writes=[bnf_])
            S.op("dve", lambda e, tb=tb: e.scalar_tensor_tensor(tb, nf_, float(-2 * np.pi), tb, ALU.mult, ALU.add), reads=[bnf_, btb], writes=[btb])
            S.op("dve", lambda e, tb=tb: e.tensor_scalar(tb, tb, -3.1415925, 3.1415925, ALU.max, ALU.min), reads=[btb], writes=[btb])
            S.op("act", lambda e, tb=tb: e.activation(tb, tb, AF.Sin), reads=[btb], writes=[btb])

        def attention_qb(qb):
            gb = ti * 4 + qb
            kbs = [kb for kb in (gb - 1, gb, gb + 1) if 0 <= kb < NBLK]
            s0 = kbs[0] - (gb - 1)
            ns = len(kbs)
            pso2 = [psum_hold(1)[0], psum_hold(1)[0]]
            def att_scores(h):
                kvh, j, p0 = h // 4, h // 2, 64 * (h % 2)
                ps, bps = psum_next()
                for kb in kbs:
                    s_ = kb - (gb - 1)
                    S.op("pe", lambda e, s_=s_, kb=kb: e.matmul(ps[:, s_ * 128:(s_ + 1) * 128], KT[p0:p0 + 64, kvh, kb * 128:(kb + 1) * 128],
                                                                X1[p0:p0 + 64, QR + j, qb * 128:(qb + 1) * 128], start=True, stop=True),
                         reads=[bKT[kb // 4], bX[QR + j]], writes=[bps], inc=(kb == kbs[-1]))
                pi_ = (qb * 8 + h) % NPT
                S.op("act", lambda e: e.activation(PT[:, pi_, s0:s0 + ns, :], ps[:, s0 * 128:(s0 + ns) * 128].rearrange("p (a b) -> p a b", a=ns),
                                                   AF.Exp, scale=0.125), reads=[bps], writes=[bPT[pi_]])
                S.op("dve", lambda e: e.tensor_tensor(PT[:, pi_, s0:s0 + ns, :], PT[:, pi_, s0:s0 + ns, :], bandb[:, s0:s0 + ns, :], ALU.mult),
                     reads=[bPT[pi_], bcst], writes=[bPT[pi_]])

            def att_pv(h):
                kvh = h // 4
                pi_ = (qb * 8 + h) % NPT
                po, bpo = PS[pso2[h // 4]], bPS[pso2[h // 4]]
                hh = h % 4
                for kb in kbs:
                    s_ = kb - (gb - 1)
                    S.op("pe", lambda e, s_=s_, kb=kb: e.matmul(po[:, hh * 65:hh * 65 + 65], PT[:, pi_, s_, :], Vst[:, kb, kvh, 0:65],
                                                                start=(kb == kbs[0]), stop=(kb == kbs[-1])),
                         reads=[bPT[pi_], bV[kb // 4]], writes=[bpo], inc=(kb == kbs[-1]))

            for h in range(8):
                att_scores(h)
                if h > 2:
                    att_pv(h - 3)
            att_pv(5)
            att_pv(6)
            att_pv(7)
            osl = qb % 2
            for hb in range(2):
                po, bpo = PS[pso2[hb]], bPS[pso2[hb]]
                pv = po[:, 0:260].rearrange("p (h c) -> p h c", c=65)
                S.op("dve", lambda e: e.tensor_tensor(dsm[:, 0, hb * 4:hb * 4 + 4].unsqueeze(2), pv[:, :, 64:65], sm[:, 24 + hb * 4:28 + hb * 4].unsqueeze(2), ALU.add),
                     reads=[bpo, bsm], writes=[bdsm[0]])
                S.op("dve", lambda e: e.reciprocal(dsm[:, 1, hb * 4:hb * 4 + 4], dsm[:, 0, hb * 4:hb * 4 + 4]), reads=[bdsm[0]], writes=[bdsm[1]])
                S.op("dve", lambda e: e.tensor_tensor(oatok[:, osl, hb * 256:(hb + 1) * 256].rearrange("p (h c) -> p h c", c=64), pv[:, :, 0:64],
                                                      dsm[:, 1, hb * 4:hb * 4 + 4].unsqueeze(2).to_broadcast([128, 4, 64]), ALU.mult),
                     reads=[bpo, bdsm[1]], writes=[boatok[osl]])
            psum_release(pso2)
            ps, bps = psum_next()
            psb = ps[:, :].bitcast(BF16)
            for kc in range(4):
                S.op("pe", lambda e, kc=kc: e.transpose(psb[:, kc * 128:(kc + 1) * 128], oatok[:, osl, kc * 128:(kc + 1) * 128], identb[:]),
                     reads=[boatok[osl], bcst], writes=[bps], inc=(kc == 3))
            S.op("act", lambda e: e.copy(X1[:, OA:OA + 4, qb * 128:(qb + 1) * 128], psb[:, 0:512].rearrange("p (a b) -> p a b", a=4)),
                 reads=[bps], writes=bX[OA:OA + 4])

        def mix_hook():
            wt, bwt = ring_next("gr")
            for hd in range(4):
                psg, bpsg = proj_F(wt, bwt, hd, nTk, bn, T)
                S.op("act", lambda e, hd=hd: e.activation(B8[:, 4 + hd, :], psg[:, :], AF.Silu), reads=[bpsg], writes=[bB[4 + hd]])
            for jj in range(2):
                wt, bwt = ring_next(f"qq{jj}")
                for sub in range(2):
                    j = 2 * jj + sub
                    ps1, bps1 = proj_F(wt, bwt, 2 * sub, nTk, bn, T)
                    ps2, bps2 = proj_F(wt, bwt, 2 * sub + 1, nTk, bn, T)
                    t1, bt1 = tmpf()
                    t2, bt2 = tmpf()
                    S.op("dve", lambda e: e.tensor_tensor(t1, ps1[:, :], ct, ALU.mult), reads=[bps1, bct], writes=[bt1])
                    S.op("dve", lambda e: e.tensor_tensor(t2, ps2[:, :], sn, ALU.mult), reads=[bps2, bsn], writes=[bt2])
                    S.op("dve", lambda e, j=j: e.tensor_tensor(X1[:, QR + j, :], t1, t2, ALU.add), reads=[bt1, bt2], writes=[bX[QR + j]])

        def mix_hook2():
            mix_hook()
            attention_qb(0)
            attention_qb(1)
            attention_qb(2)
            attention_qb(3)

        held = hgrn_tile(ti, 0, ti == 0, mid_hook=mix_hook2)
        S.dma("sp", HG[:, LF], ob_d[:, :, tok0:tok0 + T], reads=[bob[ti]], writes=bHG[LF])
        for hd in range(4):
            S.op("dve", lambda e, hd=hd: e.tensor_tensor(HG[:, KK, hd, :], PS[held[hd]][:, :], HG[:, LF, hd, :], ALU.add),
                 reads=[bPS[held[hd]], bHG[LF][hd]], writes=[bHG[KK][hd]])
        psum_release(held)
        if ti == 0:
            tap("osum", HG[:, KK], bHG[KK], [128, 4, T])
        def o_norm(hd):
            S.op("act", lambda e: e.activation(B8[:, hd, :], HG[:, KK, hd, :], AF.Square), reads=[bHG[KK][hd]], writes=[bB[hd]])
            ps, bps = psum_next()
            S.op("pe", lambda e: e.matmul(ps[:, :], onesV[:], B8[:, hd, :], start=True, stop=True), reads=[bB[hd], bcst], writes=[bps])
            rs, brs = tmpf()
            S.op("act", lambda e: e.activation(rs, ps[:, :], AF.Ln, bias=sm[:, 40:41]), reads=[bps, bsm], writes=[brs])
            S.op("act", lambda e: e.activation(rs, rs, AF.Exp, scale=-0.5), reads=[brs], writes=[brs])
            S.op("dve", lambda e: e.tensor_tensor(rs, rs, HG[:, KK, hd, :], ALU.mult), reads=[brs, bHG[KK][hd]], writes=[brs])
            S.op("dve", lambda e: e.scalar_tensor_tensor(X1[:, OG + hd, :], rs, pr("hgn", hd, hd + 1), B8[:, 4 + hd, :], ALU.mult, ALU.mult),
                 reads=[brs, bB[4 + hd], bpar], writes=[bX[OG + hd]])
        if ti == 0:
            tap("qrT", X1[:, QR:QR + 4, :], bX[QR:QR + 4], [128, 4, T])
        o_norm(0)
        o_norm(1)
        o_norm(2)
        o_norm(3)
        if ti == 0:
            tap("og", X1[:, OG:OG + 4, :], bX[OG:OG + 4], [128, 4, T])
        if ti == 0:
            tap("oaT", X1[:, OA:OA + 4, :], bX[OA:OA + 4], [128, 4, T])
        for j in range(8):
            wt, bwt = ring_next(f"mg{j}")
            psr, bpsr = proj_F(wt, bwt, 0, nTk, bn, T)
            psa, bpsa = proj_F(wt[:, 1024:], bwt, 0, nTk, bn, T)
            sr, bsr = tmpf()
            sa, bsa = tmpf()
            S.op("act", lambda e: e.activation(sr, psr[:, :], AF.Sigmoid), reads=[bpsr], writes=[bsr])
            S.op("act", lambda e: e.activation(sa, psa[:, :], AF.Sigmoid), reads=[bpsa], writes=[bsa])
            pyr, bpyr = proj_F(wt[:, 2048:], bwt, 0, lambda k: X1[:, OG + k, :], bX[OG:OG + 4], T, nk=4)
            pya, bpya = proj_F(wt[:, 2560:], bwt, 0, lambda k: X1[:, OA + k, :], bX[OA:OA + 4], T, nk=4)
            S.op("dve", lambda e: e.tensor_tensor(sr, pyr[:, :], sr, ALU.mult), reads=[bpyr, bsr], writes=[bsr])
            S.op("dve", lambda e: e.tensor_tensor(sa, pya[:, :], sa, ALU.mult), reads=[bpya, bsa], writes=[bsa])
            S.op("dve", lambda e, j=j: e.tensor_tensor(B8[:, j, :], sr, sa, ALU.add), reads=[bsr, bsa], writes=[bB[j]])
        if ti == 0:
            tap("merged", B8[:], bB, [128, 8, T])
        for w2 in range(2):
            wt, bwt = ring_next(f"wout{w2}")
            for jb in range(4):
                j = w2 * 4 + jb
                ps, bps = proj_F(wt, bwt, jb, lambda k: B8[:, k, :], bB, T)
                S.op("dve", lambda e, j=j: e.tensor_tensor(hT[:, j, H0:H0 + T], hT[:, j, H0:H0 + T], ps[:, :], ALU.add), reads=[bps, bh[j]], writes=[bh[j]])
        if ti == 0:
            tap("h1", hT[:], bh, [128, KC, HWID])
        rmsnorm(lambda k: hT[:, k, H0:H0 + T], bh, "g_mem", T, nTk, bn)
        QM, OM = 0, 8
        for w2 in range(2):
            wt, bwt = ring_next(f"wmq{w2}")
            for jb in range(4):
                j = w2 * 4 + jb
                ps, bps = proj_F(wt, bwt, jb, nTk, bn, T)
                S.op("act", lambda e, j=j: e.copy(X1[:, QM + j, :], ps[:, :]), reads=[bps], writes=[bX[QM + j]])
        def mem_scores(h):
            pb = 16 + 2 * (h % 2)
            for mc in range(2):
                ps, bps = psum_next()
                for dc in range(2):
                    S.op("pe", lambda e, dc=dc: e.matmul(ps[:, :], KmT[:, 2 * h + dc, mc * 128:(mc + 1) * 128], X1[:, QM + 2 * h + dc, :],
                                                         start=(dc == 0), stop=(dc == 1)),
                         reads=[bKm, bX[QM + 2 * h + dc]], writes=[bps], inc=(dc == 1))
                S.op("act", lambda e: e.activation(X1[:, pb + mc, :], ps[:, :], AF.Exp, scale=1.0 / 16), reads=[bps], writes=[bX[pb + mc]])

        def mem_pv(h):
            pb = 16 + 2 * (h % 2)
            psd, bpsd = psum_next()
            for mc in range(2):
                S.op("pe", lambda e: e.matmul(psd[:, :], ones1[:], X1[:, pb + mc, :], start=(mc == 0), stop=(mc == 1)),
                     reads=[bX[pb + mc], bcst], writes=[bpsd], inc=(mc == 1))
            rd, brd = tmpf()
            S.op("dve", lambda e: e.reciprocal(rd, psd[:, :]), reads=[bpsd], writes=[brd])
            for dc in range(2):
                ps, bps = psum_next()
                for mc in range(2):
                    S.op("pe", lambda e: e.matmul(ps[:, :], Vm[:, mc, h * 256 + dc * 128: h * 256 + (dc + 1) * 128], X1[:, pb + mc, :],
                                                  start=(mc == 0), stop=(mc == 1)),
                         reads=[bVm, bX[pb + mc]], writes=[bps], inc=(mc == 1))
                S.op("dve", lambda e: e.tensor_tensor(X1[:, OM + 2 * h + dc, :], ps[:, :], rd, ALU.mult), reads=[bps, brd], writes=[bX[OM + 2 * h + dc]])

        for h in range(4):
            mem_scores(h)
            if h > 0:
                mem_pv(h - 1)
        mem_pv(3)
        for w2 in range(2):
            wt, bwt = ring_next(f"wmo{w2}")
            for jb in range(4):
                j = w2 * 4 + jb
                ps, bps = proj_F(wt, bwt, jb, lambda k: X1[:, OM + k, :], bX[OM:OM + 8], T)
                S.op("dve", lambda e, j=j: e.tensor_tensor(hT[:, j, H0:H0 + T], hT[:, j, H0:H0 + T], ps[:, :], ALU.add), reads=[bps, bh[j]], writes=[bh[j]])
        if ti == 0:
            tap("h2", hT[:], bh, [128, KC, HWID])
        rmsnorm(lambda k: hT[:, k, H0:H0 + T], bh, "g_ffn", T, nTk, bn)
        cw = lambda j_, jf: pr("conv_w", j_ * JF + jf, j_ * JF + jf + 1)
        pend = []

        def ffn_B(jf, c_, bc_, psu, bpsu):
            S.op("act", lambda e: e.activation(c_, c_, AF.Silu), reads=[bc_], writes=[bc_])
            S.op("dve", lambda e: e.tensor_tensor(X1[:, jf, 1:T], c_[:, 1:T], psu[:, 0:T - 1], ALU.mult), reads=[bc_, bpsu], writes=[bX[jf]])
            S.op("dve", lambda e: e.tensor_tensor(X1[:, jf, 0:1], c_[:, 0:1], ucar[:, jf:jf + 1], ALU.mult), reads=[bc_, bucar[jf]], writes=[bX[jf]])
            S.op("dve", lambda e: e.tensor_copy(ucar[:, jf:jf + 1], psu[:, T - 1:T]), reads=[bpsu], writes=[bucar[jf]])

        for jj in range(JF // 2):
            wt, bwt = ring_next(f"up{jj}")
            for sub in range(2):
                jf = 2 * jj + sub
                psu, bpsu = proj_F(wt, bwt, 2 * sub, nTk, bn, T)
                psg, bpsg = proj_F(wt, bwt, 2 * sub + 1, nTk, bn, T)
                gi = jf % NGS
                g_, bg_ = gs[:, gi, :], bgs[gi]
                S.op("act", lambda e: e.copy(g_[:, 2:T + 2], psg[:, :]), reads=[bpsg], writes=[bg_])
                S.op("dve", lambda e: e.tensor_copy(g_[:, 0:2], gcar[:, jf, :]), reads=[bgcar[jf]], writes=[bg_])
                S.op("dve", lambda e: e.tensor_copy(gcar[:, jf, :], g_[:, T:T + 2]), reads=[bg_], writes=[bgcar[jf]])
                c_, bc_ = tmpf()
                S.op("dve", lambda e: e.tensor_scalar(c_, g_[:, 0:T], cw(0, jf), pr("conv_b", jf, jf + 1), ALU.mult, ALU.add), reads=[bg_, bpar], writes=[bc_])
                S.op("dve", lambda e: e.scalar_tensor_tensor(c_, g_[:, 1:T + 1], cw(1, jf), c_, ALU.mult, ALU.add), reads=[bg_, bpar, bc_], writes=[bc_])
                S.op("dve", lambda e: e.scalar_tensor_tensor(c_, g_[:, 2:T + 2], cw(2, jf), c_, ALU.mult, ALU.add), reads=[bg_, bpar, bc_], writes=[bc_])
                pend.append((jf, c_, bc_, psu, bpsu))
                if len(pend) > 1:
                    ffn_B(*pend.pop(0))
        while pend:
            ffn_B(*pend.pop(0))
        if ti + 1 < NT:
            S.dma("sp", nT[:], nts_d[ti + 1], reads=[bnts[ti + 1]], writes=bn)
        if ti == 0:
            tap("act", X1[:], bX, [128, JF, T])
        for j in range(8):
            wt, bwt = ring_next(f"dn{j}")
            ps, bps = proj_F(wt, bwt, 0, lambda k: X1[:, k, :], bX, T, nk=JF)
            S.op("dve", lambda e, j=j: e.tensor_tensor(hwin(j), hwin(j), ps[:, :], ALU.add), reads=[bps, bh[j]], writes=[bh[j]])
        ffn_finish(ti, T, H0 - 1, tok0 - 1, skip_first=(ti == 0))
        S.op("pool", lambda e: e.tensor_copy(hT[:, :, H0 - 1:H0], hT[:, :, H0 + T - 1:H0 + T]), reads=bh, writes=bh)

    cwa = lambda j_: pr("conv_w", j_ * JF, (j_ + 1) * JF)
    S.op("dve", lambda e: e.tensor_tensor(small[:, 0, :], gcar[:, :, 0], cwa(0), ALU.mult), reads=bgcar + [bpar], writes=[bsmall[0]])
    S.op("dve", lambda e: e.tensor_tensor(small[:, 1, :], gcar[:, :, 1], cwa(1), ALU.mult), reads=bgcar + [bpar], writes=[bsmall[1]])
    S.op("dve", lambda e: e.tensor_tensor(small[:, 0, :], small[:, 0, :], small[:, 1, :], ALU.add), reads=[bsmall[0], bsmall[1]], writes=[bsmall[0]])
    S.op("dve", lambda e: e.tensor_tensor(small[:, 2, :], small[:, 0, :], pr("conv_b"), ALU.add), reads=[bsmall[0], bpar], writes=[bsmall[2]])
    S.op("act", lambda e: e.activation(small[:, 3, :], small[:, 2, :], AF.Silu), reads=[bsmall[2]], writes=[bsmall[3]])
    S.op("dve", lambda e: e.tensor_tensor(actl[:], small[:, 3, :], ucar[:], ALU.mult), reads=[bsmall[3]] + bucar, writes=[bactl])
    for j in range(8):
        wt, bwt = ring_next(f"dn{j}")
        ps, bps = psum_next()
        for kc in range(JF):
            S.op("pe", lambda e, kc=kc: e.matmul(ps[:, 0:1], wt[:, kc * 128:(kc + 1) * 128], actl[:, kc:kc + 1], start=(kc == 0), stop=(kc == JF - 1)),
                 reads=[bwt, bactl], writes=[bps], inc=(kc == JF - 1))
        S.op("dve", lambda e, j=j: e.tensor_tensor(hT[:, j, H0 - 1:H0], hT[:, j, H0 - 1:H0], ps[:, 0:1], ALU.add), reads=[bps, bh[j]], writes=[bh[j]])
    ffn_finish(NT, 1, H0 - 1, SEQ - 1, skip_first=False)
    return finish(nc, S, dbg_out, final_toks)


def finish(nc, S, dbg_out, final_toks=()):
    for tok in list(dbg_out.values()) + list(final_toks):
        S.wait_tok("sp", tok)
    return nc, S, dbg_out


_CACHE = {}


def _get_program(SEQ):
    if SEQ not in _CACHE:
        _CACHE[SEQ] = build_program(SEQ)
    return _CACHE[SEQ]


def kernel(**inputs):
    inp = {k: np.asarray(v) for k, v in inputs.items()}
    x = inp["x"].astype(np.float32, copy=False)
    B, SEQ, _ = x.shape
    wall, index, bounds = build_wall(inp)
    par, _ = build_params(inp)
    con, _ = build_consts()
    nc, S, _ = _get_program(SEQ)
    in_maps = []
    for b in range(B):
        in_maps.append({
            "x": np.ascontiguousarray(x[b]),
            "mem": np.ascontiguousarray(inp["mem"][b].astype(np.float32, copy=False)),
            "pos": np.ascontiguousarray(inp["positions"][b].astype(np.int32, copy=False).reshape(1, SEQ)),
            "wall": wall, "par": par, "con": con,
        })
    res = run_bass_kernel_spmd(nc, in_maps, core_ids=list(range(B)))
    return np.stack([np.asarray(r["out"], dtype=np.float32) for r in res.results], axis=0)
```

```python
import numpy as np
import ml_dtypes
import concourse.bass as bass
import concourse.mybir as mybir
from concourse.bass_utils import run_bass_kernel_spmd

F32 = mybir.dt.float32
BF16 = mybir.dt.bfloat16
I32 = mybir.dt.int32
AF = mybir.ActivationFunctionType
ALU = mybir.AluOpType
AX = mybir.AxisListType


class Buf:
    __slots__ = ("name", "w", "r", "ld_sem", "ld_cnt", "st_sem", "st_cnt")

    def __init__(self, name):
        self.name = name
        self.w = None
        self.r = {}
        self.ld_sem = None
        self.ld_cnt = 0
        self.st_sem = None
        self.st_cnt = 0


class Sched:
    SEM_LIMIT = 60000

    def __init__(self, nc):
        self.nc = nc
        self.eng = {"pe": nc.tensor, "act": nc.scalar, "dve": nc.vector,
                    "pool": nc.gpsimd, "sp": nc.sync}
        self.sem = {}
        self.cnt = {}
        self.nsem = 0
        for e in ("pe", "act", "dve", "pool"):
            self._new_sem(e)
        self.waited = {e: {} for e in self.eng}
        self.n_ops = {e: 0 for e in self.eng}
        self.n_waits = {e: 0 for e in self.eng}

    def _new_sem(self, e):
        self.nsem += 1
        self.sem[e] = self.nc.alloc_semaphore(f"s_{e}_{self.nsem}")
        self.cnt[e] = 0

    def buf(self, name):
        return Buf(name)

    def _wait(self, e, sem, val):
        w = self.waited[e]
        if w.get(sem, 0) >= val:
            return
        self.eng[e].wait_ge(sem, val)
        w[sem] = val
        self.n_waits[e] += 1

    def _deps(self, e, reads, writes, own):
        strict = e != "pe"
        for b in reads:
            if b.w is not None:
                self._wait(e, *b.w)
        for b in writes:
            if b.w is not None and (strict or b.w[0] is not own):
                self._wait(e, *b.w)
            for s, v in b.r.items():
                if strict or s is not own:
                    self._wait(e, s, v)

    def op(self, e, fn, reads=(), writes=(), inc=True):
        if inc and self.cnt[e] >= self.SEM_LIMIT:
            self._new_sem(e)
        own = self.sem[e]
        self._deps(e, reads, writes, own)
        ins = fn(self.eng[e])
        self.n_ops[e] += 1
        if inc:
            self.cnt[e] += 1
            ins.then_inc(own, 1)
            val = self.cnt[e]
        else:
            val = self.cnt[e] + 1
        for b in reads:
            if b.r.get(own, 0) < val:
                b.r[own] = val
        for b in writes:
            b.w = (own, val)
            b.r = {}
        return ins

    def dma(self, q, out_ap, in_ap, reads=(), writes=(), **kw):
        own = None
        self._deps(q, reads, writes, own)
        ins = self.eng[q].dma_start(out=out_ap, in_=in_ap, **kw)
        self.n_ops[q] += 1
        if writes:
            b = writes[0]
            if b.ld_sem is None or b.ld_cnt >= self.SEM_LIMIT:
                self.nsem += 1
                b.ld_sem = self.nc.alloc_semaphore(f"ld_{b.name}_{self.nsem}")
                b.ld_cnt = 0
            b.ld_cnt += 16
            ins.then_inc(b.ld_sem, 16)
            tok = (b.ld_sem, b.ld_cnt)
            for wb in writes:
                wb.w = tok
                wb.r = {}
            for rb in reads:
                rb.r[tok[0]] = tok[1]
        else:
            b = reads[0]
            if b.st_sem is None or b.st_cnt >= self.SEM_LIMIT:
                self.nsem += 1
                b.st_sem = self.nc.alloc_semaphore(f"st_{b.name}_{self.nsem}")
                b.st_cnt = 0
            b.st_cnt += 16
            ins.then_inc(b.st_sem, 16)
            tok = (b.st_sem, b.st_cnt)
            for rb in reads:
                rb.r[tok[0]] = tok[1]
        return tok

    def wait_tok(self, e, tok):
        self._wait(e, tok[0], tok[1])


D = 1024
KC = 8
T = 512
C = 64
NCH = T // C
MID = 31
H0 = 4
HWID = T + H0
DFF = 2816
JF = DFF // 128
MEM = 256
EPS = 1e-6
SLOT = 4096
NSLOT = 4
CHUNK0 = 4096
ROPE_THETA = 500000.0


def _f_tiles(W):
    Kd, N = W.shape
    return np.ascontiguousarray(W.reshape(Kd // 128, 128, N // 128, 128).transpose(1, 2, 0, 3))


def _t_tiles(W):
    Kd, N = W.shape
    return np.ascontiguousarray(W.reshape(Kd // 128, 128, N).transpose(1, 0, 2))


def _partner(W, nh):
    Wh = W.reshape(W.shape[0], nh, 64)
    P = np.zeros_like(Wh)
    P[:, :, 0:8] = Wh[:, :, 8:16]
    P[:, :, 8:16] = Wh[:, :, 0:8]
    return P.reshape(W.shape)


def build_wall(inp):
    w_in = inp["w_in"][0]
    segs = []

    def add(name, arr):
        segs.append((name, arr.reshape(128, -1)))

    ka = w_in[:, 3072:3200]
    kdup = np.concatenate([ka[:, 0:64], ka[:, 0:64], ka[:, 64:128], ka[:, 64:128]], axis=1)
    kdup_p = _partner(kdup, 4)
    add("qr", _f_tiles(w_in[:, 0:512]))
    add("fzb", _f_tiles(w_in[:, 1024:1536]))
    add("ir", _t_tiles(w_in[:, 1536:2048]))
    add("ka4", _f_tiles(np.concatenate([kdup, kdup_p], axis=1)))
    add("va", _t_tiles(w_in[:, 3200:3328]))
    g0 = len(segs)
    add("fzf", _f_tiles(w_in[:, 512:1024]))
    add("gr", _f_tiles(w_in[:, 2048:2560]))
    qa = w_in[:, 2560:3072]
    qa_t = _f_tiles(qa)
    qp_t = _f_tiles(_partner(qa, 8))
    for jj in range(2):
        add(f"qq{jj}", np.concatenate([qa_t[:, 2 * jj].reshape(128, -1), qp_t[:, 2 * jj].reshape(128, -1),
                                       qa_t[:, 2 * jj + 1].reshape(128, -1), qp_t[:, 2 * jj + 1].reshape(128, -1)], axis=1))
    gr_t = _f_tiles(w_in[:, 3328:4352])
    ga_t = _f_tiles(w_in[:, 4352:5376])
    br_t = _f_tiles(inp["w_br_rec"][0])
    ba_t = _f_tiles(inp["w_br_att"][0])
    for j in range(8):
        add(f"mg{j}", np.concatenate([gr_t[:, j].reshape(128, -1), ga_t[:, j].reshape(128, -1),
                                      br_t[:, j].reshape(128, -1), ba_t[:, j].reshape(128, -1)], axis=1))
    wo = _f_tiles(inp["w_mix_out"][0])
    add("wout0", wo[:, 0:4]); add("wout1", wo[:, 4:8])
    g1 = len(segs)
    wq = _f_tiles(inp["w_mem_q"][0])
    add("wmq0", wq[:, 0:4]); add("wmq1", wq[:, 4:8])
    wk = _f_tiles(inp["w_mem_kv"][0][:, 0:1024])
    add("wmk0", wk[:, 0:4]); add("wmk1", wk[:, 4:8])
    add("wmv0", _t_tiles(inp["w_mem_kv"][0][:, 1024:1536]))
    add("wmv1", _t_tiles(inp["w_mem_kv"][0][:, 1536:2048]))
    wmo = _f_tiles(inp["w_mem_o"][0])
    add("wmo0", wmo[:, 0:4]); add("wmo1", wmo[:, 4:8])
    g2 = len(segs)
    wu = _f_tiles(inp["w_up"][0][:, 0:DFF])
    wg = _f_tiles(inp["w_up"][0][:, DFF:2 * DFF])
    for jj in range(JF // 2):
        add(f"up{jj}", np.concatenate([wu[:, 2 * jj].reshape(128, -1), wg[:, 2 * jj].reshape(128, -1),
                                       wu[:, 2 * jj + 1].reshape(128, -1), wg[:, 2 * jj + 1].reshape(128, -1)], axis=1))
    wd = _f_tiles(inp["w_down"][0])
    for j in range(8):
        add(f"dn{j}", wd[:, j])
    g3 = len(segs)
    index = {}
    off = 0
    for name, arr in segs:
        assert arr.shape[1] <= SLOT, (name, arr.shape)
        index[name] = (off, arr.shape[1])
        off += arr.shape[1]
    wall = np.ascontiguousarray(np.concatenate([a for _, a in segs], axis=1).astype(np.float32))
    bounds = [index[segs[g - 1][0]][0] + index[segs[g - 1][0]][1] for g in (g0, g1, g2, g3)]
    return wall, index, bounds


def wall_index():
    index = {}
    off = 0
    order = ([("qr", 4096), ("fzb", 4096), ("ir", 4096), ("ka4", 4096), ("va", 1024)],
             [("fzf", 4096), ("gr", 4096), ("qq0", 4096), ("qq1", 4096)] + [(f"mg{j}", 3072) for j in range(8)]
             + [("wout0", 4096), ("wout1", 4096)],
             [("wmq0", 4096), ("wmq1", 4096), ("wmk0", 4096), ("wmk1", 4096), ("wmv0", 4096), ("wmv1", 4096),
              ("wmo0", 4096), ("wmo1", 4096)],
             [(f"up{j}", 4096) for j in range(JF // 2)] + [(f"dn{j}", 2816) for j in range(8)])
    bounds = []
    for grp in order:
        for name, L in grp:
            index[name] = (off, L)
            off += L
        bounds.append(off)
    return index, bounds, off


def build_params(inp):
    cols = []
    names = {}

    def add(name, arr):
        arr = np.asarray(arr, np.float32).reshape(128, -1)
        names[name] = (sum(c.shape[1] for c in cols), arr.shape[1])
        cols.append(arr)

    def pk(v):
        v = np.asarray(v, np.float32).reshape(-1, 128)
        return np.ascontiguousarray(v.T)

    add("g_mix", pk(inp["norm_mix"][0]))
    add("g_mem", pk(inp["norm_mem"][0]))
    add("g_memkv", pk(inp["norm_mem_kv"][0]))
    add("g_ffn", pk(inp["norm_ffn"][0]))
    add("g_fin", pk(inp["final_norm"]))
    lbr = inp["lower_bounds"]
    add("lb_raw", np.stack([pk(lbr[d, s]) for d in range(2) for s in range(2)], axis=1))
    add("hgn", pk(inp["hg_norm"][0]))
    add("sink", np.broadcast_to(np.asarray(inp["attn_sink"][0], np.float32)[None, :], (128, 8)))
    add("conv_w", np.stack([pk(inp["conv_w"][0][j]) for j in range(3)], axis=1))
    add("conv_b", pk(inp["conv_b"][0]))
    par = np.ascontiguousarray(np.concatenate(cols, axis=1))
    return par, names


def params_index():
    names = {}
    off = 0
    for n, L in (("g_mix", 8), ("g_mem", 8), ("g_memkv", 8), ("g_ffn", 8), ("g_fin", 8), ("lb_raw", 16),
                 ("hgn", 4), ("sink", 8), ("conv_w", 66), ("conv_b", 22)):
        names[n] = (off, L)
        off += L
    return names, off


def build_consts():
    cols = {}
    p = np.arange(128)
    ident = np.eye(128, dtype=np.float32)
    s = (p % 64)[:, None]
    t = np.arange(64)[None, :]
    mask_f = (s <= t).astype(np.float32)
    mask_b = (s >= t).astype(np.float32)
    key = p[:, None]
    q = np.arange(128)[None, :]
    band_prev = (key >= q).astype(np.float32)
    band_next = (key <= q).astype(np.float32)
    m01 = np.ones((128, T), np.float32)
    m01[:, ::C] = 0.0
    inv_freq = 1.0 / (ROPE_THETA ** (np.arange(0, 16, 2, dtype=np.float32) / 16.0))
    invf = np.zeros((128, 1), np.float32)
    phase_c = np.full((128, 1), 0.5 * np.pi, np.float32)
    phase_s = np.zeros((128, 1), np.float32)
    for pp in range(128):
        r = pp % 64
        if r < 16:
            invf[pp, 0] = inv_freq[r % 8]
            phase_s[pp, 0] = np.pi if r < 8 else 0.0
    ones = np.ones((128, 128), np.float32)
    order = [("ident", ident), ("mask_f", mask_f), ("mask_b", mask_b), ("band_prev", band_prev),
             ("band_next", band_next), ("m01", m01), ("invf", invf), ("phase_c", phase_c), ("phase_s", phase_s), ("ones", ones)]
    names = {}
    off = 0
    arrs = []
    for n, a in order:
        names[n] = (off, a.shape[1])
        off += a.shape[1]
        arrs.append(a.astype(np.float32))
    return np.ascontiguousarray(np.concatenate(arrs, axis=1)), names


def weight_plan(NT):
    plan = []
    for _ in range(NT):
        plan += ["fzb", "qr", "ir", "ka4", "va"]
    plan += ["wmk0", "wmk1", "wmv0", "wmv1"]
    for _ in range(NT):
        plan += ["fzf", "gr", "qq0", "qq1"]
        plan += [f"mg{j}" for j in range(8)]
        plan += ["wout0", "wout1", "wmq0", "wmq1", "wmo0", "wmo1"]
        plan += [f"up{j}" for j in range(JF // 2)]
        plan += [f"dn{j}" for j in range(8)]
    plan += [f"dn{j}" for j in range(8)]
    return plan


def build_program(SEQ, dbg_taps=(), stop_after=None):
    nc = bass.Bass("TRN2", target_bir_lowering=False)
    S = Sched(nc)
    NT = SEQ // T
    NBLK = SEQ // 128
    widx, wbounds, LTOT = wall_index()
    pidx, NPAR = params_index()
    _, cidx = build_consts()
    NCON = sum(v[1] for v in cidx.values())

    x_d = nc.dram_tensor("x", [SEQ, D], F32, kind="ExternalInput").ap()
    mem_d = nc.dram_tensor("mem", [MEM, D], F32, kind="ExternalInput").ap()
    pos_d = nc.dram_tensor("pos", [1, SEQ], I32, kind="ExternalInput").ap()
    wall_d = nc.dram_tensor("wall", [128, LTOT], F32, kind="ExternalInput").ap()
    par_d = nc.dram_tensor("par", [128, NPAR], F32, kind="ExternalInput").ap()
    con_d = nc.dram_tensor("con", [128, NCON], F32, kind="ExternalInput").ap()
    out_d = nc.dram_tensor("out", [SEQ, D], F32, kind="ExternalOutput").ap()
    wscr_d = nc.dram_tensor("wscr", [128, LTOT], BF16).ap()
    ob_d = nc.dram_tensor("obscr", [128, 4, SEQ], F32).ap()
    nts_d = nc.dram_tensor("ntscr", [NT, 128, KC * T], BF16).ap()
    vs_d = nc.dram_tensor("vscr", [NT, 128, 4 * 512], BF16).ap()
    sqs_d = nc.dram_tensor("sqscr", [NT, 128, 4 * T], BF16).ap()
    dbg_out = {}

    def sb(name, shape, dt):
        return nc.alloc_sbuf_tensor("sb_" + name, shape, dt)

    def bufs(name, n):
        return [S.buf(f"{name}{i}") for i in range(n)]

    con = sb("con", [128, NCON], F32); bcon = S.buf("con")
    par = sb("par", [128, NPAR], F32); bpar = S.buf("par")
    identb = sb("identb", [128, 128], BF16)
    onesD = sb("onesD", [128, 128], BF16)
    onesV = sb("onesV", [128, 128], BF16)
    ones1 = sb("ones1", [128, 128], BF16)
    bandb = sb("bandb", [128, 3, 128], BF16)
    bcst = S.buf("cst")
    sm = sb("sm", [128, 64], F32); bsm = S.buf("sm")
    hT = sb("hT", [128, KC, HWID], F32); bh = bufs("h", KC)
    nT = sb("nT", [128, KC, T], BF16); bn = bufs("n", KC)
    B8 = sb("B8", [128, KC, T], BF16); bB = bufs("B", KC)
    X1 = sb("X1", [128, JF, T], BF16); bX = bufs("X", JF)
    NPF = 6
    PF = sb("PF", [128, NPF, T], F32); bPF = bufs("pf", NPF)
    HG = sb("HG", [128, 3, 4, T], F32); bHG = [bufs(f"hg{i}_", 4) for i in range(3)]
    KT = sb("KT", [128, 2, SEQ], BF16); bKT = bufs("kt", NT)
    VW = 66
    Vst = sb("Vst", [128, NBLK, 2, VW], BF16); bV = bufs("v", NT)
    KmT = sb("KmT", [128, 8, MEM], BF16); bKm = S.buf("KmT")
    Vm = sb("Vm", [128, 2, D], BF16); bVm = S.buf("Vm")
    ktok = sb("ktok", [128, 4, 512], BF16); bktok = bufs("ktok", 4)
    vtok = sb("vtok", [128, 4, 512], BF16); bvtok = bufs("vtok", 4)
    ATm = sb("ATm", [128, 2, 256], BF16); bATm = bufs("atm", 2)
    U = sb("U", [128, 4, 128], F32); bU = bufs("U", 4)
    S16 = sb("S16", [128, 2, 4, 128], BF16); bS16 = [bufs(f"s16_{i}_", 4) for i in range(2)]
    la = sb("la", [128, 4, NCH], F32); bla = S.buf("la")
    lbt = sb("lbt", [128, 4, NCH], F32); blbt = S.buf("lbt")
    refc = sb("refc", [128, 4, NCH], F32); brefc = S.buf("refc")
    lg = sb("lg", [128, 4, NCH], F32); blg = S.buf("lg")
    gam = sb("gam", [128, 4, NCH], F32); bgam = S.buf("gam")
    carry = sb("carry", [128, 2, 4], F32); bcarry = bufs("carry", 2)
    NPT = 3
    PT = sb("PT", [128, NPT, 3, 128], BF16); bPT = bufs("PT", NPT)
    oatok = sb("oatok", [128, 2, 512], BF16); boatok = bufs("oatok", 2)
    dsm = sb("dsm", [128, 2, 8], F32); bdsm = bufs("dsm", 2)
    xt = sb("xt", [128, 2, D], F32); bxt = bufs("xt", 2)
    ot = sb("ot", [128, 2, 512], F32); bot = bufs("ot", 2)
    wring = sb("wring", [128, NSLOT, SLOT], BF16); bring = bufs("ring", NSLOT)
    posi = sb("posi", [128, T], I32); bposi = S.buf("posi")
    rot = sb("rot", [128, 2, T], F32); brot = bufs("rot", 2)
    rstd_t = sb("rstd", [128, T], F32); brstd = S.buf("rstd")
    nint = sb("nint", [128, T], I32); bnint = S.buf("nint")
    gcar = sb("gcar", [128, JF, 2], F32); bgcar = bufs("gcar", JF)
    ucar = sb("ucar", [128, JF], F32); bucar = bufs("ucar", JF)
    NGS = 2
    gs = sb("gs", [128, NGS, T + 2], F32); bgs = bufs("gs", NGS)
    small = sb("small", [128, 4, JF], F32); bsmall = bufs("small", 4)
    actl = sb("actl", [128, JF], BF16); bactl = S.buf("actl")

    PS = [nc.alloc_psum_tensor(f"ps{i}", [128, 512], F32) for i in range(8)]
    bPS = bufs("ps", 8)
    ps_state = {"rot": 0, "held": set()}

    def psum_next():
        while True:
            i = ps_state["rot"] % 8
            ps_state["rot"] += 1
            if i not in ps_state["held"]:
                return PS[i], bPS[i]

    def psum_hold(n):
        res = []
        for _ in range(n):
            while True:
                i = ps_state["rot"] % 8
                ps_state["rot"] += 1
                if i not in ps_state["held"]:
                    break
            ps_state["held"].add(i)
            res.append(i)
        return res

    def psum_release(idx):
        for i in idx:
            ps_state["held"].discard(i)

    pf_state = {"rot": 0}

    def tmpf():
        i = pf_state["rot"] % NPF
        pf_state["rot"] += 1
        return PF[:, i, :], bPF[i]

    ew_state = {"rot": 0}

    def ew():
        return "dve"

    def cs(name, a=0, b=None):
        off, L = cidx[name]
        b = L if b is None else b
        return con[:, off + a: off + b]

    def pr(name, a=0, b=None):
        off, L = pidx[name]
        b = L if b is None else b
        return par[:, off + a: off + b]

    def tap(name, ap, bufl, shape):
        if name not in dbg_taps:
            return
        key = name
        n = 0
        while key in dbg_out:
            n += 1
            key = f"{name}_{n}"
        d = nc.dram_tensor("dbg_" + key, list(shape), ap.dtype, kind="ExternalOutput").ap()
        dbg_out[key] = S.dma("sp", d, ap, reads=list(bufl))

    plan = weight_plan(NT)
    ring = {"issued": 0, "consumed": 0}
    bgrp = bufs("wgrp", 4)

    def grp_of(off):
        for g, b in enumerate(wbounds):
            if off < b:
                return g
        raise AssertionError

    def ring_issue(m):
        name = plan[m]
        off, L = widx[name]
        slot = m % NSLOT
        S.dma("sp", wring[:, slot, 0:L], wscr_d[:, off:off + L], reads=[bgrp[grp_of(off)]], writes=[bring[slot]])

    def ring_next(name):
        n = ring["consumed"]
        assert plan[n] == name, (n, plan[n], name)
        while ring["issued"] < min(len(plan), n + NSLOT):
            ring_issue(ring["issued"])
            ring["issued"] += 1
        ring["consumed"] += 1
        slot = n % NSLOT
        return wring[:, slot, :], bring[slot]

    S.dma("sp", con[:], con_d, writes=[bcon])
    S.dma("sp", par[:], par_d, writes=[bpar])
    S.op("dve", lambda e: e.tensor_copy(identb[:], cs("ident")), reads=[bcon], writes=[bcst])
    S.op("dve", lambda e: e.tensor_scalar(onesD[:], cs("ones"), 1.0 / D, None, ALU.mult), reads=[bcon], writes=[bcst])
    S.op("dve", lambda e: e.tensor_scalar(onesV[:], cs("ones"), 1.0 / 128, None, ALU.mult), reads=[bcon], writes=[bcst])
    S.op("dve", lambda e: e.tensor_copy(ones1[:], cs("ones")), reads=[bcon], writes=[bcst])
    S.op("dve", lambda e: e.tensor_copy(bandb[:, 0, :], cs("band_prev")), reads=[bcon], writes=[bcst])
    S.op("dve", lambda e: e.tensor_copy(bandb[:, 1, :], cs("ones")), reads=[bcon], writes=[bcst])
    S.op("dve", lambda e: e.tensor_copy(bandb[:, 2, :], cs("band_next")), reads=[bcon], writes=[bcst])
    lbr = pr("lb_raw").rearrange("p (d s h) -> p d s h", d=2, s=2)
    S.op("dve", lambda e: e.tensor_tensor(sm[:, 32:40].rearrange("p (d h) -> p d h", d=2), lbr[:, :, 0, :], lbr[:, :, 1, :], ALU.subtract),
         reads=[bpar], writes=[bsm])
    S.op("act", lambda e: e.activation(sm[:, 0:8], sm[:, 32:40], AF.Sigmoid), reads=[bsm], writes=[bsm])
    S.op("dve", lambda e: e.tensor_scalar(sm[:, 8:16], sm[:, 0:8], -1.0, 1.0, ALU.mult, ALU.add), reads=[bsm], writes=[bsm])
    S.op("dve", lambda e: e.tensor_scalar(sm[:, 16:24], sm[:, 0:8], 1.0, -1.0, ALU.mult, ALU.add), reads=[bsm], writes=[bsm])
    S.op("act", lambda e: e.activation(sm[:, 24:32], pr("sink"), AF.Exp), reads=[bpar], writes=[bsm])
    S.op("pool", lambda e: e.memset(sm[:, 40:41], EPS), writes=[bsm])
    S.op("pool", lambda e: e.memset(hT[:], 0.0), writes=bh)
    S.op("pool", lambda e: e.memset(gcar[:], 0.0), writes=bgcar)
    S.op("pool", lambda e: e.memset(ucar[:], 0.0), writes=bucar)
    S.op("pool", lambda e: e.memset(carry[:], 0.0), writes=bcarry)
    S.op("pool", lambda e: e.memset(U[:], 0.0), writes=bU)
    S.op("pool", lambda e: e.memset(Vst[:], 1.0), writes=bV)

    ci = 0
    g_start = 0
    for g, g_end in enumerate(wbounds):
        a = g_start
        while a < g_end:
            b = min(a + CHUNK0, g_end)
            L = b - a
            sl = ci % 2
            S.dma("pool", wscr_d[:, a:b], wall_d[:, a:b], writes=[bgrp[g]])
            ci += 1
            a = b
        g_start = g_end

    xpre = {"tile": None}

    def x_dma(ti, blk):
        tok0 = ti * T
        sl = blk % 2
        S.dma("sp", xt[:, sl, :], x_d[tok0 + 128 * blk: tok0 + 128 * (blk + 1), :], writes=[bxt[sl]])

    def prefetch_x(ti):
        x_dma(ti, 0)
        x_dma(ti, 1)
        xpre["tile"] = ti

    def load_x_tile_gen(ti, nxt=None):
        have = xpre["tile"] == ti
        xpre["tile"] = None
        for blk in range(4):
            sl = blk % 2
            if not (have and blk < 2):
                x_dma(ti, blk)
            for half in range(2):
                ps, bps = psum_next()
                for kk in range(4):
                    k = half * 4 + kk
                    S.op("pe", lambda e, ps=ps, kk=kk, k=k, sl=sl: e.transpose(ps[:, kk * 128:(kk + 1) * 128], xt[:, sl, k * 128:(k + 1) * 128], cs("ident")),
                         reads=[bxt[sl], bcon], writes=[bps], inc=(kk == 3))
                S.op("act", lambda e, ps=ps, half=half, blk=blk: e.copy(
                    hT[:, half * 4:half * 4 + 4, H0 + 128 * blk: H0 + 128 * (blk + 1)],
                    ps[:, :].rearrange("p (a b) -> p a b", a=4)), reads=[bps], writes=bh[half * 4:half * 4 + 4])
            if blk >= 2 and nxt is not None:
                x_dma(nxt, blk - 2)
            if blk == 3 and nxt is not None:
                xpre["tile"] = nxt
            yield

    def load_x_tile(ti, nxt=None):
        for _ in load_x_tile_gen(ti, nxt):
            pass

    def rmsnorm(src, bsrc, gname, N, out, bout, nk=KC, ones=None):
        for _ in rmsnorm_gen(src, bsrc, gname, N, out, bout, nk, ones):
            pass

    def rmsnorm_gen(src, bsrc, gname, N, out, bout, nk=KC, ones=None):
        ones = onesD if ones is None else ones
        for k in range(nk):
            S.op("act", lambda e, k=k: e.activation(B8[:, k, 0:N], src(k), AF.Square), reads=[bsrc[k]], writes=[bB[k]])
        yield
        ps, bps = psum_next()
        for k in range(nk):
            S.op("pe", lambda e, k=k: e.matmul(ps[:, 0:N], ones[:], B8[:, k, 0:N], start=(k == 0), stop=(k == nk - 1)),
                 reads=[bB[k], bcst], writes=[bps], inc=(k == nk - 1))
        yield
        rs, brs = (rstd_t[:, :], brstd) if N == T else tmpf()
        S.op("act", lambda e: e.activation(rs[:, 0:N], ps[:, 0:N], AF.Ln, bias=sm[:, 40:41]), reads=[bps, bsm], writes=[brs])
        S.op("act", lambda e: e.activation(rs[:, 0:N], rs[:, 0:N], AF.Exp, scale=-0.5), reads=[brs], writes=[brs])
        yield
        for k in range(nk):
            S.op("dve", lambda e, k=k: e.scalar_tensor_tensor(out(k), src(k), pr(gname, k, k + 1), rs[:, 0:N], ALU.mult, ALU.mult),
                 reads=[bsrc[k], brs, bpar], writes=[bout[k]])
            if k % 4 == 3:
                yield

    def proj_F(wt, bwt, blk, rhs, brhs, N, nk=KC, wstride=None):
        ps, bps = psum_next()
        for k in range(nk):
            o = (blk * nk + k) * 128
            S.op("pe", lambda e, o=o, k=k: e.matmul(ps[:, 0:N], wt[:, o:o + 128], rhs(k), start=(k == 0), stop=(k == nk - 1)),
                 reads=[bwt, brhs[k]], writes=[bps], inc=(k == nk - 1))
        return ps, bps

    nTk = lambda k: nT[:, k, :]

    LF, KK, BQ = 0, 1, 2
    bsqs = bufs("sqs", NT)
    bvs = bufs("vs", NT)
    bnts = bufs("nts", NT)

    def hgrn_tile(ti, d, first, mid_hook=None, bg=None, late_hook=None):
        tok0 = ti * T
        wt, bwt = ring_next("fzf" if d == 0 else "fzb")
        sgs = []
        for hd in range(4):
            ps, bps = proj_F(wt, bwt, hd, nTk, bn, T)
            S.op("act", lambda e, hd=hd: e.activation(HG[:, KK, hd, :], ps[:, :], AF.Sigmoid), reads=[bps], writes=[bHG[KK][hd]])
            sgs.append((HG[:, KK, hd, :], bHG[KK][hd]))
        if d == 1:
            wt, bwt = ring_next("qr")
            for hd in range(4):
                ps, bps = proj_F(wt, bwt, hd, nTk, bn, T)
                S.op("act", lambda e, hd=hd: e.activation(X1[:, 8 + hd, :], ps[:, :], AF.Silu), reads=[bps], writes=[bX[8 + hd]])
            S.dma("sp", sqs_d[ti], X1[:, 8:12, :], reads=bX[8:12], writes=[bsqs[ti]])
        else:
            S.dma("sp", X1[:, 0:4, :], sqs_d[ti], reads=[bsqs[ti]], writes=bX[0:4])
        if d == 1:
            wt, bwt = ring_next("ir")
            for blk in range(4):
                ps, bps = psum_next()
                for k in range(KC):
                    S.op("pe", lambda e, k=k, blk=blk: e.matmul(ps[:, :], nT[:, k, blk * 128:(blk + 1) * 128], wt[:, k * 512:(k + 1) * 512],
                                                                start=(k == 0), stop=(k == KC - 1)),
                         reads=[bwt, bn[k]], writes=[bps], inc=(k == KC - 1))
                S.op("act", lambda e, blk=blk: e.copy(vtok[:, blk, :], ps[:, :]), reads=[bps], writes=[bvtok[blk]])
            S.dma("sp", vs_d[ti], vtok[:], reads=bvtok, writes=[bvs[ti]])
        else:
            S.dma("sp", vtok[:], vs_d[ti], reads=[bvs[ti]], writes=bvtok)
        if mid_hook is not None:
            mid_hook()
        for hd in range(4):
            sg, bsg = sgs[hd]
            c = d * 4 + hd
            S.op("act", lambda e, c=c, hd=hd: e.activation(HG[:, LF, hd, :], sg, AF.Ln, scale=sm[:, 8 + c:9 + c], bias=sm[:, c:c + 1]),
                 reads=[bsg, bsm], writes=[bHG[LF][hd]])
            S.op("dve", lambda e, c=c, hd=hd: e.tensor_scalar(HG[:, KK, hd, :], sg, sm[:, 16 + c:17 + c], sm[:, 8 + c:9 + c], ALU.mult, ALU.add),
                 reads=[bsg, bsm], writes=[bHG[KK][hd]])
        tap("lf", HG[:, LF], bHG[LF], [128, 4, T])
        tap("kk", HG[:, KK], bHG[KK], [128, 4, T])
        flat = lambda i: HG[:, i].rearrange("p h t -> p (h t)")
        ch = lambda i: HG[:, i].rearrange("p h (c s) -> p h c s", s=C)
        for hd in range(4):
            S.op("dve", lambda e, hd=hd: e.tensor_tensor_scan(HG[:, BQ, hd, :], cs("m01"), HG[:, LF, hd, :], 0.0, ALU.mult, ALU.add),
                 reads=[bHG[LF][hd], bcon], writes=[bHG[BQ][hd]])
        tap("bq", HG[:, BQ], bHG[BQ], [128, 4, T])
        if d == 0:
            S.op("dve", lambda e: e.tensor_copy(la[:], ch(BQ)[:, :, :, MID]), reads=bHG[BQ], writes=[bla])
            S.op("dve", lambda e: e.tensor_tensor(lbt[:], ch(BQ)[:, :, :, C - 1], ch(BQ)[:, :, :, MID], ALU.subtract), reads=bHG[BQ], writes=[blbt])
            S.op("dve", lambda e: e.tensor_copy(refc[:], ch(BQ)[:, :, :, MID]), reads=bHG[BQ], writes=[brefc])
            S.op("dve", lambda e: e.tensor_tensor(ch(BQ), ch(BQ), refc[:].unsqueeze(3).to_broadcast([128, 4, NCH, C]), ALU.subtract),
                 reads=bHG[BQ] + [brefc], writes=bHG[BQ])
            S.op("act", lambda e: e.activation(flat(LF), flat(BQ), AF.Exp), reads=bHG[BQ], writes=bHG[LF])
            S.op("act", lambda e: e.activation(flat(BQ), flat(BQ), AF.Exp, scale=-1.0), reads=bHG[BQ], writes=bHG[BQ])
        else:
            S.op("dve", lambda e: e.tensor_tensor(flat(LF), flat(BQ), flat(LF), ALU.subtract), reads=bHG[BQ] + bHG[LF], writes=bHG[LF])
            S.op("dve", lambda e: e.tensor_tensor(la[:], ch(BQ)[:, :, :, C - 1], ch(LF)[:, :, :, MID], ALU.subtract), reads=bHG[BQ] + bHG[LF], writes=[bla])
            S.op("dve", lambda e: e.tensor_copy(lbt[:], ch(LF)[:, :, :, MID]), reads=bHG[LF], writes=[blbt])
            S.op("dve", lambda e: e.tensor_copy(refc[:], ch(LF)[:, :, :, MID]), reads=bHG[LF], writes=[brefc])
            S.op("dve", lambda e: e.tensor_tensor(ch(LF), ch(LF), refc[:].unsqueeze(3).to_broadcast([128, 4, NCH, C]), ALU.subtract),
                 reads=bHG[LF] + [brefc], writes=bHG[LF])
            S.op("act", lambda e: e.activation(flat(BQ), flat(LF), AF.Exp), reads=bHG[LF], writes=bHG[BQ])
            S.op("act", lambda e: e.activation(flat(LF), flat(LF), AF.Exp, scale=-1.0), reads=bHG[LF], writes=bHG[LF])
        if d == 0:
            S.op("dve", lambda e: e.tensor_tensor(lg[:, :, 1:NCH], la[:, :, 1:NCH], lbt[:, :, 0:NCH - 1], ALU.add), reads=[bla, blbt], writes=[blg])
            S.op("dve", lambda e: e.tensor_tensor(lg[:, :, 0:1], la[:, :, 0:1], carry[:, 0, :].unsqueeze(2), ALU.add), reads=[bla, bcarry[0]], writes=[blg])
            S.op("dve", lambda e: e.tensor_copy(carry[:, 0, :].unsqueeze(2), lbt[:, :, NCH - 1:NCH]), reads=[blbt], writes=[bcarry[0]])
        else:
            S.op("dve", lambda e: e.tensor_tensor(lg[:, :, 0:NCH - 1], la[:, :, 0:NCH - 1], lbt[:, :, 1:NCH], ALU.add), reads=[bla, blbt], writes=[blg])
            S.op("dve", lambda e: e.tensor_tensor(lg[:, :, NCH - 1:NCH], la[:, :, NCH - 1:NCH], carry[:, 1, :].unsqueeze(2), ALU.add), reads=[bla, bcarry[1]], writes=[blg])
            S.op("dve", lambda e: e.tensor_copy(carry[:, 1, :].unsqueeze(2), lbt[:, :, 0:1]), reads=[blbt], writes=[bcarry[1]])
        S.op("act", lambda e: e.activation(gam[:], lg[:], AF.Exp), reads=[blg], writes=[bgam])
        for hd in range(4):
            S.op(ew(), lambda e, hd=hd: e.tensor_tensor(X1[:, 4 + hd, :], HG[:, KK, hd, :], HG[:, BQ, hd, :], ALU.mult),
                 reads=[bHG[KK][hd], bHG[BQ][hd]], writes=[bX[4 + hd]])
        for hd in range(4):
            src = 8 + hd if d == 1 else hd
            S.op("dve", lambda e, hd=hd, src=src: e.scalar_tensor_tensor(X1[:, hd, :], X1[:, src, :], float(128 ** -0.5), HG[:, LF, hd, :], ALU.mult, ALU.mult),
                 reads=[bX[src], bHG[LF][hd]], writes=[bX[hd]])
        tap("qt", X1[:, 0:4, :], bX[0:4], [128, 4, T])
        tap("kt", X1[:, 4:8, :], bX[4:8], [128, 4, T])
        tap("gam", gam[:], [bgam], [128, 4, NCH])
        if late_hook is not None:
            late_hook()
        for blk in range(4):
            ps, bps = psum_next()
            psb = ps[:, :].bitcast(BF16)
            for hd in range(4):
                S.op("pe", lambda e, hd=hd, blk=blk: e.transpose(psb[:, hd * 128:(hd + 1) * 128], X1[:, 4 + hd, blk * 128:(blk + 1) * 128], identb[:]),
                     reads=[bX[4 + hd], bcst], writes=[bps], inc=(hd == 3))
            S.op("dve", lambda e, blk=blk: e.tensor_copy(ktok[:, blk, :], psb[:, 0:512]), reads=[bps], writes=[bktok[blk]])
        held = psum_hold(4)
        order = range(NCH) if d == 0 else range(NCH - 1, -1, -1)
        mask = cs("mask_f") if d == 0 else cs("mask_b")
        order = list(order)

        def emit_AT(ci_):
            c = order[ci_]
            p0 = 64 * (c % 2)
            cols = slice(c * C, (c + 1) * C)
            ab = ci_ % 2
            psA, bpsA = psum_next()
            for hd in range(4):
                S.op("pe", lambda e, hd=hd: e.matmul(psA[p0:p0 + 64, hd * 64:(hd + 1) * 64], X1[:, 4 + hd, cols], X1[:, hd, cols],
                                                     start=True, stop=True, tile_position=(0, p0)),
                     reads=[bX[4 + hd], bX[hd]], writes=[bpsA], inc=(hd == 3))
            S.op("dve", lambda e: e.tensor_tensor(ATm[p0:p0 + 64, ab, :].rearrange("p (h t) -> p h t", h=4),
                                                  psA[p0:p0 + 64, 0:256].rearrange("p (h t) -> p h t", h=4),
                                                  mask[p0:p0 + 64, :].unsqueeze(1).to_broadcast([64, 4, C]), ALU.mult),
                 reads=[bpsA, bcon], writes=[bATm[ab]])

        emit_AT(0)
        for ci_, c in enumerate(order):
            blk, hf = c // 2, c % 2
            p0 = 64 * hf
            cols = slice(c * C, (c + 1) * C)
            very_first = first and ci_ == 0
            sb_ = ci_ % 2
            ab = ci_ % 2
            if not very_first:
                for hd in range(4):
                    S.op("act", lambda e, hd=hd: e.activation(S16[:, sb_, hd, :], U[:, hd, :], AF.Copy, scale=gam[:, hd, c:c + 1]),
                         reads=[bU[hd], bgam], writes=[bS16[sb_][hd]])
            psU, bpsU = psum_next()
            for hd in range(4):
                S.op("pe", lambda e, hd=hd: e.matmul(psU[:, hd * 128:(hd + 1) * 128], ktok[p0:p0 + 64, blk, hd * 128:(hd + 1) * 128],
                                                     vtok[p0:p0 + 64, blk, hd * 128:(hd + 1) * 128], start=True, stop=True),
                     reads=[bktok[blk], bvtok[blk]], writes=[bpsU], inc=(hd == 3))
            if ci_ + 1 < NCH:
                emit_AT(ci_ + 1)
            for hd in range(4):
                pso, bpso = PS[held[hd]], bPS[held[hd]]
                S.op("pe", lambda e, hd=hd, pso=pso: e.matmul(pso[:, cols], vtok[p0:p0 + 64, blk, hd * 128:(hd + 1) * 128],
                                                              ATm[p0:p0 + 64, ab, hd * 64:(hd + 1) * 64], start=True, stop=very_first),
                     reads=[bvtok[blk], bATm[ab]], writes=[bpso], inc=very_first)
                if not very_first:
                    S.op("pe", lambda e, hd=hd, pso=pso: e.matmul(pso[:, cols], S16[:, sb_, hd, :], X1[:, hd, cols], start=False, stop=True),
                         reads=[bS16[sb_][hd], bX[hd]], writes=[bpso])
            for hd in range(4):
                if very_first:
                    S.op("dve", lambda e, hd=hd: e.tensor_copy(U[:, hd, :], psU[:, hd * 128:(hd + 1) * 128]), reads=[bpsU], writes=[bU[hd]])
                else:
                    S.op("dve", lambda e, hd=hd: e.scalar_tensor_tensor(U[:, hd, :], U[:, hd, :], gam[:, hd, c:c + 1], psU[:, hd * 128:(hd + 1) * 128],
                                                                        ALU.mult, ALU.add),
                         reads=[bU[hd], bgam, bpsU], writes=[bU[hd]])
            if bg is not None:
                next(bg, None)
                next(bg, None)
        if bg is not None:
            for _ in bg:
                pass
        return held

    bob = bufs("ob", NT)
    def p1_prep_gen(ti):
        yield from load_x_tile_gen(ti, nxt=(ti - 1 if ti > 0 else None))
        yield from rmsnorm_gen(lambda k: hT[:, k, H0:H0 + T], bh, "g_mix", T, nTk, bn)
        S.dma("sp", nts_d[ti], nT[:], reads=bn, writes=[bnts[ti]])
        yield

    def p1_prep(ti):
        for _ in p1_prep_gen(ti):
            pass

    for idx, ti in enumerate(range(NT - 1, -1, -1)):
        tok0 = ti * T
        if idx == 0:
            p1_prep(ti)
        S.dma("sp", posi[:], pos_d[:, tok0:tok0 + T].partition_broadcast(128), writes=[bposi])
        posf, bposf = tmpf()
        S.op("dve", lambda e: e.tensor_copy(posf, posi[:]), reads=[bposi], writes=[bposf])
        ct, bct = rot[:, 0, :], brot[0]
        sn, bsn = rot[:, 1, :], brot[1]
        for tb, btb, ph in ((ct, bct, "phase_c"), (sn, bsn, "phase_s")):
            S.op("dve", lambda e, tb=tb, ph=ph: e.tensor_scalar(tb, posf, cs("invf"), cs(ph), ALU.mult, ALU.add), reads=[bposf, bcon], writes=[btb])
            S.op("dve", lambda e, tb=tb: e.tensor_scalar(nint[:], tb, float(1.0 / (2 * np.pi)), None, ALU.mult), reads=[btb], writes=[bnint])
            nf_, bnf_ = tmpf()
            S.op("dve", lambda e: e.tensor_copy(nf_, nint[:]), reads=[bnint], writes=[bnf_])
            S.op("dve", lambda e, tb=tb: e.scalar_tensor_tensor(tb, nf_, float(-2 * np.pi), tb, ALU.mult, ALU.add), reads=[bnf_, btb], writes=[btb])
            S.op("dve", lambda e, tb=tb: e.tensor_scalar(tb, tb, -3.1415925, 3.1415925, ALU.max, ALU.min), reads=[btb], writes=[btb])
            S.op("act", lambda e, tb=tb: e.activation(tb, tb, AF.Sin), reads=[btb], writes=[btb])

        def kv_hook(ti=ti, tok0=tok0):
            wt, bwt = ring_next("ka4")
            for kvh in range(2):
                ps1, bps1 = proj_F(wt, bwt, kvh, nTk, bn, T)
                ps2, bps2 = proj_F(wt, bwt, 2 + kvh, nTk, bn, T)
                t1, bt1 = tmpf()
                t2, bt2 = tmpf()
                S.op("dve", lambda e: e.tensor_tensor(t1, ps1[:, :], ct, ALU.mult), reads=[bps1, bct], writes=[bt1])
                S.op("dve", lambda e: e.tensor_tensor(t2, ps2[:, :], sn, ALU.mult), reads=[bps2, bsn], writes=[bt2])
                S.op("dve", lambda e, kvh=kvh: e.tensor_tensor(KT[:, kvh, tok0:tok0 + T], t1, t2, ALU.add), reads=[bt1, bt2], writes=[bKT[ti]])
            wt, bwt = ring_next("va")
            for blk in range(4):
                ps, bps = psum_next()
                for k in range(KC):
                    S.op("pe", lambda e, k=k, blk=blk: e.matmul(ps[:, 0:128], nT[:, k, blk * 128:(blk + 1) * 128], wt[:, k * 128:(k + 1) * 128],
                                                                start=(k == 0), stop=(k == KC - 1)),
                         reads=[bwt, bn[k]], writes=[bps], inc=(k == KC - 1))
                S.op("act", lambda e, blk=blk: e.copy(Vst[:, ti * 4 + blk, :, 0:64], ps[:, 0:128].rearrange("p (a b) -> p a b", a=2)),
                     reads=[bps], writes=[bV[ti]])
        held = hgrn_tile(ti, 1, idx == 0, mid_hook=kv_hook, late_hook=((lambda ti=ti: p1_prep(ti - 1)) if ti > 0 else None))
        for hd in range(4):
            S.op("act", lambda e, hd=hd: e.copy(HG[:, KK, hd, :], PS[held[hd]][:, :]), reads=[bPS[held[hd]]], writes=[bHG[KK][hd]])
        psum_release(held)
        S.dma("sp", ob_d[:, :, tok0:tok0 + T], HG[:, KK], reads=bHG[KK], writes=[bob[ti]])
        if idx == 0:
            tap("ob", HG[:, KK], bHG[KK], [128, 4, T])
    if stop_after == "phase1":
        tap("KT", KT[:], bKT, [128, 2, SEQ])
        tap("Vst", Vst[:], bV, [128, NBLK, 2, VW])
        return finish(nc, S, dbg_out)

    for half in range(2):
        S.dma("sp", xt[:, half, :], mem_d[128 * half:128 * (half + 1), :], writes=[bxt[half]])
        for h2 in range(2):
            ps, bps = psum_next()
            for kk in range(4):
                k = h2 * 4 + kk
                S.op("pe", lambda e, kk=kk, k=k: e.transpose(ps[:, kk * 128:(kk + 1) * 128], xt[:, half, k * 128:(k + 1) * 128], cs("ident")),
                     reads=[bxt[half], bcon], writes=[bps], inc=(kk == 3))
            S.op("act", lambda e: e.copy(hT[:, h2 * 4:h2 * 4 + 4, H0 + 128 * half: H0 + 128 * (half + 1)],
                                         ps[:, :].rearrange("p (a b) -> p a b", a=4)), reads=[bps], writes=bh[h2 * 4:h2 * 4 + 4])
    rmsnorm(lambda k: hT[:, k, H0:H0 + MEM], bh, "g_memkv", MEM, lambda k: nT[:, k, 0:MEM], bn)
    for w2 in range(2):
        wt, bwt = ring_next(f"wmk{w2}")
        for jb in range(4):
            j = w2 * 4 + jb
            ps, bps = proj_F(wt, bwt, jb, lambda k: nT[:, k, 0:MEM], bn, MEM)
            S.op("act", lambda e, j=j: e.copy(KmT[:, j, :], ps[:, 0:MEM]), reads=[bps], writes=[bKm])
    for w2 in range(2):
        wt, bwt = ring_next(f"wmv{w2}")
        for mc in range(2):
            ps, bps = psum_next()
            for k in range(KC):
                S.op("pe", lambda e, k=k: e.matmul(ps[:, :], nT[:, k, mc * 128:(mc + 1) * 128], wt[:, k * 512:(k + 1) * 512],
                                                   start=(k == 0), stop=(k == KC - 1)),
                     reads=[bwt, bn[k]], writes=[bps], inc=(k == KC - 1))
            S.op("act", lambda e: e.copy(Vm[:, mc, w2 * 512:(w2 + 1) * 512], ps[:, :]), reads=[bps], writes=[bVm])
    prefetch_x(0)
    tap("KmT", KmT[:], [bKm], [128, 8, MEM])
    tap("Vm", Vm[:], [bVm], [128, 2, D])

    OG, QR, OA = 8, 12, 16
    final_toks = []
    hwin = lambda k, N=T: hT[:, k, H0 - 1:H0 - 1 + N]

    def ffn_finish(ti, N, c0, row0, skip_first):
        rmsnorm(lambda k: hT[:, k, c0:c0 + N], bh, "g_fin", N, lambda k: hT[:, k, c0:c0 + N], bh)
        nb = (N + 127) // 128
        for blk in range(nb):
            w = min(128, N - 128 * blk)
            sl = blk % 2
            r0 = row0 + 128 * blk
            for half in range(2):
                ps, bps = psum_next()
                for kk in range(4):
                    k = half * 4 + kk
                    S.op("pe", lambda e, kk=kk, k=k: e.transpose(ps[0:w, kk * 128:(kk + 1) * 128], hT[:, k, c0 + 128 * blk: c0 + 128 * blk + w], cs("ident")),
                         reads=[bh[k], bcon], writes=[bps], inc=(kk == 3))
                S.op("act" if half == 0 else "dve",
                     (lambda e: e.copy(ot[0:w, half, :], ps[0:w, :])) if half == 0 else
                     (lambda e: e.tensor_copy(ot[0:w, half, :], ps[0:w, :])),
                     reads=[bps], writes=[bot[half]])
                cols = slice(half * 512, (half + 1) * 512)
                if skip_first and blk == 0:
                    tok = S.dma("sp", out_d[r0 + 1:r0 + w, cols], ot[1:w, half, :], reads=[bot[half]])
                else:
                    tok = S.dma("sp", out_d[r0:r0 + w, cols], ot[0:w, half, :], reads=[bot[half]])
                final_toks.append(tok)

    for ti in range(NT):
        tok0 = ti * T
        if ti == 0:
            S.dma("sp", nT[:], nts_d[0], reads=[bnts[0]], writes=bn)
        load_x_tile(ti, nxt=(ti + 1 if ti + 1 < NT else None))
        S.dma("sp", posi[:], pos_d[:, tok0:tok0 + T].partition_broadcast(128), writes=[bposi])
        posf, bposf = tmpf()
        S.op("dve", lambda e: e.tensor_copy(posf, posi[:]), reads=[bposi], writes=[bposf])
        ct, bct = rot[:, 0, :], brot[0]
        sn, bsn = rot[:, 1, :], brot[1]
        for tb, btb, ph in ((ct, bct, "phase_c"), (sn, bsn, "phase_s")):
            S.op("dve", lambda e, tb=tb, ph=ph: e.tensor_scalar(tb, posf, cs("invf"), cs(ph), ALU.mult, ALU.add), reads=[bposf, bcon], writes=[btb])
            S.op("dve", lambda e, tb=tb: e.tensor_scalar(nint[:], tb, float(1.0 / (2 * np.pi)), None, ALU.mult), reads=[btb], writes=[bnint])
            nf_, bnf_ = tmpf()
            S.op("dve", lambda e: e.tensor_copy(nf_, nint[:]), reads=[bnint], writes=[bnf_])
            S.op("dve", lambda e, tb=tb: e.scalar_tensor_tensor(tb, nf_, float(-2 * np.pi), tb, ALU.mult, ALU.add), reads=[bnf_, btb], writes=[btb])
            S.op("dve", lambda e, tb=tb: e.tensor_scalar(tb, tb, -3.1415925, 3.1415925, ALU.max, ALU.min), reads=[btb], writes=[btb])
            S.op("act", lambda e, tb=tb: e.activation(tb, tb, AF.Sin), reads=[btb], writes=[btb])

        def attention_qb(qb):
            gb = ti * 4 + qb
            kbs = [kb for kb in (gb - 1, gb, gb + 1) if 0 <= kb < NBLK]
            s0 = kbs[0] - (gb - 1)
            ns = len(kbs)
            pso2 = [psum_hold(1)[0], psum_hold(1)[0]]
            def att_scores(h):
                kvh, j, p0 = h // 4, h // 2, 64 * (h % 2)
                ps, bps = psum_next()
                for kb in kbs:
                    s_ = kb - (gb - 1)
                    S.op("pe", lambda e, s_=s_, kb=kb: e.matmul(ps[:, s_ * 128:(s_ + 1) * 128], KT[p0:p0 + 64, kvh, kb * 128:(kb + 1) * 128],
                                                                X1[p0:p0 + 64, QR + j, qb * 128:(qb + 1) * 128], start=True, stop=True),
                         reads=[bKT[kb // 4], bX[QR + j]], writes=[bps], inc=(kb == kbs[-1]))
                pi_ = (qb * 8 + h) % NPT
                S.op("act", lambda e: e.activation(PT[:, pi_, s0:s0 + ns, :], ps[:, s0 * 128:(s0 + ns) * 128].rearrange("p (a b) -> p a b", a=ns),
                                                   AF.Exp, scale=0.125), reads=[bps], writes=[bPT[pi_]])
                S.op("dve", lambda e: e.tensor_tensor(PT[:, pi_, s0:s0 + ns, :], PT[:, pi_, s0:s0 + ns, :], bandb[:, s0:s0 + ns, :], ALU.mult),
                     reads=[bPT[pi_], bcst], writes=[bPT[pi_]])

            def att_pv(h):
                kvh = h // 4
                pi_ = (qb * 8 + h) % NPT
                po, bpo = PS[pso2[h // 4]], bPS[pso2[h // 4]]
                hh = h % 4
                for kb in kbs:
                    s_ = kb - (gb - 1)
                    S.op("pe", lambda e, s_=s_, kb=kb: e.matmul(po[:, hh * 65:hh * 65 + 65], PT[:, pi_, s_, :], Vst[:, kb, kvh, 0:65],
                                                                start=(kb == kbs[0]), stop=(kb == kbs[-1])),
                         reads=[bPT[pi_], bV[kb // 4]], writes=[bpo], inc=(kb == kbs[-1]))

            for h in range(8):
                att_scores(h)
                if h > 1:
                    att_pv(h - 2)
            att_pv(6)
            att_pv(7)
            osl = qb % 2
            for hb in range(2):
                po, bpo = PS[pso2[hb]], bPS[pso2[hb]]
                pv = po[:, 0:260].rearrange("p (h c) -> p h c", c=65)
                S.op("dve", lambda e: e.tensor_tensor(dsm[:, 0, hb * 4:hb * 4 + 4].unsqueeze(2), pv[:, :, 64:65], sm[:, 24 + hb * 4:28 + hb * 4].unsqueeze(2), ALU.add),
                     reads=[bpo, bsm], writes=[bdsm[0]])
                S.op("dve", lambda e: e.reciprocal(dsm[:, 1, hb * 4:hb * 4 + 4], dsm[:, 0, hb * 4:hb * 4 + 4]), reads=[bdsm[0]], writes=[bdsm[1]])
                S.op("dve", lambda e: e.tensor_tensor(oatok[:, osl, hb * 256:(hb + 1) * 256].rearrange("p (h c) -> p h c", c=64), pv[:, :, 0:64],
                                                      dsm[:, 1, hb * 4:hb * 4 + 4].unsqueeze(2).to_broadcast([128, 4, 64]), ALU.mult),
                     reads=[bpo, bdsm[1]], writes=[boatok[osl]])
            psum_release(pso2)
            ps, bps = psum_next()
            psb = ps[:, :].bitcast(BF16)
            for kc in range(4):
                S.op("pe", lambda e, kc=kc: e.transpose(psb[:, kc * 128:(kc + 1) * 128], oatok[:, osl, kc * 128:(kc + 1) * 128], identb[:]),
                     reads=[boatok[osl], bcst], writes=[bps], inc=(kc == 3))
            S.op("act", lambda e: e.copy(X1[:, OA:OA + 4, qb * 128:(qb + 1) * 128], psb[:, 0:512].rearrange("p (a b) -> p a b", a=4)),
                 reads=[bps], writes=bX[OA:OA + 4])

        def mix_hook():
            wt, bwt = ring_next("gr")
            for hd in range(4):
                psg, bpsg = proj_F(wt, bwt, hd, nTk, bn, T)
                S.op("act", lambda e, hd=hd: e.activation(B8[:, 4 + hd, :], psg[:, :], AF.Silu), reads=[bpsg], writes=[bB[4 + hd]])
            for jj in range(2):
                wt, bwt = ring_next(f"qq{jj}")
                for sub in range(2):
                    j = 2 * jj + sub
                    ps1, bps1 = proj_F(wt, bwt, 2 * sub, nTk, bn, T)
                    ps2, bps2 = proj_F(wt, bwt, 2 * sub + 1, nTk, bn, T)
                    t1, bt1 = tmpf()
                    t2, bt2 = tmpf()
                    S.op("dve", lambda e: e.tensor_tensor(t1, ps1[:, :], ct, ALU.mult), reads=[bps1, bct], writes=[bt1])
                    S.op("dve", lambda e: e.tensor_tensor(t2, ps2[:, :], sn, ALU.mult), reads=[bps2, bsn], writes=[bt2])
                    S.op("dve", lambda e, j=j: e.tensor_tensor(X1[:, QR + j, :], t1, t2, ALU.add), reads=[bt1, bt2], writes=[bX[QR + j]])

        def mix_hook2():
            mix_hook()
            attention_qb(0)
            attention_qb(1)
            attention_qb(2)
            attention_qb(3)

        held = hgrn_tile(ti, 0, ti == 0, mid_hook=mix_hook2)
        S.dma("sp", HG[:, LF], ob_d[:, :, tok0:tok0 + T], reads=[bob[ti]], writes=bHG[LF])
        for hd in range(4):
            S.op("dve", lambda e, hd=hd: e.tensor_tensor(HG[:, KK, hd, :], PS[held[hd]][:, :], HG[:, LF, hd, :], ALU.add),
                 reads=[bPS[held[hd]], bHG[LF][hd]], writes=[bHG[KK][hd]])
        psum_release(held)
        if ti == 0:
            tap("osum", HG[:, KK], bHG[KK], [128, 4, T])
        def o_norm(hd):
            S.op("act", lambda e: e.activation(B8[:, hd, :], HG[:, KK, hd, :], AF.Square), reads=[bHG[KK][hd]], writes=[bB[hd]])
            ps, bps = psum_next()
            S.op("pe", lambda e: e.matmul(ps[:, :], onesV[:], B8[:, hd, :], start=True, stop=True), reads=[bB[hd], bcst], writes=[bps])
            rs, brs = tmpf()
            S.op("act", lambda e: e.activation(rs, ps[:, :], AF.Ln, bias=sm[:, 40:41]), reads=[bps, bsm], writes=[brs])
            S.op("act", lambda e: e.activation(rs, rs, AF.Exp, scale=-0.5), reads=[brs], writes=[brs])
            S.op("dve", lambda e: e.tensor_tensor(rs, rs, HG[:, KK, hd, :], ALU.mult), reads=[brs, bHG[KK][hd]], writes=[brs])
            S.op("dve", lambda e: e.scalar_tensor_tensor(X1[:, OG + hd, :], rs, pr("hgn", hd, hd + 1), B8[:, 4 + hd, :], ALU.mult, ALU.mult),
                 reads=[brs, bB[4 + hd], bpar], writes=[bX[OG + hd]])
        if ti == 0:
            tap("qrT", X1[:, QR:QR + 4, :], bX[QR:QR + 4], [128, 4, T])
        o_norm(0)
        o_norm(1)
        o_norm(2)
        o_norm(3)
        if ti == 0:
            tap("og", X1[:, OG:OG + 4, :], bX[OG:OG + 4], [128, 4, T])
        if ti == 0:
            tap("oaT", X1[:, OA:OA + 4, :], bX[OA:OA + 4], [128, 4, T])
        for j in range(8):
            wt, bwt = ring_next(f"mg{j}")
            psr, bpsr = proj_F(wt, bwt, 0, nTk, bn, T)
            psa, bpsa = proj_F(wt[:, 1024:], bwt, 0, nTk, bn, T)
            sr, bsr = tmpf()
            sa, bsa = tmpf()
            S.op("act", lambda e: e.activation(sr, psr[:, :], AF.Sigmoid), reads=[bpsr], writes=[bsr])
            S.op("act", lambda e: e.activation(sa, psa[:, :], AF.Sigmoid), reads=[bpsa], writes=[bsa])
            pyr, bpyr = proj_F(wt[:, 2048:], bwt, 0, lambda k: X1[:, OG + k, :], bX[OG:OG + 4], T, nk=4)
            pya, bpya = proj_F(wt[:, 2560:], bwt, 0, lambda k: X1[:, OA + k, :], bX[OA:OA + 4], T, nk=4)
            S.op("dve", lambda e: e.tensor_tensor(sr, pyr[:, :], sr, ALU.mult), reads=[bpyr, bsr], writes=[bsr])
            S.op("dve", lambda e: e.tensor_tensor(sa, pya[:, :], sa, ALU.mult), reads=[bpya, bsa], writes=[bsa])
            S.op("dve", lambda e, j=j: e.tensor_tensor(B8[:, j, :], sr, sa, ALU.add), reads=[bsr, bsa], writes=[bB[j]])
        if ti == 0:
            tap("merged", B8[:], bB, [128, 8, T])
        for w2 in range(2):
            wt, bwt = ring_next(f"wout{w2}")
            for jb in range(4):
                j = w2 * 4 + jb
                ps, bps = proj_F(wt, bwt, jb, lambda k: B8[:, k, :], bB, T)
                S.op("dve", lambda e, j=j: e.tensor_tensor(hT[:, j, H0:H0 + T], hT[:, j, H0:H0 + T], ps[:, :], ALU.add), reads=[bps, bh[j]], writes=[bh[j]])
        if ti == 0:
            tap("h1", hT[:], bh, [128, KC, HWID])
        rmsnorm(lambda k: hT[:, k, H0:H0 + T], bh, "g_mem", T, nTk, bn)
        QM, OM = 0, 8
        for w2 in range(2):
            wt, bwt = ring_next(f"wmq{w2}")
            for jb in range(4):
                j = w2 * 4 + jb
                ps, bps = proj_F(wt, bwt, jb, nTk, bn, T)
                S.op("act", lambda e, j=j: e.copy(X1[:, QM + j, :], ps[:, :]), reads=[bps], writes=[bX[QM + j]])
        def mem_scores(h):
            pb = 16 + 2 * (h % 2)
            for mc in range(2):
                ps, bps = psum_next()
                for dc in range(2):
                    S.op("pe", lambda e, dc=dc: e.matmul(ps[:, :], KmT[:, 2 * h + dc, mc * 128:(mc + 1) * 128], X1[:, QM + 2 * h + dc, :],
                                                         start=(dc == 0), stop=(dc == 1)),
                         reads=[bKm, bX[QM + 2 * h + dc]], writes=[bps], inc=(dc == 1))
                S.op("act", lambda e: e.activation(X1[:, pb + mc, :], ps[:, :], AF.Exp, scale=1.0 / 16), reads=[bps], writes=[bX[pb + mc]])

        def mem_pv(h):
            pb = 16 + 2 * (h % 2)
            psd, bpsd = psum_next()
            for mc in range(2):
                S.op("pe", lambda e: e.matmul(psd[:, :], ones1[:], X1[:, pb + mc, :], start=(mc == 0), stop=(mc == 1)),
                     reads=[bX[pb + mc], bcst], writes=[bpsd], inc=(mc == 1))
            rd, brd = tmpf()
            S.op("dve", lambda e: e.reciprocal(rd, psd[:, :]), reads=[bpsd], writes=[brd])
            for dc in range(2):
                ps, bps = psum_next()
                for mc in range(2):
                    S.op("pe", lambda e: e.matmul(ps[:, :], Vm[:, mc, h * 256 + dc * 128: h * 256 + (dc + 1) * 128], X1[:, pb + mc, :],
                                                  start=(mc == 0), stop=(mc == 1)),
                         reads=[bVm, bX[pb + mc]], writes=[bps], inc=(mc == 1))
                S.op("dve", lambda e: e.tensor_tensor(X1[:, OM + 2 * h + dc, :], ps[:, :], rd, ALU.mult), reads=[bps, brd], writes=[bX[OM + 2 * h + dc]])

        for h in range(4):
            mem_scores(h)
            if h > 0:
                mem_pv(h - 1)
        mem_pv(3)
        for w2 in range(2):
            wt, bwt = ring_next(f"wmo{w2}")
            for jb in range(4):
                j = w2 * 4 + jb
                ps, bps = proj_F(wt, bwt, jb, lambda k: X1[:, OM + k, :], bX[OM:OM + 8], T)
                S.op("dve", lambda e, j=j: e.tensor_tensor(hT[:, j, H0:H0 + T], hT[:, j, H0:H0 + T], ps[:, :], ALU.add), reads=[bps, bh[j]], writes=[bh[j]])
        if ti == 0:
            tap("h2", hT[:], bh, [128, KC, HWID])
        rmsnorm(lambda k: hT[:, k, H0:H0 + T], bh, "g_ffn", T, nTk, bn)
        cw = lambda j_, jf: pr("conv_w", j_ * JF + jf, j_ * JF + jf + 1)
        pend = []

        def ffn_B(jf, c_, bc_, psu, bpsu):
            S.op("act", lambda e: e.activation(c_, c_, AF.Silu), reads=[bc_], writes=[bc_])
            S.op("dve", lambda e: e.tensor_tensor(X1[:, jf, 1:T], c_[:, 1:T], psu[:, 0:T - 1], ALU.mult), reads=[bc_, bpsu], writes=[bX[jf]])
            S.op("dve", lambda e: e.tensor_tensor(X1[:, jf, 0:1], c_[:, 0:1], ucar[:, jf:jf + 1], ALU.mult), reads=[bc_, bucar[jf]], writes=[bX[jf]])
            S.op("dve", lambda e: e.tensor_copy(ucar[:, jf:jf + 1], psu[:, T - 1:T]), reads=[bpsu], writes=[bucar[jf]])

        for jj in range(JF // 2):
            wt, bwt = ring_next(f"up{jj}")
            for sub in range(2):
                jf = 2 * jj + sub
                psu, bpsu = proj_F(wt, bwt, 2 * sub, nTk, bn, T)
                psg, bpsg = proj_F(wt, bwt, 2 * sub + 1, nTk, bn, T)
                gi = jf % NGS
                g_, bg_ = gs[:, gi, :], bgs[gi]
                S.op("act", lambda e: e.copy(g_[:, 2:T + 2], psg[:, :]), reads=[bpsg], writes=[bg_])
                S.op("dve", lambda e: e.tensor_copy(g_[:, 0:2], gcar[:, jf, :]), reads=[bgcar[jf]], writes=[bg_])
                S.op("dve", lambda e: e.tensor_copy(gcar[:, jf, :], g_[:, T:T + 2]), reads=[bg_], writes=[bgcar[jf]])
                c_, bc_ = tmpf()
                S.op("dve", lambda e: e.tensor_scalar(c_, g_[:, 0:T], cw(0, jf), pr("conv_b", jf, jf + 1), ALU.mult, ALU.add), reads=[bg_, bpar], writes=[bc_])
                S.op("dve", lambda e: e.scalar_tensor_tensor(c_, g_[:, 1:T + 1], cw(1, jf), c_, ALU.mult, ALU.add), reads=[bg_, bpar, bc_], writes=[bc_])
                S.op("dve", lambda e: e.scalar_tensor_tensor(c_, g_[:, 2:T + 2], cw(2, jf), c_, ALU.mult, ALU.add), reads=[bg_, bpar, bc_], writes=[bc_])
                pend.append((jf, c_, bc_, psu, bpsu))
                if len(pend) > 1:
                    ffn_B(*pend.pop(0))
        while pend:
            ffn_B(*pend.pop(0))
        if ti + 1 < NT:
            S.dma("sp", nT[:], nts_d[ti + 1], reads=[bnts[ti + 1]], writes=bn)
        if ti == 0:
            tap("act", X1[:], bX, [128, JF, T])
        for j in range(8):
            wt, bwt = ring_next(f"dn{j}")
            ps, bps = proj_F(wt, bwt, 0, lambda k: X1[:, k, :], bX, T, nk=JF)
            S.op("dve", lambda e, j=j: e.tensor_tensor(hwin(j), hwin(j), ps[:, :], ALU.add), reads=[bps, bh[j]], writes=[bh[j]])
        ffn_finish(ti, T, H0 - 1, tok0 - 1, skip_first=(ti == 0))
        S.op("pool", lambda e: e.tensor_copy(hT[:, :, H0 - 1:H0], hT[:, :, H0 + T - 1:H0 + T]), reads=bh, writes=bh)

    cwa = lambda j_: pr("conv_w", j_ * JF, (j_ + 1) * JF)
    S.op("dve", lambda e: e.tensor_tensor(small[:, 0, :], gcar[:, :, 0], cwa(0), ALU.mult), reads=bgcar + [bpar], writes=[bsmall[0]])
    S.op("dve", lambda e: e.tensor_tensor(small[:, 1, :], gcar[:, :, 1], cwa(1), ALU.mult), reads=bgcar + [bpar], writes=[bsmall[1]])
    S.op("dve", lambda e: e.tensor_tensor(small[:, 0, :], small[:, 0, :], small[:, 1, :], ALU.add), reads=[bsmall[0], bsmall[1]], writes=[bsmall[0]])
    S.op("dve", lambda e: e.tensor_tensor(small[:, 2, :], small[:, 0, :], pr("conv_b"), ALU.add), reads=[bsmall[0], bpar], writes=[bsmall[2]])
    S.op("act", lambda e: e.activation(small[:, 3, :], small[:, 2, :], AF.Silu), reads=[bsmall[2]], writes=[bsmall[3]])
    S.op("dve", lambda e: e.tensor_tensor(actl[:], small[:, 3, :], ucar[:], ALU.mult), reads=[bsmall[3]] + bucar, writes=[bactl])
    for j in range(8):
        wt, bwt = ring_next(f"dn{j}")
        ps, bps = psum_next()
        for kc in range(JF):
            S.op("pe", lambda e, kc=kc: e.matmul(ps[:, 0:1], wt[:, kc * 128:(kc + 1) * 128], actl[:, kc:kc + 1], start=(kc == 0), stop=(kc == JF - 1)),
                 reads=[bwt, bactl], writes=[bps], inc=(kc == JF - 1))
        S.op("dve", lambda e, j=j: e.tensor_tensor(hT[:, j, H0 - 1:H0], hT[:, j, H0 - 1:H0], ps[:, 0:1], ALU.add), reads=[bps, bh[j]], writes=[bh[j]])
    ffn_finish(NT, 1, H0 - 1, SEQ - 1, skip_first=False)
    return finish(nc, S, dbg_out, final_toks)


def finish(nc, S, dbg_out, final_toks=()):
    for tok in list(dbg_out.values()) + list(final_toks):
        S.wait_tok("sp", tok)
    return nc, S, dbg_out


_CACHE = {}


def _get_program(SEQ):
    if SEQ not in _CACHE:
        _CACHE[SEQ] = build_program(SEQ)
    return _CACHE[SEQ]


def kernel(**inputs):
    inp = {k: np.asarray(v) for k, v in inputs.items()}
    x = inp["x"].astype(np.float32, copy=False)
    B, SEQ, _ = x.shape
    wall, index, bounds = build_wall(inp)
    par, _ = build_params(inp)
    con, _ = build_consts()
    nc, S, _ = _get_program(SEQ)
    in_maps = []
    for b in range(B):
        in_maps.append({
            "x": np.ascontiguousarray(x[b]),
            "mem": np.ascontiguousarray(inp["mem"][b].astype(np.float32, copy=False)),
            "pos": np.ascontiguousarray(inp["positions"][b].astype(np.int32, copy=False).reshape(1, SEQ)),
            "wall": wall, "par": par, "con": con,
        })
    res = run_bass_kernel_spmd(nc, in_maps, core_ids=list(range(B)))
    return np.stack([np.asarray(r["out"], dtype=np.float32) for r in res.results], axis=0)
```

```python
import numpy as np
import ml_dtypes
import concourse.bass as bass
import concourse.mybir as mybir
from concourse.bass_utils import run_bass_kernel_spmd

F32 = mybir.dt.float32
BF16 = mybir.dt.bfloat16
I32 = mybir.dt.int32
AF = mybir.ActivationFunctionType
ALU = mybir.AluOpType
AX = mybir.AxisListType


class Buf:
    __slots__ = ("name", "w", "r", "ld_sem", "ld_cnt", "st_sem", "st_cnt")

    def __init__(self, name):
        self.name = name
        self.w = None
        self.r = {}
        self.ld_sem = None
        self.ld_cnt = 0
        self.st_sem = None
        self.st_cnt = 0


class Sched:
    SEM_LIMIT = 60000

    def __init__(self, nc):
        self.nc = nc
        self.eng = {"pe": nc.tensor, "act": nc.scalar, "dve": nc.vector,
                    "pool": nc.gpsimd, "sp": nc.sync}
        self.sem = {}
        self.cnt = {}
        self.nsem = 0
        for e in ("pe", "act", "dve", "pool"):
            self._new_sem(e)
        self.waited = {e: {} for e in self.eng}
        self.n_ops = {e: 0 for e in self.eng}
        self.n_waits = {e: 0 for e in self.eng}

    def _new_sem(self, e):
        self.nsem += 1
        self.sem[e] = self.nc.alloc_semaphore(f"s_{e}_{self.nsem}")
        self.cnt[e] = 0

    def buf(self, name):
        return Buf(name)

    def _wait(self, e, sem, val):
        w = self.waited[e]
        if w.get(sem, 0) >= val:
            return
        self.eng[e].wait_ge(sem, val)
        w[sem] = val
        self.n_waits[e] += 1

    def _deps(self, e, reads, writes, own):
        strict = e != "pe"
        for b in reads:
            if b.w is not None:
                self._wait(e, *b.w)
        for b in writes:
            if b.w is not None and (strict or b.w[0] is not own):
                self._wait(e, *b.w)
            for s, v in b.r.items():
                if strict or s is not own:
                    self._wait(e, s, v)

    def op(self, e, fn, reads=(), writes=(), inc=True):
        if inc and self.cnt[e] >= self.SEM_LIMIT:
            self._new_sem(e)
        own = self.sem[e]
        self._deps(e, reads, writes, own)
        ins = fn(self.eng[e])
        self.n_ops[e] += 1
        if inc:
            self.cnt[e] += 1
            ins.then_inc(own, 1)
            val = self.cnt[e]
        else:
            val = self.cnt[e] + 1
        for b in reads:
            if b.r.get(own, 0) < val:
                b.r[own] = val
        for b in writes:
            b.w = (own, val)
            b.r = {}
        return ins

    def dma(self, q, out_ap, in_ap, reads=(), writes=(), **kw):
        own = None
        self._deps(q, reads, writes, own)
        ins = self.eng[q].dma_start(out=out_ap, in_=in_ap, **kw)
        self.n_ops[q] += 1
        if writes:
            b = writes[0]
            if b.ld_sem is None or b.ld_cnt >= self.SEM_LIMIT:
                self.nsem += 1
                b.ld_sem = self.nc.alloc_semaphore(f"ld_{b.name}_{self.nsem}")
                b.ld_cnt = 0
            b.ld_cnt += 16
            ins.then_inc(b.ld_sem, 16)
            tok = (b.ld_sem, b.ld_cnt)
            for wb in writes:
                wb.w = tok
                wb.r = {}
            for rb in reads:
                rb.r[tok[0]] = tok[1]
        else:
            b = reads[0]
            if b.st_sem is None or b.st_cnt >= self.SEM_LIMIT:
                self.nsem += 1
                b.st_sem = self.nc.alloc_semaphore(f"st_{b.name}_{self.nsem}")
                b.st_cnt = 0
            b.st_cnt += 16
            ins.then_inc(b.st_sem, 16)
            tok = (b.st_sem, b.st_cnt)
            for rb in reads:
                rb.r[tok[0]] = tok[1]
        return tok

    def wait_tok(self, e, tok):
        self._wait(e, tok[0], tok[1])


D = 1024
KC = 8
T = 512
C = 64
NCH = T // C
MID = 31
H0 = 4
HWID = T + H0
DFF = 2816
JF = DFF // 128
MEM = 256
EPS = 1e-6
SLOT = 4096
NSLOT = 4
CHUNK0 = 4096
ROPE_THETA = 500000.0


def _f_tiles(W):
    Kd, N = W.shape
    return np.ascontiguousarray(W.reshape(Kd // 128, 128, N // 128, 128).transpose(1, 2, 0, 3))


def _t_tiles(W):
    Kd, N = W.shape
    return np.ascontiguousarray(W.reshape(Kd // 128, 128, N).transpose(1, 0, 2))


def _partner(W, nh):
    Wh = W.reshape(W.shape[0], nh, 64)
    P = np.zeros_like(Wh)
    P[:, :, 0:8] = Wh[:, :, 8:16]
    P[:, :, 8:16] = Wh[:, :, 0:8]
    return P.reshape(W.shape)


def build_wall(inp):
    w_in = inp["w_in"][0]
    segs = []

    def add(name, arr):
        segs.append((name, arr.reshape(128, -1)))

    ka = w_in[:, 3072:3200]
    kdup = np.concatenate([ka[:, 0:64], ka[:, 0:64], ka[:, 64:128], ka[:, 64:128]], axis=1)
    kdup_p = _partner(kdup, 4)
    add("fzb", _f_tiles(w_in[:, 1024:1536]))
    gA = len(segs)
    add("qr", _f_tiles(w_in[:, 0:512]))
    add("ir", _t_tiles(w_in[:, 1536:2048]))
    add("ka4", _f_tiles(np.concatenate([kdup, kdup_p], axis=1)))
    add("va", _t_tiles(w_in[:, 3200:3328]))
    g0 = len(segs)
    add("fzf", _f_tiles(w_in[:, 512:1024]))
    add("gr", _f_tiles(w_in[:, 2048:2560]))
    qa = w_in[:, 2560:3072]
    qa_t = _f_tiles(qa)
    qp_t = _f_tiles(_partner(qa, 8))
    for jj in range(2):
        add(f"qq{jj}", np.concatenate([qa_t[:, 2 * jj].reshape(128, -1), qp_t[:, 2 * jj].reshape(128, -1),
                                       qa_t[:, 2 * jj + 1].reshape(128, -1), qp_t[:, 2 * jj + 1].reshape(128, -1)], axis=1))
    gr_t = _f_tiles(w_in[:, 3328:4352])
    ga_t = _f_tiles(w_in[:, 4352:5376])
    br_t = _f_tiles(inp["w_br_rec"][0])
    ba_t = _f_tiles(inp["w_br_att"][0])
    for j in range(8):
        add(f"mg{j}", np.concatenate([gr_t[:, j].reshape(128, -1), ga_t[:, j].reshape(128, -1),
                                      br_t[:, j].reshape(128, -1), ba_t[:, j].reshape(128, -1)], axis=1))
    wo = _f_tiles(inp["w_mix_out"][0])
    add("wout0", wo[:, 0:4]); add("wout1", wo[:, 4:8])
    g1 = len(segs)
    wq = _f_tiles(inp["w_mem_q"][0])
    add("wmq0", wq[:, 0:4]); add("wmq1", wq[:, 4:8])
    wk = _f_tiles(inp["w_mem_kv"][0][:, 0:1024])
    add("wmk0", wk[:, 0:4]); add("wmk1", wk[:, 4:8])
    add("wmv0", _t_tiles(inp["w_mem_kv"][0][:, 1024:1536]))
    add("wmv1", _t_tiles(inp["w_mem_kv"][0][:, 1536:2048]))
    wmo = _f_tiles(inp["w_mem_o"][0])
    add("wmo0", wmo[:, 0:4]); add("wmo1", wmo[:, 4:8])
    g2 = len(segs)
    wu = _f_tiles(inp["w_up"][0][:, 0:DFF])
    wg = _f_tiles(inp["w_up"][0][:, DFF:2 * DFF])
    for jj in range(JF // 2):
        add(f"up{jj}", np.concatenate([wu[:, 2 * jj].reshape(128, -1), wg[:, 2 * jj].reshape(128, -1),
                                       wu[:, 2 * jj + 1].reshape(128, -1), wg[:, 2 * jj + 1].reshape(128, -1)], axis=1))
    wd = _f_tiles(inp["w_down"][0])
    for j in range(8):
        add(f"dn{j}", wd[:, j])
    g3 = len(segs)
    index = {}
    off = 0
    for name, arr in segs:
        assert arr.shape[1] <= SLOT, (name, arr.shape)
        index[name] = (off, arr.shape[1])
        off += arr.shape[1]
    wall = np.ascontiguousarray(np.concatenate([a for _, a in segs], axis=1).astype(np.float32))
    bounds = [index[segs[g - 1][0]][0] + index[segs[g - 1][0]][1] for g in (gA, g0, g1, g2, g3)]
    return wall, index, bounds


def wall_index():
    index = {}
    off = 0
    order = ([("fzb", 4096)], [("qr", 4096), ("ir", 4096), ("ka4", 4096), ("va", 1024)],
             [("fzf", 4096), ("gr", 4096), ("qq0", 4096), ("qq1", 4096)] + [(f"mg{j}", 3072) for j in range(8)]
             + [("wout0", 4096), ("wout1", 4096)],
             [("wmq0", 4096), ("wmq1", 4096), ("wmk0", 4096), ("wmk1", 4096), ("wmv0", 4096), ("wmv1", 4096),
              ("wmo0", 4096), ("wmo1", 4096)],
             [(f"up{j}", 4096) for j in range(JF // 2)] + [(f"dn{j}", 2816) for j in range(8)])
    bounds = []
    for grp in order:
        for name, L in grp:
            index[name] = (off, L)
            off += L
        bounds.append(off)
    return index, bounds, off


def build_params(inp):
    cols = []
    names = {}

    def add(name, arr):
        arr = np.asarray(arr, np.float32).reshape(128, -1)
        names[name] = (sum(c.shape[1] for c in cols), arr.shape[1])
        cols.append(arr)

    def pk(v):
        v = np.asarray(v, np.float32).reshape(-1, 128)
        return np.ascontiguousarray(v.T)

    add("g_mix", pk(inp["norm_mix"][0]))
    add("g_mem", pk(inp["norm_mem"][0]))
    add("g_memkv", pk(inp["norm_mem_kv"][0]))
    add("g_ffn", pk(inp["norm_ffn"][0]))
    add("g_fin", pk(inp["final_norm"]))
    lbr = inp["lower_bounds"]
    add("lb_raw", np.stack([pk(lbr[d, s]) for d in range(2) for s in range(2)], axis=1))
    add("hgn", pk(inp["hg_norm"][0]))
    add("sink", np.broadcast_to(np.asarray(inp["attn_sink"][0], np.float32)[None, :], (128, 8)))
    add("conv_w", np.stack([pk(inp["conv_w"][0][j]) for j in range(3)], axis=1))
    add("conv_b", pk(inp["conv_b"][0]))
    par = np.ascontiguousarray(np.concatenate(cols, axis=1))
    return par, names


def params_index():
    names = {}
    off = 0
    for n, L in (("g_mix", 8), ("g_mem", 8), ("g_memkv", 8), ("g_ffn", 8), ("g_fin", 8), ("lb_raw", 16),
                 ("hgn", 4), ("sink", 8), ("conv_w", 66), ("conv_b", 22)):
        names[n] = (off, L)
        off += L
    return names, off


def build_consts():
    cols = {}
    p = np.arange(128)
    ident = np.eye(128, dtype=np.float32)
    s = (p % 64)[:, None]
    t = np.arange(64)[None, :]
    mask_f = (s <= t).astype(np.float32)
    mask_b = (s >= t).astype(np.float32)
    key = p[:, None]
    q = np.arange(128)[None, :]
    band_prev = (key >= q).astype(np.float32)
    band_next = (key <= q).astype(np.float32)
    m01 = np.ones((128, T), np.float32)
    m01[:, ::C] = 0.0
    inv_freq = 1.0 / (ROPE_THETA ** (np.arange(0, 16, 2, dtype=np.float32) / 16.0))
    invf = np.zeros((128, 1), np.float32)
    phase_c = np.full((128, 1), 0.5 * np.pi, np.float32)
    phase_s = np.zeros((128, 1), np.float32)
    for pp in range(128):
        r = pp % 64
        if r < 16:
            invf[pp, 0] = inv_freq[r % 8]
            phase_s[pp, 0] = np.pi if r < 8 else 0.0
    ones = np.ones((128, 128), np.float32)
    order = [("ident", ident), ("mask_f", mask_f), ("mask_b", mask_b), ("band_prev", band_prev),
             ("band_next", band_next), ("m01", m01), ("invf", invf), ("phase_c", phase_c), ("phase_s", phase_s), ("ones", ones)]
    names = {}
    off = 0
    arrs = []
    for n, a in order:
        names[n] = (off, a.shape[1])
        off += a.shape[1]
        arrs.append(a.astype(np.float32))
    return np.ascontiguousarray(np.concatenate(arrs, axis=1)), names


def weight_plan(NT):
    plan = []
    for _ in range(NT):
        plan += ["fzb", "qr", "ir", "ka4", "va"]
    plan += ["wmk0", "wmk1", "wmv0", "wmv1"]
    for _ in range(NT):
        plan += ["fzf", "gr", "qq0", "qq1"]
        plan += [f"mg{j}" for j in range(8)]
        plan += ["wout0", "wout1", "wmq0", "wmq1", "wmo0", "wmo1"]
        plan += [f"up{j}" for j in range(JF // 2)]
        plan += [f"dn{j}" for j in range(8)]
    plan += [f"dn{j}" for j in range(8)]
    return plan


def build_program(SEQ, dbg_taps=(), stop_after=None):
    nc = bass.Bass("TRN2", target_bir_lowering=False)
    S = Sched(nc)
    NT = SEQ // T
    NBLK = SEQ // 128
    widx, wbounds, LTOT = wall_index()
    pidx, NPAR = params_index()
    _, cidx = build_consts()
    NCON = sum(v[1] for v in cidx.values())

    x_d = nc.dram_tensor("x", [SEQ, D], F32, kind="ExternalInput").ap()
    mem_d = nc.dram_tensor("mem", [MEM, D], F32, kind="ExternalInput").ap()
    pos_d = nc.dram_tensor("pos", [1, SEQ], I32, kind="ExternalInput").ap()
    wall_d = nc.dram_tensor("wall", [128, LTOT], F32, kind="ExternalInput").ap()
    par_d = nc.dram_tensor("par", [128, NPAR], F32, kind="ExternalInput").ap()
    con_d = nc.dram_tensor("con", [128, NCON], F32, kind="ExternalInput").ap()
    out_d = nc.dram_tensor("out", [SEQ, D], F32, kind="ExternalOutput").ap()
    wscr_d = nc.dram_tensor("wscr", [128, LTOT], BF16).ap()
    ob_d = nc.dram_tensor("obscr", [128, 4, SEQ], F32).ap()
    nts_d = nc.dram_tensor("ntscr", [NT, 128, KC * T], BF16).ap()
    vs_d = nc.dram_tensor("vscr", [NT, 128, 4 * 512], BF16).ap()
    sqs_d = nc.dram_tensor("sqscr", [NT, 128, 4 * T], BF16).ap()
    dbg_out = {}

    def sb(name, shape, dt):
        return nc.alloc_sbuf_tensor("sb_" + name, shape, dt)

    def bufs(name, n):
        return [S.buf(f"{name}{i}") for i in range(n)]

    con = sb("con", [128, NCON], F32); bcon = S.buf("con")
    par = sb("par", [128, NPAR], F32); bpar = S.buf("par")
    identb = sb("identb", [128, 128], BF16)
    onesD = sb("onesD", [128, 128], BF16)
    onesV = sb("onesV", [128, 128], BF16)
    ones1 = sb("ones1", [128, 128], BF16)
    bandb = sb("bandb", [128, 3, 128], BF16)
    bcst = S.buf("cst")
    sm = sb("sm", [128, 64], F32); bsm = S.buf("sm")
    hT = sb("hT", [128, KC, HWID], F32); bh = bufs("h", KC)
    nT = sb("nT", [128, KC, T], BF16); bn = bufs("n", KC)
    B8 = sb("B8", [128, KC, T], BF16); bB = bufs("B", KC)
    X1 = sb("X1", [128, JF, T], BF16); bX = bufs("X", JF)
    NPF = 6
    PF = sb("PF", [128, NPF, T], F32); bPF = bufs("pf", NPF)
    HG = sb("HG", [128, 3, 4, T], F32); bHG = [bufs(f"hg{i}_", 4) for i in range(3)]
    KT = sb("KT", [128, 2, SEQ], BF16); bKT = bufs("kt", NT)
    VW = 66
    Vst = sb("Vst", [128, NBLK, 2, VW], BF16); bV = bufs("v", NT)
    KmT = sb("KmT", [128, 8, MEM], BF16); bKm = S.buf("KmT")
    Vm = sb("Vm", [128, 2, D], BF16); bVm = S.buf("Vm")
    ktok = sb("ktok", [128, 4, 512], BF16); bktok = bufs("ktok", 4)
    vtok = sb("vtok", [128, 4, 512], BF16); bvtok = bufs("vtok", 4)
    ATm = sb("ATm", [128, 2, 256], BF16); bATm = bufs("atm", 2)
    U = sb("U", [128, 4, 128], F32); bU = bufs("U", 4)
    S16 = sb("S16", [128, 2, 4, 128], BF16); bS16 = [bufs(f"s16_{i}_", 4) for i in range(2)]
    la = sb("la", [128, 4, NCH], F32); bla = S.buf("la")
    lbt = sb("lbt", [128, 4, NCH], F32); blbt = S.buf("lbt")
    refc = sb("refc", [128, 4, NCH], F32); brefc = S.buf("refc")
    lg = sb("lg", [128, 4, NCH], F32); blg = S.buf("lg")
    gam = sb("gam", [128, 4, NCH], F32); bgam = S.buf("gam")
    carry = sb("carry", [128, 2, 4], F32); bcarry = bufs("carry", 2)
    NPT = 3
    PT = sb("PT", [128, NPT, 3, 128], BF16); bPT = bufs("PT", NPT)
    oatok = sb("oatok", [128, 2, 512], BF16); boatok = bufs("oatok", 2)
    dsm = sb("dsm", [128, 2, 8], F32); bdsm = bufs("dsm", 2)
    xt = sb("xt", [128, 2, D], F32); bxt = bufs("xt", 2)
    ot = sb("ot", [128, 2, 512], F32); bot = bufs("ot", 2)
    wring = sb("wring", [128, NSLOT, SLOT], BF16); bring = bufs("ring", NSLOT)
    posi = sb("posi", [128, T], I32); bposi = S.buf("posi")
    rot = sb("rot", [128, 2, T], F32); brot = bufs("rot", 2)
    rstd_t = sb("rstd", [128, T], F32); brstd = S.buf("rstd")
    nint = sb("nint", [128, T], I32); bnint = S.buf("nint")
    gcar = sb("gcar", [128, JF, 2], F32); bgcar = bufs("gcar", JF)
    ucar = sb("ucar", [128, JF], F32); bucar = bufs("ucar", JF)
    NGS = 2
    gs = sb("gs", [128, NGS, T + 2], F32); bgs = bufs("gs", NGS)
    small = sb("small", [128, 4, JF], F32); bsmall = bufs("small", 4)
    actl = sb("actl", [128, JF], BF16); bactl = S.buf("actl")

    PS = [nc.alloc_psum_tensor(f"ps{i}", [128, 512], F32) for i in range(8)]
    bPS = bufs("ps", 8)
    ps_state = {"rot": 0, "held": set()}

    def psum_next():
        while True:
            i = ps_state["rot"] % 8
            ps_state["rot"] += 1
            if i not in ps_state["held"]:
                return PS[i], bPS[i]

    def psum_hold(n):
        res = []
        for _ in range(n):
            while True:
                i = ps_state["rot"] % 8
                ps_state["rot"] += 1
                if i not in ps_state["held"]:
                    break
            ps_state["held"].add(i)
            res.append(i)
        return res

    def psum_release(idx):
        for i in idx:
            ps_state["held"].discard(i)

    pf_state = {"rot": 0}

    def tmpf():
        i = pf_state["rot"] % NPF
        pf_state["rot"] += 1
        return PF[:, i, :], bPF[i]

    ew_state = {"rot": 0}

    def ew():
        return "dve"

    def cs(name, a=0, b=None):
        off, L = cidx[name]
        b = L if b is None else b
        return con[:, off + a: off + b]

    def pr(name, a=0, b=None):
        off, L = pidx[name]
        b = L if b is None else b
        return par[:, off + a: off + b]

    def tap(name, ap, bufl, shape):
        if name not in dbg_taps:
            return
        key = name
        n = 0
        while key in dbg_out:
            n += 1
            key = f"{name}_{n}"
        d = nc.dram_tensor("dbg_" + key, list(shape), ap.dtype, kind="ExternalOutput").ap()
        dbg_out[key] = S.dma("sp", d, ap, reads=list(bufl))

    plan = weight_plan(NT)
    ring = {"issued": 0, "consumed": 0}
    bgrp = bufs("wgrp", len(wbounds))

    def grp_of(off):
        for g, b in enumerate(wbounds):
            if off < b:
                return g
        raise AssertionError

    def ring_issue(m):
        name = plan[m]
        off, L = widx[name]
        slot = m % NSLOT
        S.dma("sp", wring[:, slot, 0:L], wscr_d[:, off:off + L], reads=[bgrp[grp_of(off)]], writes=[bring[slot]])

    def ring_next(name):
        n = ring["consumed"]
        assert plan[n] == name, (n, plan[n], name)
        while ring["issued"] < min(len(plan), n + NSLOT):
            ring_issue(ring["issued"])
            ring["issued"] += 1
        ring["consumed"] += 1
        slot = n % NSLOT
        return wring[:, slot, :], bring[slot]

    S.dma("sp", con[:], con_d, writes=[bcon])
    S.dma("sp", par[:], par_d, writes=[bpar])
    S.op("dve", lambda e: e.tensor_copy(identb[:], cs("ident")), reads=[bcon], writes=[bcst])
    S.op("dve", lambda e: e.tensor_scalar(onesD[:], cs("ones"), 1.0 / D, None, ALU.mult), reads=[bcon], writes=[bcst])
    S.op("dve", lambda e: e.tensor_scalar(onesV[:], cs("ones"), 1.0 / 128, None, ALU.mult), reads=[bcon], writes=[bcst])
    S.op("dve", lambda e: e.tensor_copy(ones1[:], cs("ones")), reads=[bcon], writes=[bcst])
    S.op("dve", lambda e: e.tensor_copy(bandb[:, 0, :], cs("band_prev")), reads=[bcon], writes=[bcst])
    S.op("dve", lambda e: e.tensor_copy(bandb[:, 1, :], cs("ones")), reads=[bcon], writes=[bcst])
    S.op("dve", lambda e: e.tensor_copy(bandb[:, 2, :], cs("band_next")), reads=[bcon], writes=[bcst])
    lbr = pr("lb_raw").rearrange("p (d s h) -> p d s h", d=2, s=2)
    S.op("dve", lambda e: e.tensor_tensor(sm[:, 32:40].rearrange("p (d h) -> p d h", d=2), lbr[:, :, 0, :], lbr[:, :, 1, :], ALU.subtract),
         reads=[bpar], writes=[bsm])
    S.op("act", lambda e: e.activation(sm[:, 0:8], sm[:, 32:40], AF.Sigmoid), reads=[bsm], writes=[bsm])
    S.op("dve", lambda e: e.tensor_scalar(sm[:, 8:16], sm[:, 0:8], -1.0, 1.0, ALU.mult, ALU.add), reads=[bsm], writes=[bsm])
    S.op("dve", lambda e: e.tensor_scalar(sm[:, 16:24], sm[:, 0:8], 1.0, -1.0, ALU.mult, ALU.add), reads=[bsm], writes=[bsm])
    S.op("act", lambda e: e.activation(sm[:, 24:32], pr("sink"), AF.Exp), reads=[bpar], writes=[bsm])
    S.op("pool", lambda e: e.memset(sm[:, 40:41], EPS), writes=[bsm])
    S.op("pool", lambda e: e.memset(hT[:], 0.0), writes=bh)
    S.op("pool", lambda e: e.memset(gcar[:], 0.0), writes=bgcar)
    S.op("pool", lambda e: e.memset(ucar[:], 0.0), writes=bucar)
    S.op("pool", lambda e: e.memset(carry[:], 0.0), writes=bcarry)
    S.op("pool", lambda e: e.memset(U[:], 0.0), writes=bU)
    S.op("pool", lambda e: e.memset(Vst[:], 1.0), writes=bV)

    ci = 0
    g_start = 0
    for g, g_end in enumerate(wbounds):
        a = g_start
        while a < g_end:
            b = min(a + CHUNK0, g_end)
            L = b - a
            sl = ci % 2
            S.dma("pool", wscr_d[:, a:b], wall_d[:, a:b], writes=[bgrp[g]])
            ci += 1
            a = b
        g_start = g_end

    xpre = {"tile": None}

    def x_dma(ti, blk):
        tok0 = ti * T
        sl = blk % 2
        S.dma("sp", xt[:, sl, :], x_d[tok0 + 128 * blk: tok0 + 128 * (blk + 1), :], writes=[bxt[sl]])

    def prefetch_x(ti):
        x_dma(ti, 0)
        x_dma(ti, 1)
        xpre["tile"] = ti

    def load_x_tile_gen(ti, nxt=None):
        have = xpre["tile"] == ti
        xpre["tile"] = None
        for blk in range(4):
            sl = blk % 2
            if not (have and blk < 2):
                x_dma(ti, blk)
            for half in range(2):
                ps, bps = psum_next()
                for kk in range(4):
                    k = half * 4 + kk
                    S.op("pe", lambda e, ps=ps, kk=kk, k=k, sl=sl: e.transpose(ps[:, kk * 128:(kk + 1) * 128], xt[:, sl, k * 128:(k + 1) * 128], cs("ident")),
                         reads=[bxt[sl], bcon], writes=[bps], inc=(kk == 3))
                S.op("act", lambda e, ps=ps, half=half, blk=blk: e.copy(
                    hT[:, half * 4:half * 4 + 4, H0 + 128 * blk: H0 + 128 * (blk + 1)],
                    ps[:, :].rearrange("p (a b) -> p a b", a=4)), reads=[bps], writes=bh[half * 4:half * 4 + 4])
            if blk >= 2 and nxt is not None:
                x_dma(nxt, blk - 2)
            if blk == 3 and nxt is not None:
                xpre["tile"] = nxt
            yield

    def load_x_tile(ti, nxt=None):
        for _ in load_x_tile_gen(ti, nxt):
            pass

    def rmsnorm(src, bsrc, gname, N, out, bout, nk=KC, ones=None):
        for _ in rmsnorm_gen(src, bsrc, gname, N, out, bout, nk, ones):
            pass

    def rmsnorm_gen(src, bsrc, gname, N, out, bout, nk=KC, ones=None):
        ones = onesD if ones is None else ones
        for k in range(nk):
            S.op("act", lambda e, k=k: e.activation(B8[:, k, 0:N], src(k), AF.Square), reads=[bsrc[k]], writes=[bB[k]])
        yield
        ps, bps = psum_next()
        for k in range(nk):
            S.op("pe", lambda e, k=k: e.matmul(ps[:, 0:N], ones[:], B8[:, k, 0:N], start=(k == 0), stop=(k == nk - 1)),
                 reads=[bB[k], bcst], writes=[bps], inc=(k == nk - 1))
        yield
        rs, brs = (rstd_t[:, :], brstd) if N == T else tmpf()
        S.op("act", lambda e: e.activation(rs[:, 0:N], ps[:, 0:N], AF.Ln, bias=sm[:, 40:41]), reads=[bps, bsm], writes=[brs])
        S.op("act", lambda e: e.activation(rs[:, 0:N], rs[:, 0:N], AF.Exp, scale=-0.5), reads=[brs], writes=[brs])
        yield
        for k in range(nk):
            S.op("dve", lambda e, k=k: e.scalar_tensor_tensor(out(k), src(k), pr(gname, k, k + 1), rs[:, 0:N], ALU.mult, ALU.mult),
                 reads=[bsrc[k], brs, bpar], writes=[bout[k]])
            if k % 4 == 3:
                yield

    def proj_F(wt, bwt, blk, rhs, brhs, N, nk=KC, wstride=None):
        ps, bps = psum_next()
        for k in range(nk):
            o = (blk * nk + k) * 128
            S.op("pe", lambda e, o=o, k=k: e.matmul(ps[:, 0:N], wt[:, o:o + 128], rhs(k), start=(k == 0), stop=(k == nk - 1)),
                 reads=[bwt, brhs[k]], writes=[bps], inc=(k == nk - 1))
        return ps, bps

    nTk = lambda k: nT[:, k, :]

    LF, KK, BQ = 0, 1, 2
    bsqs = bufs("sqs", NT)
    bvs = bufs("vs", NT)
    bnts = bufs("nts", NT)

    def hg_front(d):
        wt, bwt = ring_next("fzf" if d == 0 else "fzb")
        for hd in range(4):
            ps, bps = proj_F(wt, bwt, hd, nTk, bn, T)
            S.op("act", lambda e, hd=hd: e.activation(HG[:, KK, hd, :], ps[:, :], AF.Sigmoid), reads=[bps], writes=[bHG[KK][hd]])

    def hgrn_tile(ti, d, first, mid_hook=None, bg=None, late_hook=None, front_done=False):
        tok0 = ti * T
        if not front_done:
            hg_front(d)
        sgs = [(HG[:, KK, hd, :], bHG[KK][hd]) for hd in range(4)]
        if d == 1:
            wt, bwt = ring_next("qr")
            for hd in range(4):
                ps, bps = proj_F(wt, bwt, hd, nTk, bn, T)
                S.op("act", lambda e, hd=hd: e.activation(X1[:, 8 + hd, :], ps[:, :], AF.Silu), reads=[bps], writes=[bX[8 + hd]])
            S.dma("sp", sqs_d[ti], X1[:, 8:12, :], reads=bX[8:12], writes=[bsqs[ti]])
        else:
            S.dma("sp", X1[:, 0:4, :], sqs_d[ti], reads=[bsqs[ti]], writes=bX[0:4])
        if d == 1:
            wt, bwt = ring_next("ir")
            for blk in range(4):
                ps, bps = psum_next()
                for k in range(KC):
                    S.op("pe", lambda e, k=k, blk=blk: e.matmul(ps[:, :], nT[:, k, blk * 128:(blk + 1) * 128], wt[:, k * 512:(k + 1) * 512],
                                                                start=(k == 0), stop=(k == KC - 1)),
                         reads=[bwt, bn[k]], writes=[bps], inc=(k == KC - 1))
                S.op("act", lambda e, blk=blk: e.copy(vtok[:, blk, :], ps[:, :]), reads=[bps], writes=[bvtok[blk]])
            S.dma("sp", vs_d[ti], vtok[:], reads=bvtok, writes=[bvs[ti]])
        else:
            S.dma("sp", vtok[:], vs_d[ti], reads=[bvs[ti]], writes=bvtok)
        if mid_hook is not None:
            mid_hook()
        for hd in range(4):
            sg, bsg = sgs[hd]
            c = d * 4 + hd
            S.op("act", lambda e, c=c, hd=hd: e.activation(HG[:, LF, hd, :], sg, AF.Ln, scale=sm[:, 8 + c:9 + c], bias=sm[:, c:c + 1]),
                 reads=[bsg, bsm], writes=[bHG[LF][hd]])
            S.op("dve", lambda e, c=c, hd=hd: e.tensor_scalar(HG[:, KK, hd, :], sg, sm[:, 16 + c:17 + c], sm[:, 8 + c:9 + c], ALU.mult, ALU.add),
                 reads=[bsg, bsm], writes=[bHG[KK][hd]])
        tap("lf", HG[:, LF], bHG[LF], [128, 4, T])
        tap("kk", HG[:, KK], bHG[KK], [128, 4, T])
        flat = lambda i: HG[:, i].rearrange("p h t -> p (h t)")
        ch = lambda i: HG[:, i].rearrange("p h (c s) -> p h c s", s=C)
        for hd in range(4):
            S.op("dve", lambda e, hd=hd: e.tensor_tensor_scan(HG[:, BQ, hd, :], cs("m01"), HG[:, LF, hd, :], 0.0, ALU.mult, ALU.add),
                 reads=[bHG[LF][hd], bcon], writes=[bHG[BQ][hd]])
        tap("bq", HG[:, BQ], bHG[BQ], [128, 4, T])
        if d == 0:
            S.op("dve", lambda e: e.tensor_copy(la[:], ch(BQ)[:, :, :, MID]), reads=bHG[BQ], writes=[bla])
            S.op("dve", lambda e: e.tensor_tensor(lbt[:], ch(BQ)[:, :, :, C - 1], ch(BQ)[:, :, :, MID], ALU.subtract), reads=bHG[BQ], writes=[blbt])
            S.op("dve", lambda e: e.tensor_copy(refc[:], ch(BQ)[:, :, :, MID]), reads=bHG[BQ], writes=[brefc])
            S.op("dve", lambda e: e.tensor_tensor(ch(BQ), ch(BQ), refc[:].unsqueeze(3).to_broadcast([128, 4, NCH, C]), ALU.subtract),
                 reads=bHG[BQ] + [brefc], writes=bHG[BQ])
            S.op("act", lambda e: e.activation(flat(LF), flat(BQ), AF.Exp), reads=bHG[BQ], writes=bHG[LF])
            S.op("act", lambda e: e.activation(flat(BQ), flat(BQ), AF.Exp, scale=-1.0), reads=bHG[BQ], writes=bHG[BQ])
        else:
            S.op("dve", lambda e: e.tensor_tensor(flat(LF), flat(BQ), flat(LF), ALU.subtract), reads=bHG[BQ] + bHG[LF], writes=bHG[LF])
            S.op("dve", lambda e: e.tensor_tensor(la[:], ch(BQ)[:, :, :, C - 1], ch(LF)[:, :, :, MID], ALU.subtract), reads=bHG[BQ] + bHG[LF], writes=[bla])
            S.op("dve", lambda e: e.tensor_copy(lbt[:], ch(LF)[:, :, :, MID]), reads=bHG[LF], writes=[blbt])
            S.op("dve", lambda e: e.tensor_copy(refc[:], ch(LF)[:, :, :, MID]), reads=bHG[LF], writes=[brefc])
            S.op("dve", lambda e: e.tensor_tensor(ch(LF), ch(LF), refc[:].unsqueeze(3).to_broadcast([128, 4, NCH, C]), ALU.subtract),
                 reads=bHG[LF] + [brefc], writes=bHG[LF])
            S.op("act", lambda e: e.activation(flat(BQ), flat(LF), AF.Exp), reads=bHG[LF], writes=bHG[BQ])
            S.op("act", lambda e: e.activation(flat(LF), flat(LF), AF.Exp, scale=-1.0), reads=bHG[LF], writes=bHG[LF])
        if d == 0:
            S.op("dve", lambda e: e.tensor_tensor(lg[:, :, 1:NCH], la[:, :, 1:NCH], lbt[:, :, 0:NCH - 1], ALU.add), reads=[bla, blbt], writes=[blg])
            S.op("dve", lambda e: e.tensor_tensor(lg[:, :, 0:1], la[:, :, 0:1], carry[:, 0, :].unsqueeze(2), ALU.add), reads=[bla, bcarry[0]], writes=[blg])
            S.op("dve", lambda e: e.tensor_copy(carry[:, 0, :].unsqueeze(2), lbt[:, :, NCH - 1:NCH]), reads=[blbt], writes=[bcarry[0]])
        else:
            S.op("dve", lambda e: e.tensor_tensor(lg[:, :, 0:NCH - 1], la[:, :, 0:NCH - 1], lbt[:, :, 1:NCH], ALU.add), reads=[bla, blbt], writes=[blg])
            S.op("dve", lambda e: e.tensor_tensor(lg[:, :, NCH - 1:NCH], la[:, :, NCH - 1:NCH], carry[:, 1, :].unsqueeze(2), ALU.add), reads=[bla, bcarry[1]], writes=[blg])
            S.op("dve", lambda e: e.tensor_copy(carry[:, 1, :].unsqueeze(2), lbt[:, :, 0:1]), reads=[blbt], writes=[bcarry[1]])
        S.op("act", lambda e: e.activation(gam[:], lg[:], AF.Exp), reads=[blg], writes=[bgam])
        for hd in range(4):
            S.op(ew(), lambda e, hd=hd: e.tensor_tensor(X1[:, 4 + hd, :], HG[:, KK, hd, :], HG[:, BQ, hd, :], ALU.mult),
                 reads=[bHG[KK][hd], bHG[BQ][hd]], writes=[bX[4 + hd]])
        for hd in range(4):
            src = 8 + hd if d == 1 else hd
            S.op("dve", lambda e, hd=hd, src=src: e.scalar_tensor_tensor(X1[:, hd, :], X1[:, src, :], float(128 ** -0.5), HG[:, LF, hd, :], ALU.mult, ALU.mult),
                 reads=[bX[src], bHG[LF][hd]], writes=[bX[hd]])
        tap("qt", X1[:, 0:4, :], bX[0:4], [128, 4, T])
        tap("kt", X1[:, 4:8, :], bX[4:8], [128, 4, T])
        tap("gam", gam[:], [bgam], [128, 4, NCH])
        if late_hook is not None:
            late_hook()
        for blk in range(4):
            ps, bps = psum_next()
            psb = ps[:, :].bitcast(BF16)
            for hd in range(4):
                S.op("pe", lambda e, hd=hd, blk=blk: e.transpose(psb[:, hd * 128:(hd + 1) * 128], X1[:, 4 + hd, blk * 128:(blk + 1) * 128], identb[:]),
                     reads=[bX[4 + hd], bcst], writes=[bps], inc=(hd == 3))
            S.op("dve", lambda e, blk=blk: e.tensor_copy(ktok[:, blk, :], psb[:, 0:512]), reads=[bps], writes=[bktok[blk]])
        held = psum_hold(4)
        order = range(NCH) if d == 0 else range(NCH - 1, -1, -1)
        mask = cs("mask_f") if d == 0 else cs("mask_b")
        order = list(order)

        def emit_AT(ci_):
            c = order[ci_]
            p0 = 64 * (c % 2)
            cols = slice(c * C, (c + 1) * C)
            ab = ci_ % 2
            psA, bpsA = psum_next()
            for hd in range(4):
                S.op("pe", lambda e, hd=hd: e.matmul(psA[p0:p0 + 64, hd * 64:(hd + 1) * 64], X1[:, 4 + hd, cols], X1[:, hd, cols],
                                                     start=True, stop=True, tile_position=(0, p0)),
                     reads=[bX[4 + hd], bX[hd]], writes=[bpsA], inc=(hd == 3))
            S.op("dve", lambda e: e.tensor_tensor(ATm[p0:p0 + 64, ab, :].rearrange("p (h t) -> p h t", h=4),
                                                  psA[p0:p0 + 64, 0:256].rearrange("p (h t) -> p h t", h=4),
                                                  mask[p0:p0 + 64, :].unsqueeze(1).to_broadcast([64, 4, C]), ALU.mult),
                 reads=[bpsA, bcon], writes=[bATm[ab]])

        emit_AT(0)
        for ci_, c in enumerate(order):
            blk, hf = c // 2, c % 2
            p0 = 64 * hf
            cols = slice(c * C, (c + 1) * C)
            very_first = first and ci_ == 0
            sb_ = ci_ % 2
            ab = ci_ % 2
            if not very_first:
                for hd in range(4):
                    S.op("act", lambda e, hd=hd: e.activation(S16[:, sb_, hd, :], U[:, hd, :], AF.Copy, scale=gam[:, hd, c:c + 1]),
                         reads=[bU[hd], bgam], writes=[bS16[sb_][hd]])
            psU, bpsU = psum_next()
            for hd in range(4):
                S.op("pe", lambda e, hd=hd: e.matmul(psU[:, hd * 128:(hd + 1) * 128], ktok[p0:p0 + 64, blk, hd * 128:(hd + 1) * 128],
                                                     vtok[p0:p0 + 64, blk, hd * 128:(hd + 1) * 128], start=True, stop=True),
                     reads=[bktok[blk], bvtok[blk]], writes=[bpsU], inc=(hd == 3))
            if ci_ + 1 < NCH:
                emit_AT(ci_ + 1)
            for hd in range(4):
                pso, bpso = PS[held[hd]], bPS[held[hd]]
                S.op("pe", lambda e, hd=hd, pso=pso: e.matmul(pso[:, cols], vtok[p0:p0 + 64, blk, hd * 128:(hd + 1) * 128],
                                                              ATm[p0:p0 + 64, ab, hd * 64:(hd + 1) * 64], start=True, stop=very_first),
                     reads=[bvtok[blk], bATm[ab]], writes=[bpso], inc=very_first)
                if not very_first:
                    S.op("pe", lambda e, hd=hd, pso=pso: e.matmul(pso[:, cols], S16[:, sb_, hd, :], X1[:, hd, cols], start=False, stop=True),
                         reads=[bS16[sb_][hd], bX[hd]], writes=[bpso])
            for hd in range(4):
                if very_first:
                    S.op("dve", lambda e, hd=hd: e.tensor_copy(U[:, hd, :], psU[:, hd * 128:(hd + 1) * 128]), reads=[bpsU], writes=[bU[hd]])
                else:
                    S.op("dve", lambda e, hd=hd: e.scalar_tensor_tensor(U[:, hd, :], U[:, hd, :], gam[:, hd, c:c + 1], psU[:, hd * 128:(hd + 1) * 128],
                                                                        ALU.mult, ALU.add),
                         reads=[bU[hd], bgam, bpsU], writes=[bU[hd]])
            if bg is not None:
                next(bg, None)
                next(bg, None)
        if bg is not None:
            for _ in bg:
                pass
        return held

    bob = bufs("ob", NT)
    def p1_prep_gen(ti):
        yield from load_x_tile_gen(ti, nxt=(ti - 1 if ti > 0 else None))
        yield from rmsnorm_gen(lambda k: hT[:, k, H0:H0 + T], bh, "g_mix", T, nTk, bn)
        S.dma("sp", nts_d[ti], nT[:], reads=bn, writes=[bnts[ti]])
        yield

    def p1_prep(ti):
        for _ in p1_prep_gen(ti):
            pass

    for idx, ti in enumerate(range(NT - 1, -1, -1)):
        tok0 = ti * T
        if idx == 0:
            p1_prep(ti)
        S.dma("sp", posi[:], pos_d[:, tok0:tok0 + T].partition_broadcast(128), writes=[bposi])
        posf, bposf = tmpf()
        S.op("dve", lambda e: e.tensor_copy(posf, posi[:]), reads=[bposi], writes=[bposf])
        ct, bct = rot[:, 0, :], brot[0]
        sn, bsn = rot[:, 1, :], brot[1]
        for tb, btb, ph in ((ct, bct, "phase_c"), (sn, bsn, "phase_s")):
            S.op("dve", lambda e, tb=tb, ph=ph: e.tensor_scalar(tb, posf, cs("invf"), cs(ph), ALU.mult, ALU.add), reads=[bposf, bcon], writes=[btb])
            S.op("dve", lambda e, tb=tb: e.tensor_scalar(nint[:], tb, float(1.0 / (2 * np.pi)), None, ALU.mult), reads=[btb], writes=[bnint])
            nf_, bnf_ = tmpf()
            S.op("dve", lambda e: e.tensor_copy(nf_, nint[:]), reads=[bnint], writes=[bnf_])
            S.op("dve", lambda e, tb=tb: e.scalar_tensor_tensor(tb, nf_, float(-2 * np.pi), tb, ALU.mult, ALU.add), reads=[bnf_, btb], writes=[btb])
            S.op("dve", lambda e, tb=tb: e.tensor_scalar(tb, tb, -3.1415925, 3.1415925, ALU.max, ALU.min), reads=[btb], writes=[btb])
            S.op("act", lambda e, tb=tb: e.activation(tb, tb, AF.Sin), reads=[btb], writes=[btb])

        def kv_hook(ti=ti, tok0=tok0):
            wt, bwt = ring_next("ka4")
            for kvh in range(2):
                ps1, bps1 = proj_F(wt, bwt, kvh, nTk, bn, T)
                ps2, bps2 = proj_F(wt, bwt, 2 + kvh, nTk, bn, T)
                t1, bt1 = tmpf()
                t2, bt2 = tmpf()
                S.op("dve", lambda e: e.tensor_tensor(t1, ps1[:, :], ct, ALU.mult), reads=[bps1, bct], writes=[bt1])
                S.op("dve", lambda e: e.tensor_tensor(t2, ps2[:, :], sn, ALU.mult), reads=[bps2, bsn], writes=[bt2])
                S.op("dve", lambda e, kvh=kvh: e.tensor_tensor(KT[:, kvh, tok0:tok0 + T], t1, t2, ALU.add), reads=[bt1, bt2], writes=[bKT[ti]])
            wt, bwt = ring_next("va")
            for blk in range(4):
                ps, bps = psum_next()
                for k in range(KC):
                    S.op("pe", lambda e, k=k, blk=blk: e.matmul(ps[:, 0:128], nT[:, k, blk * 128:(blk + 1) * 128], wt[:, k * 128:(k + 1) * 128],
                                                                start=(k == 0), stop=(k == KC - 1)),
                         reads=[bwt, bn[k]], writes=[bps], inc=(k == KC - 1))
                S.op("act", lambda e, blk=blk: e.copy(Vst[:, ti * 4 + blk, :, 0:64], ps[:, 0:128].rearrange("p (a b) -> p a b", a=2)),
                     reads=[bps], writes=[bV[ti]])
        held = hgrn_tile(ti, 1, idx == 0, mid_hook=kv_hook, late_hook=((lambda ti=ti: p1_prep(ti - 1)) if ti > 0 else None))
        for hd in range(4):
            S.op("act", lambda e, hd=hd: e.copy(HG[:, KK, hd, :], PS[held[hd]][:, :]), reads=[bPS[held[hd]]], writes=[bHG[KK][hd]])
        psum_release(held)
        S.dma("sp", ob_d[:, :, tok0:tok0 + T], HG[:, KK], reads=bHG[KK], writes=[bob[ti]])
        if idx == 0:
            tap("ob", HG[:, KK], bHG[KK], [128, 4, T])
    if stop_after == "phase1":
        tap("KT", KT[:], bKT, [128, 2, SEQ])
        tap("Vst", Vst[:], bV, [128, NBLK, 2, VW])
        return finish(nc, S, dbg_out)

    for half in range(2):
        S.dma("sp", xt[:, half, :], mem_d[128 * half:128 * (half + 1), :], writes=[bxt[half]])
        for h2 in range(2):
            ps, bps = psum_next()
            for kk in range(4):
                k = h2 * 4 + kk
                S.op("pe", lambda e, kk=kk, k=k: e.transpose(ps[:, kk * 128:(kk + 1) * 128], xt[:, half, k * 128:(k + 1) * 128], cs("ident")),
                     reads=[bxt[half], bcon], writes=[bps], inc=(kk == 3))
            S.op("act", lambda e: e.copy(hT[:, h2 * 4:h2 * 4 + 4, H0 + 128 * half: H0 + 128 * (half + 1)],
                                         ps[:, :].rearrange("p (a b) -> p a b", a=4)), reads=[bps], writes=bh[h2 * 4:h2 * 4 + 4])
    rmsnorm(lambda k: hT[:, k, H0:H0 + MEM], bh, "g_memkv", MEM, lambda k: nT[:, k, 0:MEM], bn)
    for w2 in range(2):
        wt, bwt = ring_next(f"wmk{w2}")
        for jb in range(4):
            j = w2 * 4 + jb
            ps, bps = proj_F(wt, bwt, jb, lambda k: nT[:, k, 0:MEM], bn, MEM)
            S.op("act", lambda e, j=j: e.copy(KmT[:, j, :], ps[:, 0:MEM]), reads=[bps], writes=[bKm])
    for w2 in range(2):
        wt, bwt = ring_next(f"wmv{w2}")
        for mc in range(2):
            ps, bps = psum_next()
            for k in range(KC):
                S.op("pe", lambda e, k=k: e.matmul(ps[:, :], nT[:, k, mc * 128:(mc + 1) * 128], wt[:, k * 512:(k + 1) * 512],
                                                   start=(k == 0), stop=(k == KC - 1)),
                     reads=[bwt, bn[k]], writes=[bps], inc=(k == KC - 1))
            S.op("act", lambda e: e.copy(Vm[:, mc, w2 * 512:(w2 + 1) * 512], ps[:, :]), reads=[bps], writes=[bVm])
    prefetch_x(0)
    tap("KmT", KmT[:], [bKm], [128, 8, MEM])
    tap("Vm", Vm[:], [bVm], [128, 2, D])

    OG, QR, OA = 8, 12, 16
    final_toks = []
    hwin = lambda k, N=T: hT[:, k, H0 - 1:H0 - 1 + N]

    def ffn_finish(ti, N, c0, row0, skip_first, mid=None):
        g_ = rmsnorm_gen(lambda k: hT[:, k, c0:c0 + N], bh, "g_fin", N, lambda k: hT[:, k, c0:c0 + N], bh)
        next(g_, None)
        next(g_, None)
        if mid is not None:
            mid()
        for _ in g_:
            pass
        nb = (N + 127) // 128
        for blk in range(nb):
            w = min(128, N - 128 * blk)
            sl = blk % 2
            r0 = row0 + 128 * blk
            for half in range(2):
                ps, bps = psum_next()
                for kk in range(4):
                    k = half * 4 + kk
                    S.op("pe", lambda e, kk=kk, k=k: e.transpose(ps[0:w, kk * 128:(kk + 1) * 128], hT[:, k, c0 + 128 * blk: c0 + 128 * blk + w], cs("ident")),
                         reads=[bh[k], bcon], writes=[bps], inc=(kk == 3))
                S.op("act" if half == 0 else "dve",
                     (lambda e: e.copy(ot[0:w, half, :], ps[0:w, :])) if half == 0 else
                     (lambda e: e.tensor_copy(ot[0:w, half, :], ps[0:w, :])),
                     reads=[bps], writes=[bot[half]])
                cols = slice(half * 512, (half + 1) * 512)
                if skip_first and blk == 0:
                    tok = S.dma("sp", out_d[r0 + 1:r0 + w, cols], ot[1:w, half, :], reads=[bot[half]])
                else:
                    tok = S.dma("sp", out_d[r0:r0 + w, cols], ot[0:w, half, :], reads=[bot[half]])
                final_toks.append(tok)

    for ti in range(NT):
        tok0 = ti * T
        if ti == 0:
            S.dma("sp", nT[:], nts_d[0], reads=[bnts[0]], writes=bn)
        load_x_tile(ti, nxt=(ti + 1 if ti + 1 < NT else None))
        S.dma("sp", posi[:], pos_d[:, tok0:tok0 + T].partition_broadcast(128), writes=[bposi])
        posf, bposf = tmpf()
        S.op("dve", lambda e: e.tensor_copy(posf, posi[:]), reads=[bposi], writes=[bposf])
        ct, bct = rot[:, 0, :], brot[0]
        sn, bsn = rot[:, 1, :], brot[1]
        for tb, btb, ph in ((ct, bct, "phase_c"), (sn, bsn, "phase_s")):
            S.op("dve", lambda e, tb=tb, ph=ph: e.tensor_scalar(tb, posf, cs("invf"), cs(ph), ALU.mult, ALU.add), reads=[bposf, bcon], writes=[btb])
            S.op("dve", lambda e, tb=tb: e.tensor_scalar(nint[:], tb, float(1.0 / (2 * np.pi)), None, ALU.mult), reads=[btb], writes=[bnint])
            nf_, bnf_ = tmpf()
            S.op("dve", lambda e: e.tensor_copy(nf_, nint[:]), reads=[bnint], writes=[bnf_])
            S.op("dve", lambda e, tb=tb: e.scalar_tensor_tensor(tb, nf_, float(-2 * np.pi), tb, ALU.mult, ALU.add), reads=[bnf_, btb], writes=[btb])
            S.op("dve", lambda e, tb=tb: e.tensor_scalar(tb, tb, -3.1415925, 3.1415925, ALU.max, ALU.min), reads=[btb], writes=[btb])
            S.op("act", lambda e, tb=tb: e.activation(tb, tb, AF.Sin), reads=[btb], writes=[btb])

        def attention_qb(qb):
            gb = ti * 4 + qb
            kbs = [kb for kb in (gb - 1, gb, gb + 1) if 0 <= kb < NBLK]
            s0 = kbs[0] - (gb - 1)
            ns = len(kbs)
            pso2 = [psum_hold(1)[0], psum_hold(1)[0]]
            def att_scores(h):
                kvh, j, p0 = h // 4, h // 2, 64 * (h % 2)
                ps, bps = psum_next()
                for kb in kbs:
                    s_ = kb - (gb - 1)
                    S.op("pe", lambda e, s_=s_, kb=kb: e.matmul(ps[:, s_ * 128:(s_ + 1) * 128], KT[p0:p0 + 64, kvh, kb * 128:(kb + 1) * 128],
                                                                X1[p0:p0 + 64, QR + j, qb * 128:(qb + 1) * 128], start=True, stop=True),
                         reads=[bKT[kb // 4], bX[QR + j]], writes=[bps], inc=(kb == kbs[-1]))
                pi_ = (qb * 8 + h) % NPT
                S.op("act", lambda e: e.activation(PT[:, pi_, s0:s0 + ns, :], ps[:, s0 * 128:(s0 + ns) * 128].rearrange("p (a b) -> p a b", a=ns),
                                                   AF.Exp, scale=0.125), reads=[bps], writes=[bPT[pi_]])
                S.op("dve", lambda e: e.tensor_tensor(PT[:, pi_, s0:s0 + ns, :], PT[:, pi_, s0:s0 + ns, :], bandb[:, s0:s0 + ns, :], ALU.mult),
                     reads=[bPT[pi_], bcst], writes=[bPT[pi_]])

            def att_pv(h):
                kvh = h // 4
                pi_ = (qb * 8 + h) % NPT
                po, bpo = PS[pso2[h // 4]], bPS[pso2[h // 4]]
                hh = h % 4
                for kb in kbs:
                    s_ = kb - (gb - 1)
                    S.op("pe", lambda e, s_=s_, kb=kb: e.matmul(po[:, hh * 65:hh * 65 + 65], PT[:, pi_, s_, :], Vst[:, kb, kvh, 0:65],
                                                                start=(kb == kbs[0]), stop=(kb == kbs[-1])),
                         reads=[bPT[pi_], bV[kb // 4]], writes=[bpo], inc=(kb == kbs[-1]))

            for h in range(8):
                att_scores(h)
                if h > 1:
                    att_pv(h - 2)
            att_pv(6)
            att_pv(7)
            osl = qb % 2
            for hb in range(2):
                po, bpo = PS[pso2[hb]], bPS[pso2[hb]]
                pv = po[:, 0:260].rearrange("p (h c) -> p h c", c=65)
                S.op("dve", lambda e: e.tensor_tensor(dsm[:, 0, hb * 4:hb * 4 + 4].unsqueeze(2), pv[:, :, 64:65], sm[:, 24 + hb * 4:28 + hb * 4].unsqueeze(2), ALU.add),
                     reads=[bpo, bsm], writes=[bdsm[0]])
                S.op("dve", lambda e: e.reciprocal(dsm[:, 1, hb * 4:hb * 4 + 4], dsm[:, 0, hb * 4:hb * 4 + 4]), reads=[bdsm[0]], writes=[bdsm[1]])
                S.op("dve", lambda e: e.tensor_tensor(oatok[:, osl, hb * 256:(hb + 1) * 256].rearrange("p (h c) -> p h c", c=64), pv[:, :, 0:64],
                                                      dsm[:, 1, hb * 4:hb * 4 + 4].unsqueeze(2).to_broadcast([128, 4, 64]), ALU.mult),
                     reads=[bpo, bdsm[1]], writes=[boatok[osl]])
            psum_release(pso2)
            ps, bps = psum_next()
            psb = ps[:, :].bitcast(BF16)
            for kc in range(4):
                S.op("pe", lambda e, kc=kc: e.transpose(psb[:, kc * 128:(kc + 1) * 128], oatok[:, osl, kc * 128:(kc + 1) * 128], identb[:]),
                     reads=[boatok[osl], bcst], writes=[bps], inc=(kc == 3))
            S.op("act", lambda e: e.copy(X1[:, OA:OA + 4, qb * 128:(qb + 1) * 128], psb[:, 0:512].rearrange("p (a b) -> p a b", a=4)),
                 reads=[bps], writes=bX[OA:OA + 4])

        def mix_hook():
            wt, bwt = ring_next("gr")
            for hd in range(4):
                psg, bpsg = proj_F(wt, bwt, hd, nTk, bn, T)
                S.op("act", lambda e, hd=hd: e.activation(B8[:, 4 + hd, :], psg[:, :], AF.Silu), reads=[bpsg], writes=[bB[4 + hd]])
            for jj in range(2):
                wt, bwt = ring_next(f"qq{jj}")
                for sub in range(2):
                    j = 2 * jj + sub
                    ps1, bps1 = proj_F(wt, bwt, 2 * sub, nTk, bn, T)
                    ps2, bps2 = proj_F(wt, bwt, 2 * sub + 1, nTk, bn, T)
                    t1, bt1 = tmpf()
                    t2, bt2 = tmpf()
                    S.op("dve", lambda e: e.tensor_tensor(t1, ps1[:, :], ct, ALU.mult), reads=[bps1, bct], writes=[bt1])
                    S.op("dve", lambda e: e.tensor_tensor(t2, ps2[:, :], sn, ALU.mult), reads=[bps2, bsn], writes=[bt2])
                    S.op("dve", lambda e, j=j: e.tensor_tensor(X1[:, QR + j, :], t1, t2, ALU.add), reads=[bt1, bt2], writes=[bX[QR + j]])

        def mix_hook2():
            mix_hook()
            attention_qb(0)
            attention_qb(1)
            attention_qb(2)
            attention_qb(3)

        held = hgrn_tile(ti, 0, ti == 0, mid_hook=mix_hook2, front_done=(ti > 0))
        S.dma("sp", HG[:, LF], ob_d[:, :, tok0:tok0 + T], reads=[bob[ti]], writes=bHG[LF])
        for hd in range(4):
            S.op("dve", lambda e, hd=hd: e.tensor_tensor(HG[:, KK, hd, :], PS[held[hd]][:, :], HG[:, LF, hd, :], ALU.add),
                 reads=[bPS[held[hd]], bHG[LF][hd]], writes=[bHG[KK][hd]])
        psum_release(held)
        if ti == 0:
            tap("osum", HG[:, KK], bHG[KK], [128, 4, T])
        def o_norm(hd):
            S.op("act", lambda e: e.activation(B8[:, hd, :], HG[:, KK, hd, :], AF.Square), reads=[bHG[KK][hd]], writes=[bB[hd]])
            ps, bps = psum_next()
            S.op("pe", lambda e: e.matmul(ps[:, :], onesV[:], B8[:, hd, :], start=True, stop=True), reads=[bB[hd], bcst], writes=[bps])
            rs, brs = tmpf()
            S.op("act", lambda e: e.activation(rs, ps[:, :], AF.Ln, bias=sm[:, 40:41]), reads=[bps, bsm], writes=[brs])
            S.op("act", lambda e: e.activation(rs, rs, AF.Exp, scale=-0.5), reads=[brs], writes=[brs])
            S.op("dve", lambda e: e.tensor_tensor(rs, rs, HG[:, KK, hd, :], ALU.mult), reads=[brs, bHG[KK][hd]], writes=[brs])
            S.op("dve", lambda e: e.scalar_tensor_tensor(X1[:, OG + hd, :], rs, pr("hgn", hd, hd + 1), B8[:, 4 + hd, :], ALU.mult, ALU.mult),
                 reads=[brs, bB[4 + hd], bpar], writes=[bX[OG + hd]])
        if ti == 0:
            tap("qrT", X1[:, QR:QR + 4, :], bX[QR:QR + 4], [128, 4, T])
        o_norm(0)
        o_norm(1)
        o_norm(2)
        o_norm(3)
        if ti == 0:
            tap("og", X1[:, OG:OG + 4, :], bX[OG:OG + 4], [128, 4, T])
        if ti == 0:
            tap("oaT", X1[:, OA:OA + 4, :], bX[OA:OA + 4], [128, 4, T])
        for j in range(8):
            wt, bwt = ring_next(f"mg{j}")
            psr, bpsr = proj_F(wt, bwt, 0, nTk, bn, T)
            psa, bpsa = proj_F(wt[:, 1024:], bwt, 0, nTk, bn, T)
            sr, bsr = tmpf()
            sa, bsa = tmpf()
            S.op("act", lambda e: e.activation(sr, psr[:, :], AF.Sigmoid), reads=[bpsr], writes=[bsr])
            S.op("act", lambda e: e.activation(sa, psa[:, :], AF.Sigmoid), reads=[bpsa], writes=[bsa])
            pyr, bpyr = proj_F(wt[:, 2048:], bwt, 0, lambda k: X1[:, OG + k, :], bX[OG:OG + 4], T, nk=4)
            pya, bpya = proj_F(wt[:, 2560:], bwt, 0, lambda k: X1[:, OA + k, :], bX[OA:OA + 4], T, nk=4)
            S.op("dve", lambda e: e.tensor_tensor(sr, pyr[:, :], sr, ALU.mult), reads=[bpyr, bsr], writes=[bsr])
            S.op("dve", lambda e: e.tensor_tensor(sa, pya[:, :], sa, ALU.mult), reads=[bpya, bsa], writes=[bsa])
            S.op("dve", lambda e, j=j: e.tensor_tensor(B8[:, j, :], sr, sa, ALU.add), reads=[bsr, bsa], writes=[bB[j]])
        if ti == 0:
            tap("merged", B8[:], bB, [128, 8, T])
        for w2 in range(2):
            wt, bwt = ring_next(f"wout{w2}")
            for jb in range(4):
                j = w2 * 4 + jb
                ps, bps = proj_F(wt, bwt, jb, lambda k: B8[:, k, :], bB, T)
                S.op("dve", lambda e, j=j: e.tensor_tensor(hT[:, j, H0:H0 + T], hT[:, j, H0:H0 + T], ps[:, :], ALU.add), reads=[bps, bh[j]], writes=[bh[j]])
        if ti == 0:
            tap("h1", hT[:], bh, [128, KC, HWID])
        rmsnorm(lambda k: hT[:, k, H0:H0 + T], bh, "g_mem", T, nTk, bn)
        QM, OM = 0, 8
        for w2 in range(2):
            wt, bwt = ring_next(f"wmq{w2}")
            for jb in range(4):
                j = w2 * 4 + jb
                ps, bps = proj_F(wt, bwt, jb, nTk, bn, T)
                S.op("act", lambda e, j=j: e.copy(X1[:, QM + j, :], ps[:, :]), reads=[bps], writes=[bX[QM + j]])
        def mem_scores(h):
            pb = 16 + 2 * (h % 2)
            for mc in range(2):
                ps, bps = psum_next()
                for dc in range(2):
                    S.op("pe", lambda e, dc=dc: e.matmul(ps[:, :], KmT[:, 2 * h + dc, mc * 128:(mc + 1) * 128], X1[:, QM + 2 * h + dc, :],
                                                         start=(dc == 0), stop=(dc == 1)),
                         reads=[bKm, bX[QM + 2 * h + dc]], writes=[bps], inc=(dc == 1))
                S.op("act", lambda e: e.activation(X1[:, pb + mc, :], ps[:, :], AF.Exp, scale=1.0 / 16), reads=[bps], writes=[bX[pb + mc]])

        def mem_pv(h):
            pb = 16 + 2 * (h % 2)
            psd, bpsd = psum_next()
            for mc in range(2):
                S.op("pe", lambda e: e.matmul(psd[:, :], ones1[:], X1[:, pb + mc, :], start=(mc == 0), stop=(mc == 1)),
                     reads=[bX[pb + mc], bcst], writes=[bpsd], inc=(mc == 1))
            rd, brd = tmpf()
            S.op("dve", lambda e: e.reciprocal(rd, psd[:, :]), reads=[bpsd], writes=[brd])
            for dc in range(2):
                ps, bps = psum_next()
                for mc in range(2):
                    S.op("pe", lambda e: e.matmul(ps[:, :], Vm[:, mc, h * 256 + dc * 128: h * 256 + (dc + 1) * 128], X1[:, pb + mc, :],
                                                  start=(mc == 0), stop=(mc == 1)),
                         reads=[bVm, bX[pb + mc]], writes=[bps], inc=(mc == 1))
                S.op("dve", lambda e: e.tensor_tensor(X1[:, OM + 2 * h + dc, :], ps[:, :], rd, ALU.mult), reads=[bps, brd], writes=[bX[OM + 2 * h + dc]])

        for h in range(4):
            mem_scores(h)
            if h > 0:
                mem_pv(h - 1)
        mem_pv(3)
        for w2 in range(2):
            wt, bwt = ring_next(f"wmo{w2}")
            for jb in range(4):
                j = w2 * 4 + jb
                ps, bps = proj_F(wt, bwt, jb, lambda k: X1[:, OM + k, :], bX[OM:OM + 8], T)
                S.op("dve", lambda e, j=j: e.tensor_tensor(hT[:, j, H0:H0 + T], hT[:, j, H0:H0 + T], ps[:, :], ALU.add), reads=[bps, bh[j]], writes=[bh[j]])
        if ti == 0:
            tap("h2", hT[:], bh, [128, KC, HWID])
        rmsnorm(lambda k: hT[:, k, H0:H0 + T], bh, "g_ffn", T, nTk, bn)
        cw = lambda j_, jf: pr("conv_w", j_ * JF + jf, j_ * JF + jf + 1)
        pend = []

        def ffn_B(jf, c_, bc_, psu, bpsu):
            S.op("act", lambda e: e.activation(c_, c_, AF.Silu), reads=[bc_], writes=[bc_])
            S.op("dve", lambda e: e.tensor_tensor(X1[:, jf, 1:T], c_[:, 1:T], psu[:, 0:T - 1], ALU.mult), reads=[bc_, bpsu], writes=[bX[jf]])
            S.op("dve", lambda e: e.tensor_tensor(X1[:, jf, 0:1], c_[:, 0:1], ucar[:, jf:jf + 1], ALU.mult), reads=[bc_, bucar[jf]], writes=[bX[jf]])
            S.op("dve", lambda e: e.tensor_copy(ucar[:, jf:jf + 1], psu[:, T - 1:T]), reads=[bpsu], writes=[bucar[jf]])

        for jj in range(JF // 2):
            wt, bwt = ring_next(f"up{jj}")
            for sub in range(2):
                jf = 2 * jj + sub
                psu, bpsu = proj_F(wt, bwt, 2 * sub, nTk, bn, T)
                psg, bpsg = proj_F(wt, bwt, 2 * sub + 1, nTk, bn, T)
                gi = jf % NGS
                g_, bg_ = gs[:, gi, :], bgs[gi]
                S.op("act", lambda e: e.copy(g_[:, 2:T + 2], psg[:, :]), reads=[bpsg], writes=[bg_])
                S.op("dve", lambda e: e.tensor_copy(g_[:, 0:2], gcar[:, jf, :]), reads=[bgcar[jf]], writes=[bg_])
                S.op("dve", lambda e: e.tensor_copy(gcar[:, jf, :], g_[:, T:T + 2]), reads=[bg_], writes=[bgcar[jf]])
                c_, bc_ = tmpf()
                S.op("dve", lambda e: e.tensor_scalar(c_, g_[:, 0:T], cw(0, jf), pr("conv_b", jf, jf + 1), ALU.mult, ALU.add), reads=[bg_, bpar], writes=[bc_])
                S.op("dve", lambda e: e.scalar_tensor_tensor(c_, g_[:, 1:T + 1], cw(1, jf), c_, ALU.mult, ALU.add), reads=[bg_, bpar, bc_], writes=[bc_])
                S.op("dve", lambda e: e.scalar_tensor_tensor(c_, g_[:, 2:T + 2], cw(2, jf), c_, ALU.mult, ALU.add), reads=[bg_, bpar, bc_], writes=[bc_])
                pend.append((jf, c_, bc_, psu, bpsu))
                if len(pend) > 1:
                    ffn_B(*pend.pop(0))
        while pend:
            ffn_B(*pend.pop(0))
        if ti + 1 < NT:
            S.dma("sp", nT[:], nts_d[ti + 1], reads=[bnts[ti + 1]], writes=bn)
        if ti == 0:
            tap("act", X1[:], bX, [128, JF, T])
        for j in range(8):
            wt, bwt = ring_next(f"dn{j}")
            ps, bps = proj_F(wt, bwt, 0, lambda k: X1[:, k, :], bX, T, nk=JF)
            S.op("dve", lambda e, j=j: e.tensor_tensor(hwin(j), hwin(j), ps[:, :], ALU.add), reads=[bps, bh[j]], writes=[bh[j]])
        ffn_finish(ti, T, H0 - 1, tok0 - 1, skip_first=(ti == 0), mid=((lambda: hg_front(0)) if ti + 1 < NT else None))
        S.op("pool", lambda e: e.tensor_copy(hT[:, :, H0 - 1:H0], hT[:, :, H0 + T - 1:H0 + T]), reads=bh, writes=bh)

    cwa = lambda j_: pr("conv_w", j_ * JF, (j_ + 1) * JF)
    S.op("dve", lambda e: e.tensor_tensor(small[:, 0, :], gcar[:, :, 0], cwa(0), ALU.mult), reads=bgcar + [bpar], writes=[bsmall[0]])
    S.op("dve", lambda e: e.tensor_tensor(small[:, 1, :], gcar[:, :, 1], cwa(1), ALU.mult), reads=bgcar + [bpar], writes=[bsmall[1]])
    S.op("dve", lambda e: e.tensor_tensor(small[:, 0, :], small[:, 0, :], small[:, 1, :], ALU.add), reads=[bsmall[0], bsmall[1]], writes=[bsmall[0]])
    S.op("dve", lambda e: e.tensor_tensor(small[:, 2, :], small[:, 0, :], pr("conv_b"), ALU.add), reads=[bsmall[0], bpar], writes=[bsmall[2]])
    S.op("act", lambda e: e.activation(small[:, 3, :], small[:, 2, :], AF.Silu), reads=[bsmall[2]], writes=[bsmall[3]])
    S.op("dve", lambda e: e.tensor_tensor(actl[:], small[:, 3, :], ucar[:], ALU.mult), reads=[bsmall[3]] + bucar, writes=[bactl])
    for j in range(8):
        wt, bwt = ring_next(f"dn{j}")
        ps, bps = psum_next()
        for kc in range(JF):
            S.op("pe", lambda e, kc=kc: e.matmul(ps[:, 0:1], wt[:, kc * 128:(kc + 1) * 128], actl[:, kc:kc + 1], start=(kc == 0), stop=(kc == JF - 1)),
                 reads=[bwt, bactl], writes=[bps], inc=(kc == JF - 1))
        S.op("dve", lambda e, j=j: e.tensor_tensor(hT[:, j, H0 - 1:H0], hT[:, j, H0 - 1:H0], ps[:, 0:1], ALU.add), reads=[bps, bh[j]], writes=[bh[j]])
    ffn_finish(NT, 1, H0 - 1, SEQ - 1, skip_first=False)
    return finish(nc, S, dbg_out, final_toks)


def finish(nc, S, dbg_out, final_toks=()):
    for tok in list(dbg_out.values()) + list(final_toks):
        S.wait_tok("sp", tok)
    return nc, S, dbg_out


_CACHE = {}


def _get_program(SEQ):
    if SEQ not in _CACHE:
        _CACHE[SEQ] = build_program(SEQ)
    return _CACHE[SEQ]


def kernel(**inputs):
    inp = {k: np.asarray(v) for k, v in inputs.items()}
    x = inp["x"].astype(np.float32, copy=False)
    B, SEQ, _ = x.shape
    wall, index, bounds = build_wall(inp)
    par, _ = build_params(inp)
    con, _ = build_consts()
    nc, S, _ = _get_program(SEQ)
    in_maps = []
    for b in range(B):
        in_maps.append({
            "x": np.ascontiguousarray(x[b]),
            "mem": np.ascontiguousarray(inp["mem"][b].astype(np.float32, copy=False)),
            "pos": np.ascontiguousarray(inp["positions"][b].astype(np.int32, copy=False).reshape(1, SEQ)),
            "wall": wall, "par": par, "con": con,
        })
    res = run_bass_kernel_spmd(nc, in_maps, core_ids=list(range(B)))
    return np.stack([np.asarray(r["out"], dtype=np.float32) for r in res.results], axis=0)
```

```python
import numpy as np
import ml_dtypes
import concourse.bass as bass
import concourse.mybir as mybir
from concourse.bass_utils import run_bass_kernel_spmd

F32 = mybir.dt.float32
BF16 = mybir.dt.bfloat16
I32 = mybir.dt.int32
AF = mybir.ActivationFunctionType
ALU = mybir.AluOpType
AX = mybir.AxisListType


class Buf:
    __slots__ = ("name", "w", "r", "ld_sem", "ld_cnt", "st_sem", "st_cnt")

    def __init__(self, name):
        self.name = name
        self.w = None
        self.r = {}
        self.ld_sem = None
        self.ld_cnt = 0
        self.st_sem = None
        self.st_cnt = 0


class Sched:
    SEM_LIMIT = 60000

    def __init__(self, nc):
        self.nc = nc
        self.eng = {"pe": nc.tensor, "act": nc.scalar, "dve": nc.vector,
                    "pool": nc.gpsimd, "sp": nc.sync}
        self.sem = {}
        self.cnt = {}
        self.nsem = 0
        for e in ("pe", "act", "dve", "pool"):
            self._new_sem(e)
        self.waited = {e: {} for e in self.eng}
        self.n_ops = {e: 0 for e in self.eng}
        self.n_waits = {e: 0 for e in self.eng}

    def _new_sem(self, e):
        self.nsem += 1
        self.sem[e] = self.nc.alloc_semaphore(f"s_{e}_{self.nsem}")
        self.cnt[e] = 0

    def buf(self, name):
        return Buf(name)

    def _wait(self, e, sem, val):
        w = self.waited[e]
        if w.get(sem, 0) >= val:
            return
        self.eng[e].wait_ge(sem, val)
        w[sem] = val
        self.n_waits[e] += 1

    def _deps(self, e, reads, writes, own):
        strict = e != "pe"
        for b in reads:
            if b.w is not None:
                self._wait(e, *b.w)
        for b in writes:
            if b.w is not None and (strict or b.w[0] is not own):
                self._wait(e, *b.w)
            for s, v in b.r.items():
                if strict or s is not own:
                    self._wait(e, s, v)

    def op(self, e, fn, reads=(), writes=(), inc=True):
        if inc and self.cnt[e] >= self.SEM_LIMIT:
            self._new_sem(e)
        own = self.sem[e]
        self._deps(e, reads, writes, own)
        ins = fn(self.eng[e])
        self.n_ops[e] += 1
        if inc:
            self.cnt[e] += 1
            ins.then_inc(own, 1)
            val = self.cnt[e]
        else:
            val = self.cnt[e] + 1
        for b in reads:
            if b.r.get(own, 0) < val:
                b.r[own] = val
        for b in writes:
            b.w = (own, val)
            b.r = {}
        return ins

    def dma(self, q, out_ap, in_ap, reads=(), writes=(), **kw):
        own = None
        self._deps(q, reads, writes, own)
        ins = self.eng[q].dma_start(out=out_ap, in_=in_ap, **kw)
        self.n_ops[q] += 1
        if writes:
            b = writes[0]
            if b.ld_sem is None or b.ld_cnt >= self.SEM_LIMIT:
                self.nsem += 1
                b.ld_sem = self.nc.alloc_semaphore(f"ld_{b.name}_{self.nsem}")
                b.ld_cnt = 0
            b.ld_cnt += 16
            ins.then_inc(b.ld_sem, 16)
            tok = (b.ld_sem, b.ld_cnt)
            for wb in writes:
                wb.w = tok
                wb.r = {}
            for rb in reads:
                rb.r[tok[0]] = tok[1]
        else:
            b = reads[0]
            if b.st_sem is None or b.st_cnt >= self.SEM_LIMIT:
                self.nsem += 1
                b.st_sem = self.nc.alloc_semaphore(f"st_{b.name}_{self.nsem}")
                b.st_cnt = 0
            b.st_cnt += 16
            ins.then_inc(b.st_sem, 16)
            tok = (b.st_sem, b.st_cnt)
            for rb in reads:
                rb.r[tok[0]] = tok[1]
        return tok

    def wait_tok(self, e, tok):
        self._wait(e, tok[0], tok[1])


D = 1024
KC = 8
T = 512
C = 64
NCH = T // C
MID = 31
H0 = 4
HWID = T + H0
DFF = 2816
JF = DFF // 128
MEM = 256
EPS = 1e-6
SLOT = 4096
NSLOT = 4
CHUNK0 = 4096
ROPE_THETA = 500000.0


def _f_tiles(W):
    Kd, N = W.shape
    return np.ascontiguousarray(W.reshape(Kd // 128, 128, N // 128, 128).transpose(1, 2, 0, 3))


def _t_tiles(W):
    Kd, N = W.shape
    return np.ascontiguousarray(W.reshape(Kd // 128, 128, N).transpose(1, 0, 2))


def _partner(W, nh):
    Wh = W.reshape(W.shape[0], nh, 64)
    P = np.zeros_like(Wh)
    P[:, :, 0:8] = Wh[:, :, 8:16]
    P[:, :, 8:16] = Wh[:, :, 0:8]
    return P.reshape(W.shape)


def build_wall(inp):
    w_in = inp["w_in"][0]
    segs = []

    def add(name, arr):
        segs.append((name, arr.reshape(128, -1)))

    ka = w_in[:, 3072:3200]
    kdup = np.concatenate([ka[:, 0:64], ka[:, 0:64], ka[:, 64:128], ka[:, 64:128]], axis=1)
    kdup_p = _partner(kdup, 4)
    add("qr", _f_tiles(w_in[:, 0:512]))
    add("fzb", _f_tiles(w_in[:, 1024:1536]))
    add("ir", _t_tiles(w_in[:, 1536:2048]))
    add("ka4", _f_tiles(np.concatenate([kdup, kdup_p], axis=1)))
    add("va", _t_tiles(w_in[:, 3200:3328]))
    g0 = len(segs)
    add("fzf", _f_tiles(w_in[:, 512:1024]))
    add("gr", _f_tiles(w_in[:, 2048:2560]))
    qa = w_in[:, 2560:3072]
    qa_t = _f_tiles(qa)
    qp_t = _f_tiles(_partner(qa, 8))
    for jj in range(2):
        add(f"qq{jj}", np.concatenate([qa_t[:, 2 * jj].reshape(128, -1), qp_t[:, 2 * jj].reshape(128, -1),
                                       qa_t[:, 2 * jj + 1].reshape(128, -1), qp_t[:, 2 * jj + 1].reshape(128, -1)], axis=1))
    gr_t = _f_tiles(w_in[:, 3328:4352])
    ga_t = _f_tiles(w_in[:, 4352:5376])
    br_t = _f_tiles(inp["w_br_rec"][0])
    ba_t = _f_tiles(inp["w_br_att"][0])
    for j in range(8):
        add(f"mg{j}", np.concatenate([gr_t[:, j].reshape(128, -1), ga_t[:, j].reshape(128, -1),
                                      br_t[:, j].reshape(128, -1), ba_t[:, j].reshape(128, -1)], axis=1))
    wo = _f_tiles(inp["w_mix_out"][0])
    add("wout0", wo[:, 0:4]); add("wout1", wo[:, 4:8])
    g1 = len(segs)
    wq = _f_tiles(inp["w_mem_q"][0])
    add("wmq0", wq[:, 0:4]); add("wmq1", wq[:, 4:8])
    wk = _f_tiles(inp["w_mem_kv"][0][:, 0:1024])
    add("wmk0", wk[:, 0:4]); add("wmk1", wk[:, 4:8])
    add("wmv0", _t_tiles(inp["w_mem_kv"][0][:, 1024:1536]))
    add("wmv1", _t_tiles(inp["w_mem_kv"][0][:, 1536:2048]))
    wmo = _f_tiles(inp["w_mem_o"][0])
    add("wmo0", wmo[:, 0:4]); add("wmo1", wmo[:, 4:8])
    g2 = len(segs)
    wu = _f_tiles(inp["w_up"][0][:, 0:DFF])
    wg = _f_tiles(inp["w_up"][0][:, DFF:2 * DFF])
    for jj in range(JF // 2):
        add(f"up{jj}", np.concatenate([wu[:, 2 * jj].reshape(128, -1), wg[:, 2 * jj].reshape(128, -1),
                                       wu[:, 2 * jj + 1].reshape(128, -1), wg[:, 2 * jj + 1].reshape(128, -1)], axis=1))
    wd = _f_tiles(inp["w_down"][0])
    for j in range(8):
        add(f"dn{j}", wd[:, j])
    g3 = len(segs)
    index = {}
    off = 0
    for name, arr in segs:
        assert arr.shape[1] <= SLOT, (name, arr.shape)
        index[name] = (off, arr.shape[1])
        off += arr.shape[1]
    wall = np.ascontiguousarray(np.concatenate([a for _, a in segs], axis=1).astype(np.float32))
    bounds = [index[segs[g - 1][0]][0] + index[segs[g - 1][0]][1] for g in (g0, g1, g2, g3)]
    return wall, index, bounds


def wall_index():
    index = {}
    off = 0
    order = ([("qr", 4096), ("fzb", 4096), ("ir", 4096), ("ka4", 4096), ("va", 1024)],
             [("fzf", 4096), ("gr", 4096), ("qq0", 4096), ("qq1", 4096)] + [(f"mg{j}", 3072) for j in range(8)]
             + [("wout0", 4096), ("wout1", 4096)],
             [("wmq0", 4096), ("wmq1", 4096), ("wmk0", 4096), ("wmk1", 4096), ("wmv0", 4096), ("wmv1", 4096),
              ("wmo0", 4096), ("wmo1", 4096)],
             [(f"up{j}", 4096) for j in range(JF // 2)] + [(f"dn{j}", 2816) for j in range(8)])
    bounds = []
    for grp in order:
        for name, L in grp:
            index[name] = (off, L)
            off += L
        bounds.append(off)
    return index, bounds, off


def build_params(inp):
    cols = []
    names = {}

    def add(name, arr):
        arr = np.asarray(arr, np.float32).reshape(128, -1)
        names[name] = (sum(c.shape[1] for c in cols), arr.shape[1])
        cols.append(arr)

    def pk(v):
        v = np.asarray(v, np.float32).reshape(-1, 128)
        return np.ascontiguousarray(v.T)

    add("g_mix", pk(inp["norm_mix"][0]))
    add("g_mem", pk(inp["norm_mem"][0]))
    add("g_memkv", pk(inp["norm_mem_kv"][0]))
    add("g_ffn", pk(inp["norm_ffn"][0]))
    add("g_fin", pk(inp["final_norm"]))
    lbr = inp["lower_bounds"]
    add("lb_raw", np.stack([pk(lbr[d, s]) for d in range(2) for s in range(2)], axis=1))
    add("hgn", pk(inp["hg_norm"][0]))
    add("sink", np.broadcast_to(np.asarray(inp["attn_sink"][0], np.float32)[None, :], (128, 8)))
    add("conv_w", np.stack([pk(inp["conv_w"][0][j]) for j in range(3)], axis=1))
    add("conv_b", pk(inp["conv_b"][0]))
    par = np.ascontiguousarray(np.concatenate(cols, axis=1))
    return par, names


def params_index():
    names = {}
    off = 0
    for n, L in (("g_mix", 8), ("g_mem", 8), ("g_memkv", 8), ("g_ffn", 8), ("g_fin", 8), ("lb_raw", 16),
                 ("hgn", 4), ("sink", 8), ("conv_w", 66), ("conv_b", 22)):
        names[n] = (off, L)
        off += L
    return names, off


def build_consts():
    cols = {}
    p = np.arange(128)
    ident = np.eye(128, dtype=np.float32)
    s = (p % 64)[:, None]
    t = np.arange(64)[None, :]
    mask_f = (s <= t).astype(np.float32)
    mask_b = (s >= t).astype(np.float32)
    key = p[:, None]
    q = np.arange(128)[None, :]
    band_prev = (key >= q).astype(np.float32)
    band_next = (key <= q).astype(np.float32)
    m01 = np.ones((128, T), np.float32)
    m01[:, ::C] = 0.0
    inv_freq = 1.0 / (ROPE_THETA ** (np.arange(0, 16, 2, dtype=np.float32) / 16.0))
    invf = np.zeros((128, 1), np.float32)
    phase_c = np.full((128, 1), 0.5 * np.pi, np.float32)
    phase_s = np.zeros((128, 1), np.float32)
    for pp in range(128):
        r = pp % 64
        if r < 16:
            invf[pp, 0] = inv_freq[r % 8]
            phase_s[pp, 0] = np.pi if r < 8 else 0.0
    ones = np.ones((128, 128), np.float32)
    order = [("ident", ident), ("mask_f", mask_f), ("mask_b", mask_b), ("band_prev", band_prev),
             ("band_next", band_next), ("m01", m01), ("invf", invf), ("phase_c", phase_c), ("phase_s", phase_s), ("ones", ones)]
    names = {}
    off = 0
    arrs = []
    for n, a in order:
        names[n] = (off, a.shape[1])
        off += a.shape[1]
        arrs.append(a.astype(np.float32))
    return np.ascontiguousarray(np.concatenate(arrs, axis=1)), names


def weight_plan(NT):
    plan = []
    for _ in range(NT):
        plan += ["fzb", "qr", "ir", "ka4", "va"]
    plan += ["wmk0", "wmk1", "wmv0", "wmv1"]
    for _ in range(NT):
        plan += ["fzf", "gr", "qq0", "qq1"]
        plan += [f"mg{j}" for j in range(8)]
        plan += ["wout0", "wout1", "wmq0", "wmq1", "wmo0", "wmo1"]
        plan += [f"up{j}" for j in range(JF // 2)]
        plan += [f"dn{j}" for j in range(8)]
    plan += [f"dn{j}" for j in range(8)]
    return plan


def build_program(SEQ, dbg_taps=(), stop_after=None):
    nc = bass.Bass("TRN2", target_bir_lowering=False)
    S = Sched(nc)
    NT = SEQ // T
    NBLK = SEQ // 128
    widx, wbounds, LTOT = wall_index()
    pidx, NPAR = params_index()
    _, cidx = build_consts()
    NCON = sum(v[1] for v in cidx.values())

    x_d = nc.dram_tensor("x", [SEQ, D], F32, kind="ExternalInput").ap()
    mem_d = nc.dram_tensor("mem", [MEM, D], F32, kind="ExternalInput").ap()
    pos_d = nc.dram_tensor("pos", [1, SEQ], I32, kind="ExternalInput").ap()
    wall_d = nc.dram_tensor("wall", [128, LTOT], F32, kind="ExternalInput").ap()
    par_d = nc.dram_tensor("par", [128, NPAR], F32, kind="ExternalInput").ap()
    con_d = nc.dram_tensor("con", [128, NCON], F32, kind="ExternalInput").ap()
    out_d = nc.dram_tensor("out", [SEQ, D], F32, kind="ExternalOutput").ap()
    wscr_d = nc.dram_tensor("wscr", [128, LTOT], BF16).ap()
    ob_d = nc.dram_tensor("obscr", [128, 4, SEQ], F32).ap()
    nts_d = nc.dram_tensor("ntscr", [NT, 128, KC * T], BF16).ap()
    vs_d = nc.dram_tensor("vscr", [NT, 128, 4 * 512], BF16).ap()
    sqs_d = nc.dram_tensor("sqscr", [NT, 128, 4 * T], BF16).ap()
    dbg_out = {}

    def sb(name, shape, dt):
        return nc.alloc_sbuf_tensor("sb_" + name, shape, dt)

    def bufs(name, n):
        return [S.buf(f"{name}{i}") for i in range(n)]

    con = sb("con", [128, NCON], F32); bcon = S.buf("con")
    par = sb("par", [128, NPAR], F32); bpar = S.buf("par")
    identb = sb("identb", [128, 128], BF16)
    onesD = sb("onesD", [128, 128], BF16)
    onesV = sb("onesV", [128, 128], BF16)
    ones1 = sb("ones1", [128, 128], BF16)
    bandb = sb("bandb", [128, 3, 128], BF16)
    bcst = S.buf("cst")
    sm = sb("sm", [128, 64], F32); bsm = S.buf("sm")
    hT = sb("hT", [128, KC, HWID], F32); bh = bufs("h", KC)
    nT = sb("nT", [128, KC, T], BF16); bn = bufs("n", KC)
    B8 = sb("B8", [128, KC, T], BF16); bB = bufs("B", KC)
    X1 = sb("X1", [128, JF, T], BF16); bX = bufs("X", JF)
    NPF = 6
    PF = sb("PF", [128, NPF, T], F32); bPF = bufs("pf", NPF)
    HG = sb("HG", [128, 3, 4, T], F32); bHG = [bufs(f"hg{i}_", 4) for i in range(3)]
    KT = sb("KT", [128, 2, SEQ], BF16); bKT = bufs("kt", NT)
    VW = 66
    Vst = sb("Vst", [128, NBLK, 2, VW], BF16); bV = bufs("v", NT)
    KmT = sb("KmT", [128, 8, MEM], BF16); bKm = S.buf("KmT")
    Vm = sb("Vm", [128, 2, D], BF16); bVm = S.buf("Vm")
    ktok = sb("ktok", [128, 4, 512], BF16); bktok = bufs("ktok", 4)
    vtok = sb("vtok", [128, 4, 512], BF16); bvtok = bufs("vtok", 4)
    ATm = sb("ATm", [128, 2, 256], BF16); bATm = bufs("atm", 2)
    U = sb("U", [128, 4, 128], F32); bU = bufs("U", 4)
    S16 = sb("S16", [128, 2, 4, 128], BF16); bS16 = [bufs(f"s16_{i}_", 4) for i in range(2)]
    la = sb("la", [128, 4, NCH], F32); bla = S.buf("la")
    lbt = sb("lbt", [128, 4, NCH], F32); blbt = S.buf("lbt")
    refc = sb("refc", [128, 4, NCH], F32); brefc = S.buf("refc")
    lg = sb("lg", [128, 4, NCH], F32); blg = S.buf("lg")
    gam = sb("gam", [128, 4, NCH], F32); bgam = S.buf("gam")
    carry = sb("carry", [128, 2, 4], F32); bcarry = bufs("carry", 2)
    NPT = 3
    PT = sb("PT", [128, NPT, 3, 128], BF16); bPT = bufs("PT", NPT)
    oatok = sb("oatok", [128, 2, 512], BF16); boatok = bufs("oatok", 2)
    dsm = sb("dsm", [128, 2, 8], F32); bdsm = bufs("dsm", 2)
    xt = sb("xt", [128, 2, D], F32); bxt = bufs("xt", 2)
    ot = sb("ot", [128, 2, 512], F32); bot = bufs("ot", 2)
    wring = sb("wring", [128, NSLOT, SLOT], BF16); bring = bufs("ring", NSLOT)
    posi = sb("posi", [128, T], I32); bposi = S.buf("posi")
    rot = sb("rot", [128, 2, T], F32); brot = bufs("rot", 2)
    rstd_t = sb("rstd", [128, T], F32); brstd = S.buf("rstd")
    nint = sb("nint", [128, T], I32); bnint = S.buf("nint")
    gcar = sb("gcar", [128, JF, 2], F32); bgcar = bufs("gcar", JF)
    ucar = sb("ucar", [128, JF], F32); bucar = bufs("ucar", JF)
    NGS = 2
    gs = sb("gs", [128, NGS, T + 2], F32); bgs = bufs("gs", NGS)
    small = sb("small", [128, 4, JF], F32); bsmall = bufs("small", 4)
    actl = sb("actl", [128, JF], BF16); bactl = S.buf("actl")

    PS = [nc.alloc_psum_tensor(f"ps{i}", [128, 512], F32) for i in range(8)]
    bPS = bufs("ps", 8)
    ps_state = {"rot": 0, "held": set()}

    def psum_next():
        while True:
            i = ps_state["rot"] % 8
            ps_state["rot"] += 1
            if i not in ps_state["held"]:
                return PS[i], bPS[i]

    def psum_hold(n):
        res = []
        for _ in range(n):
            while True:
                i = ps_state["rot"] % 8
                ps_state["rot"] += 1
                if i not in ps_state["held"]:
                    break
            ps_state["held"].add(i)
            res.append(i)
        return res

    def psum_release(idx):
        for i in idx:
            ps_state["held"].discard(i)

    pf_state = {"rot": 0}

    def tmpf():
        i = pf_state["rot"] % NPF
        pf_state["rot"] += 1
        return PF[:, i, :], bPF[i]

    ew_state = {"rot": 0}

    def ew():
        return "dve"

    def cs(name, a=0, b=None):
        off, L = cidx[name]
        b = L if b is None else b
        return con[:, off + a: off + b]

    def pr(name, a=0, b=None):
        off, L = pidx[name]
        b = L if b is None else b
        return par[:, off + a: off + b]

    def tap(name, ap, bufl, shape):
        if name not in dbg_taps:
            return
        key = name
        n = 0
        while key in dbg_out:
            n += 1
            key = f"{name}_{n}"
        d = nc.dram_tensor("dbg_" + key, list(shape), ap.dtype, kind="ExternalOutput").ap()
        dbg_out[key] = S.dma("sp", d, ap, reads=list(bufl))

    plan = weight_plan(NT)
    ring = {"issued": 0, "consumed": 0}
    bgrp = bufs("wgrp", len(wbounds))

    def grp_of(off):
        for g, b in enumerate(wbounds):
            if off < b:
                return g
        raise AssertionError

    def ring_issue(m):
        name = plan[m]
        off, L = widx[name]
        slot = m % NSLOT
        S.dma("sp", wring[:, slot, 0:L], wscr_d[:, off:off + L], reads=[bgrp[grp_of(off)]], writes=[bring[slot]])

    def ring_next(name):
        n = ring["consumed"]
        assert plan[n] == name, (n, plan[n], name)
        while ring["issued"] < min(len(plan), n + NSLOT):
            ring_issue(ring["issued"])
            ring["issued"] += 1
        ring["consumed"] += 1
        slot = n % NSLOT
        return wring[:, slot, :], bring[slot]

    S.dma("sp", con[:], con_d, writes=[bcon])
    S.dma("sp", par[:], par_d, writes=[bpar])
    S.op("dve", lambda e: e.tensor_copy(identb[:], cs("ident")), reads=[bcon], writes=[bcst])
    S.op("dve", lambda e: e.tensor_scalar(onesD[:], cs("ones"), 1.0 / D, None, ALU.mult), reads=[bcon], writes=[bcst])
    S.op("dve", lambda e: e.tensor_scalar(onesV[:], cs("ones"), 1.0 / 128, None, ALU.mult), reads=[bcon], writes=[bcst])
    S.op("dve", lambda e: e.tensor_copy(ones1[:], cs("ones")), reads=[bcon], writes=[bcst])
    S.op("dve", lambda e: e.tensor_copy(bandb[:, 0, :], cs("band_prev")), reads=[bcon], writes=[bcst])
    S.op("dve", lambda e: e.tensor_copy(bandb[:, 1, :], cs("ones")), reads=[bcon], writes=[bcst])
    S.op("dve", lambda e: e.tensor_copy(bandb[:, 2, :], cs("band_next")), reads=[bcon], writes=[bcst])
    lbr = pr("lb_raw").rearrange("p (d s h) -> p d s h", d=2, s=2)
    S.op("dve", lambda e: e.tensor_tensor(sm[:, 32:40].rearrange("p (d h) -> p d h", d=2), lbr[:, :, 0, :], lbr[:, :, 1, :], ALU.subtract),
         reads=[bpar], writes=[bsm])
    S.op("act", lambda e: e.activation(sm[:, 0:8], sm[:, 32:40], AF.Sigmoid), reads=[bsm], writes=[bsm])
    S.op("dve", lambda e: e.tensor_scalar(sm[:, 8:16], sm[:, 0:8], -1.0, 1.0, ALU.mult, ALU.add), reads=[bsm], writes=[bsm])
    S.op("dve", lambda e: e.tensor_scalar(sm[:, 16:24], sm[:, 0:8], 1.0, -1.0, ALU.mult, ALU.add), reads=[bsm], writes=[bsm])
    S.op("act", lambda e: e.activation(sm[:, 24:32], pr("sink"), AF.Exp), reads=[bpar], writes=[bsm])
    S.op("pool", lambda e: e.memset(sm[:, 40:41], EPS), writes=[bsm])
    S.op("pool", lambda e: e.memset(hT[:], 0.0), writes=bh)
    S.op("pool", lambda e: e.memset(gcar[:], 0.0), writes=bgcar)
    S.op("pool", lambda e: e.memset(ucar[:], 0.0), writes=bucar)
    S.op("pool", lambda e: e.memset(carry[:], 0.0), writes=bcarry)
    S.op("pool", lambda e: e.memset(U[:], 0.0), writes=bU)
    S.op("pool", lambda e: e.memset(Vst[:], 1.0), writes=bV)

    ci = 0
    g_start = 0
    for g, g_end in enumerate(wbounds):
        a = g_start
        while a < g_end:
            b = min(a + CHUNK0, g_end)
            L = b - a
            sl = ci % 2
            S.dma("pool", wscr_d[:, a:b], wall_d[:, a:b], writes=[bgrp[g]])
            ci += 1
            a = b
        g_start = g_end

    xpre = {"tile": None}

    def x_dma(ti, blk):
        tok0 = ti * T
        sl = blk % 2
        S.dma("sp", xt[:, sl, :], x_d[tok0 + 128 * blk: tok0 + 128 * (blk + 1), :], writes=[bxt[sl]])

    def prefetch_x(ti):
        x_dma(ti, 0)
        x_dma(ti, 1)
        xpre["tile"] = ti

    def load_x_tile_gen(ti, nxt=None):
        have = xpre["tile"] == ti
        xpre["tile"] = None
        for blk in range(4):
            sl = blk % 2
            if not (have and blk < 2):
                x_dma(ti, blk)
            for half in range(2):
                ps, bps = psum_next()
                for kk in range(4):
                    k = half * 4 + kk
                    S.op("pe", lambda e, ps=ps, kk=kk, k=k, sl=sl: e.transpose(ps[:, kk * 128:(kk + 1) * 128], xt[:, sl, k * 128:(k + 1) * 128], cs("ident")),
                         reads=[bxt[sl], bcon], writes=[bps], inc=(kk == 3))
                S.op("act", lambda e, ps=ps, half=half, blk=blk: e.copy(
                    hT[:, half * 4:half * 4 + 4, H0 + 128 * blk: H0 + 128 * (blk + 1)],
                    ps[:, :].rearrange("p (a b) -> p a b", a=4)), reads=[bps], writes=bh[half * 4:half * 4 + 4])
            if blk >= 2 and nxt is not None:
                x_dma(nxt, blk - 2)
            if blk == 3 and nxt is not None:
                xpre["tile"] = nxt
            yield

    def load_x_tile(ti, nxt=None):
        for _ in load_x_tile_gen(ti, nxt):
            pass

    def rmsnorm(src, bsrc, gname, N, out, bout, nk=KC, ones=None):
        for _ in rmsnorm_gen(src, bsrc, gname, N, out, bout, nk, ones):
            pass

    def rmsnorm_gen(src, bsrc, gname, N, out, bout, nk=KC, ones=None):
        ones = onesD if ones is None else ones
        for k in range(nk):
            S.op("act", lambda e, k=k: e.activation(B8[:, k, 0:N], src(k), AF.Square), reads=[bsrc[k]], writes=[bB[k]])
        yield
        ps, bps = psum_next()
        for k in range(nk):
            S.op("pe", lambda e, k=k: e.matmul(ps[:, 0:N], ones[:], B8[:, k, 0:N], start=(k == 0), stop=(k == nk - 1)),
                 reads=[bB[k], bcst], writes=[bps], inc=(k == nk - 1))
        yield
        rs, brs = (rstd_t[:, :], brstd) if N == T else tmpf()
        S.op("act", lambda e: e.activation(rs[:, 0:N], ps[:, 0:N], AF.Ln, bias=sm[:, 40:41]), reads=[bps, bsm], writes=[brs])
        S.op("act", lambda e: e.activation(rs[:, 0:N], rs[:, 0:N], AF.Exp, scale=-0.5), reads=[brs], writes=[brs])
        yield
        for k in range(nk):
            S.op("dve", lambda e, k=k: e.scalar_tensor_tensor(out(k), src(k), pr(gname, k, k + 1), rs[:, 0:N], ALU.mult, ALU.mult),
                 reads=[bsrc[k], brs, bpar], writes=[bout[k]])
            if k % 4 == 3:
                yield

    def proj_F(wt, bwt, blk, rhs, brhs, N, nk=KC, wstride=None):
        ps, bps = psum_next()
        for k in range(nk):
            o = (blk * nk + k) * 128
            S.op("pe", lambda e, o=o, k=k: e.matmul(ps[:, 0:N], wt[:, o:o + 128], rhs(k), start=(k == 0), stop=(k == nk - 1)),
                 reads=[bwt, brhs[k]], writes=[bps], inc=(k == nk - 1))
        return ps, bps

    nTk = lambda k: nT[:, k, :]

    LF, KK, BQ = 0, 1, 2
    bsqs = bufs("sqs", NT)
    bvs = bufs("vs", NT)
    bnts = bufs("nts", NT)

    def hg_front(d):
        wt, bwt = ring_next("fzf" if d == 0 else "fzb")
        for hd in range(4):
            ps, bps = proj_F(wt, bwt, hd, nTk, bn, T)
            S.op("act", lambda e, hd=hd: e.activation(HG[:, KK, hd, :], ps[:, :], AF.Sigmoid), reads=[bps], writes=[bHG[KK][hd]])

    def hgrn_tile(ti, d, first, mid_hook=None, bg=None, late_hook=None, front_done=False):
        tok0 = ti * T
        if not front_done:
            hg_front(d)
        sgs = [(HG[:, KK, hd, :], bHG[KK][hd]) for hd in range(4)]
        if d == 1:
            wt, bwt = ring_next("qr")
            for hd in range(4):
                ps, bps = proj_F(wt, bwt, hd, nTk, bn, T)
                S.op("act", lambda e, hd=hd: e.activation(X1[:, 8 + hd, :], ps[:, :], AF.Silu), reads=[bps], writes=[bX[8 + hd]])
            S.dma("sp", sqs_d[ti], X1[:, 8:12, :], reads=bX[8:12], writes=[bsqs[ti]])
        else:
            S.dma("sp", X1[:, 0:4, :], sqs_d[ti], reads=[bsqs[ti]], writes=bX[0:4])
        if d == 1:
            wt, bwt = ring_next("ir")
            for blk in range(4):
                ps, bps = psum_next()
                for k in range(KC):
                    S.op("pe", lambda e, k=k, blk=blk: e.matmul(ps[:, :], nT[:, k, blk * 128:(blk + 1) * 128], wt[:, k * 512:(k + 1) * 512],
                                                                start=(k == 0), stop=(k == KC - 1)),
                         reads=[bwt, bn[k]], writes=[bps], inc=(k == KC - 1))
                S.op("act", lambda e, blk=blk: e.copy(vtok[:, blk, :], ps[:, :]), reads=[bps], writes=[bvtok[blk]])
            S.dma("sp", vs_d[ti], vtok[:], reads=bvtok, writes=[bvs[ti]])
        else:
            S.dma("sp", vtok[:], vs_d[ti], reads=[bvs[ti]], writes=bvtok)
        if mid_hook is not None:
            mid_hook()
        for hd in range(4):
            sg, bsg = sgs[hd]
            c = d * 4 + hd
            S.op("act", lambda e, c=c, hd=hd: e.activation(HG[:, LF, hd, :], sg, AF.Ln, scale=sm[:, 8 + c:9 + c], bias=sm[:, c:c + 1]),
                 reads=[bsg, bsm], writes=[bHG[LF][hd]])
            S.op("dve", lambda e, c=c, hd=hd: e.tensor_scalar(HG[:, KK, hd, :], sg, sm[:, 16 + c:17 + c], sm[:, 8 + c:9 + c], ALU.mult, ALU.add),
                 reads=[bsg, bsm], writes=[bHG[KK][hd]])
        tap("lf", HG[:, LF], bHG[LF], [128, 4, T])
        tap("kk", HG[:, KK], bHG[KK], [128, 4, T])
        flat = lambda i: HG[:, i].rearrange("p h t -> p (h t)")
        ch = lambda i: HG[:, i].rearrange("p h (c s) -> p h c s", s=C)
        for hd in range(4):
            S.op("dve", lambda e, hd=hd: e.tensor_tensor_scan(HG[:, BQ, hd, :], cs("m01"), HG[:, LF, hd, :], 0.0, ALU.mult, ALU.add),
                 reads=[bHG[LF][hd], bcon], writes=[bHG[BQ][hd]])
        tap("bq", HG[:, BQ], bHG[BQ], [128, 4, T])
        if d == 0:
            S.op("dve", lambda e: e.tensor_copy(la[:], ch(BQ)[:, :, :, MID]), reads=bHG[BQ], writes=[bla])
            S.op("dve", lambda e: e.tensor_tensor(lbt[:], ch(BQ)[:, :, :, C - 1], ch(BQ)[:, :, :, MID], ALU.subtract), reads=bHG[BQ], writes=[blbt])
            S.op("dve", lambda e: e.tensor_copy(refc[:], ch(BQ)[:, :, :, MID]), reads=bHG[BQ], writes=[brefc])
            S.op("dve", lambda e: e.tensor_tensor(ch(BQ), ch(BQ), refc[:].unsqueeze(3).to_broadcast([128, 4, NCH, C]), ALU.subtract),
                 reads=bHG[BQ] + [brefc], writes=bHG[BQ])
            S.op("act", lambda e: e.activation(flat(LF), flat(BQ), AF.Exp), reads=bHG[BQ], writes=bHG[LF])
            S.op("act", lambda e: e.activation(flat(BQ), flat(BQ), AF.Exp, scale=-1.0), reads=bHG[BQ], writes=bHG[BQ])
        else:
            S.op("dve", lambda e: e.tensor_tensor(flat(LF), flat(BQ), flat(LF), ALU.subtract), reads=bHG[BQ] + bHG[LF], writes=bHG[LF])
            S.op("dve", lambda e: e.tensor_tensor(la[:], ch(BQ)[:, :, :, C - 1], ch(LF)[:, :, :, MID], ALU.subtract), reads=bHG[BQ] + bHG[LF], writes=[bla])
            S.op("dve", lambda e: e.tensor_copy(lbt[:], ch(LF)[:, :, :, MID]), reads=bHG[LF], writes=[blbt])
            S.op("dve", lambda e: e.tensor_copy(refc[:], ch(LF)[:, :, :, MID]), reads=bHG[LF], writes=[brefc])
            S.op("dve", lambda e: e.tensor_tensor(ch(LF), ch(LF), refc[:].unsqueeze(3).to_broadcast([128, 4, NCH, C]), ALU.subtract),
                 reads=bHG[LF] + [brefc], writes=bHG[LF])
            S.op("act", lambda e: e.activation(flat(BQ), flat(LF), AF.Exp), reads=bHG[LF], writes=bHG[BQ])
            S.op("act", lambda e: e.activation(flat(LF), flat(LF), AF.Exp, scale=-1.0), reads=bHG[LF], writes=bHG[LF])
        if d == 0:
            S.op("dve", lambda e: e.tensor_tensor(lg[:, :, 1:NCH], la[:, :, 1:NCH], lbt[:, :, 0:NCH - 1], ALU.add), reads=[bla, blbt], writes=[blg])
            S.op("dve", lambda e: e.tensor_tensor(lg[:, :, 0:1], la[:, :, 0:1], carry[:, 0, :].unsqueeze(2), ALU.add), reads=[bla, bcarry[0]], writes=[blg])
            S.op("dve", lambda e: e.tensor_copy(carry[:, 0, :].unsqueeze(2), lbt[:, :, NCH - 1:NCH]), reads=[blbt], writes=[bcarry[0]])
        else:
            S.op("dve", lambda e: e.tensor_tensor(lg[:, :, 0:NCH - 1], la[:, :, 0:NCH - 1], lbt[:, :, 1:NCH], ALU.add), reads=[bla, blbt], writes=[blg])
            S.op("dve", lambda e: e.tensor_tensor(lg[:, :, NCH - 1:NCH], la[:, :, NCH - 1:NCH], carry[:, 1, :].unsqueeze(2), ALU.add), reads=[bla, bcarry[1]], writes=[blg])
            S.op("dve", lambda e: e.tensor_copy(carry[:, 1, :].unsqueeze(2), lbt[:, :, 0:1]), reads=[blbt], writes=[bcarry[1]])
        S.op("act", lambda e: e.activation(gam[:], lg[:], AF.Exp), reads=[blg], writes=[bgam])
        for hd in range(4):
            S.op(ew(), lambda e, hd=hd: e.tensor_tensor(X1[:, 4 + hd, :], HG[:, KK, hd, :], HG[:, BQ, hd, :], ALU.mult),
                 reads=[bHG[KK][hd], bHG[BQ][hd]], writes=[bX[4 + hd]])
        for hd in range(4):
            src = 8 + hd if d == 1 else hd
            S.op("dve", lambda e, hd=hd, src=src: e.scalar_tensor_tensor(X1[:, hd, :], X1[:, src, :], float(128 ** -0.5), HG[:, LF, hd, :], ALU.mult, ALU.mult),
                 reads=[bX[src], bHG[LF][hd]], writes=[bX[hd]])
        tap("qt", X1[:, 0:4, :], bX[0:4], [128, 4, T])
        tap("kt", X1[:, 4:8, :], bX[4:8], [128, 4, T])
        tap("gam", gam[:], [bgam], [128, 4, NCH])
        if late_hook is not None:
            late_hook()
        for blk in range(4):
            ps, bps = psum_next()
            psb = ps[:, :].bitcast(BF16)
            for hd in range(4):
                S.op("pe", lambda e, hd=hd, blk=blk: e.transpose(psb[:, hd * 128:(hd + 1) * 128], X1[:, 4 + hd, blk * 128:(blk + 1) * 128], identb[:]),
                     reads=[bX[4 + hd], bcst], writes=[bps], inc=(hd == 3))
            S.op("dve", lambda e, blk=blk: e.tensor_copy(ktok[:, blk, :], psb[:, 0:512]), reads=[bps], writes=[bktok[blk]])
        held = psum_hold(4)
        order = range(NCH) if d == 0 else range(NCH - 1, -1, -1)
        mask = cs("mask_f") if d == 0 else cs("mask_b")
        order = list(order)

        def emit_AT(ci_):
            c = order[ci_]
            p0 = 64 * (c % 2)
            cols = slice(c * C, (c + 1) * C)
            ab = ci_ % 2
            psA, bpsA = psum_next()
            for hd in range(4):
                S.op("pe", lambda e, hd=hd: e.matmul(psA[p0:p0 + 64, hd * 64:(hd + 1) * 64], X1[:, 4 + hd, cols], X1[:, hd, cols],
                                                     start=True, stop=True, tile_position=(0, p0)),
                     reads=[bX[4 + hd], bX[hd]], writes=[bpsA], inc=(hd == 3))
            S.op("dve", lambda e: e.tensor_tensor(ATm[p0:p0 + 64, ab, :].rearrange("p (h t) -> p h t", h=4),
                                                  psA[p0:p0 + 64, 0:256].rearrange("p (h t) -> p h t", h=4),
                                                  mask[p0:p0 + 64, :].unsqueeze(1).to_broadcast([64, 4, C]), ALU.mult),
                 reads=[bpsA, bcon], writes=[bATm[ab]])

        emit_AT(0)
        for ci_, c in enumerate(order):
            blk, hf = c // 2, c % 2
            p0 = 64 * hf
            cols = slice(c * C, (c + 1) * C)
            very_first = first and ci_ == 0
            sb_ = ci_ % 2
            ab = ci_ % 2
            if not very_first:
                for hd in range(4):
                    S.op("act", lambda e, hd=hd: e.activation(S16[:, sb_, hd, :], U[:, hd, :], AF.Copy, scale=gam[:, hd, c:c + 1]),
                         reads=[bU[hd], bgam], writes=[bS16[sb_][hd]])
            psU, bpsU = psum_next()
            for hd in range(4):
                S.op("pe", lambda e, hd=hd: e.matmul(psU[:, hd * 128:(hd + 1) * 128], ktok[p0:p0 + 64, blk, hd * 128:(hd + 1) * 128],
                                                     vtok[p0:p0 + 64, blk, hd * 128:(hd + 1) * 128], start=True, stop=True),
                     reads=[bktok[blk], bvtok[blk]], writes=[bpsU], inc=(hd == 3))
            if ci_ + 1 < NCH:
                emit_AT(ci_ + 1)
            for hd in range(4):
                pso, bpso = PS[held[hd]], bPS[held[hd]]
                S.op("pe", lambda e, hd=hd, pso=pso: e.matmul(pso[:, cols], vtok[p0:p0 + 64, blk, hd * 128:(hd + 1) * 128],
                                                              ATm[p0:p0 + 64, ab, hd * 64:(hd + 1) * 64], start=True, stop=very_first),
                     reads=[bvtok[blk], bATm[ab]], writes=[bpso], inc=very_first)
                if not very_first:
                    S.op("pe", lambda e, hd=hd, pso=pso: e.matmul(pso[:, cols], S16[:, sb_, hd, :], X1[:, hd, cols], start=False, stop=True),
                         reads=[bS16[sb_][hd], bX[hd]], writes=[bpso])
            for hd in range(4):
                if very_first:
                    S.op("dve", lambda e, hd=hd: e.tensor_copy(U[:, hd, :], psU[:, hd * 128:(hd + 1) * 128]), reads=[bpsU], writes=[bU[hd]])
                else:
                    S.op("dve", lambda e, hd=hd: e.scalar_tensor_tensor(U[:, hd, :], U[:, hd, :], gam[:, hd, c:c + 1], psU[:, hd * 128:(hd + 1) * 128],
                                                                        ALU.mult, ALU.add),
                         reads=[bU[hd], bgam, bpsU], writes=[bU[hd]])
            if bg is not None:
                next(bg, None)
                next(bg, None)
        if bg is not None:
            for _ in bg:
                pass
        return held

    bob = bufs("ob", NT)
    def p1_prep_gen(ti):
        yield from load_x_tile_gen(ti, nxt=(ti - 1 if ti > 0 else None))
        yield from rmsnorm_gen(lambda k: hT[:, k, H0:H0 + T], bh, "g_mix", T, nTk, bn)
        S.dma("sp", nts_d[ti], nT[:], reads=bn, writes=[bnts[ti]])
        yield

    def p1_prep(ti):
        for _ in p1_prep_gen(ti):
            pass

    for idx, ti in enumerate(range(NT - 1, -1, -1)):
        tok0 = ti * T
        if idx == 0:
            p1_prep(ti)
        S.dma("sp", posi[:], pos_d[:, tok0:tok0 + T].partition_broadcast(128), writes=[bposi])
        posf, bposf = tmpf()
        S.op("dve", lambda e: e.tensor_copy(posf, posi[:]), reads=[bposi], writes=[bposf])
        ct, bct = rot[:, 0, :], brot[0]
        sn, bsn = rot[:, 1, :], brot[1]
        for tb, btb, ph in ((ct, bct, "phase_c"), (sn, bsn, "phase_s")):
            S.op("dve", lambda e, tb=tb, ph=ph: e.tensor_scalar(tb, posf, cs("invf"), cs(ph), ALU.mult, ALU.add), reads=[bposf, bcon], writes=[btb])
            S.op("dve", lambda e, tb=tb: e.tensor_scalar(nint[:], tb, float(1.0 / (2 * np.pi)), None, ALU.mult), reads=[btb], writes=[bnint])
            nf_, bnf_ = tmpf()
            S.op("dve", lambda e: e.tensor_copy(nf_, nint[:]), reads=[bnint], writes=[bnf_])
            S.op("dve", lambda e, tb=tb: e.scalar_tensor_tensor(tb, nf_, float(-2 * np.pi), tb, ALU.mult, ALU.add), reads=[bnf_, btb], writes=[btb])
            S.op("dve", lambda e, tb=tb: e.tensor_scalar(tb, tb, -3.1415925, 3.1415925, ALU.max, ALU.min), reads=[btb], writes=[btb])
            S.op("act", lambda e, tb=tb: e.activation(tb, tb, AF.Sin), reads=[btb], writes=[btb])

        def kv_hook(ti=ti, tok0=tok0):
            wt, bwt = ring_next("ka4")
            for kvh in range(2):
                ps1, bps1 = proj_F(wt, bwt, kvh, nTk, bn, T)
                ps2, bps2 = proj_F(wt, bwt, 2 + kvh, nTk, bn, T)
                t1, bt1 = tmpf()
                t2, bt2 = tmpf()
                S.op("dve", lambda e: e.tensor_tensor(t1, ps1[:, :], ct, ALU.mult), reads=[bps1, bct], writes=[bt1])
                S.op("dve", lambda e: e.tensor_tensor(t2, ps2[:, :], sn, ALU.mult), reads=[bps2, bsn], writes=[bt2])
                S.op("dve", lambda e, kvh=kvh: e.tensor_tensor(KT[:, kvh, tok0:tok0 + T], t1, t2, ALU.add), reads=[bt1, bt2], writes=[bKT[ti]])
            wt, bwt = ring_next("va")
            for blk in range(4):
                ps, bps = psum_next()
                for k in range(KC):
                    S.op("pe", lambda e, k=k, blk=blk: e.matmul(ps[:, 0:128], nT[:, k, blk * 128:(blk + 1) * 128], wt[:, k * 128:(k + 1) * 128],
                                                                start=(k == 0), stop=(k == KC - 1)),
                         reads=[bwt, bn[k]], writes=[bps], inc=(k == KC - 1))
                S.op("act", lambda e, blk=blk: e.copy(Vst[:, ti * 4 + blk, :, 0:64], ps[:, 0:128].rearrange("p (a b) -> p a b", a=2)),
                     reads=[bps], writes=[bV[ti]])
        held = hgrn_tile(ti, 1, idx == 0, mid_hook=kv_hook, late_hook=((lambda ti=ti: p1_prep(ti - 1)) if ti > 0 else None))
        for hd in range(4):
            S.op("act", lambda e, hd=hd: e.copy(HG[:, KK, hd, :], PS[held[hd]][:, :]), reads=[bPS[held[hd]]], writes=[bHG[KK][hd]])
        psum_release(held)
        S.dma("sp", ob_d[:, :, tok0:tok0 + T], HG[:, KK], reads=bHG[KK], writes=[bob[ti]])
        if idx == 0:
            tap("ob", HG[:, KK], bHG[KK], [128, 4, T])
    if stop_after == "phase1":
        tap("KT", KT[:], bKT, [128, 2, SEQ])
        tap("Vst", Vst[:], bV, [128, NBLK, 2, VW])
        return finish(nc, S, dbg_out)

    for half in range(2):
        S.dma("sp", xt[:, half, :], mem_d[128 * half:128 * (half + 1), :], writes=[bxt[half]])
        for h2 in range(2):
            ps, bps = psum_next()
            for kk in range(4):
                k = h2 * 4 + kk
                S.op("pe", lambda e, kk=kk, k=k: e.transpose(ps[:, kk * 128:(kk + 1) * 128], xt[:, half, k * 128:(k + 1) * 128], cs("ident")),
                     reads=[bxt[half], bcon], writes=[bps], inc=(kk == 3))
            S.op("act", lambda e: e.copy(hT[:, h2 * 4:h2 * 4 + 4, H0 + 128 * half: H0 + 128 * (half + 1)],
                                         ps[:, :].rearrange("p (a b) -> p a b", a=4)), reads=[bps], writes=bh[h2 * 4:h2 * 4 + 4])
    rmsnorm(lambda k: hT[:, k, H0:H0 + MEM], bh, "g_memkv", MEM, lambda k: nT[:, k, 0:MEM], bn)
    for w2 in range(2):
        wt, bwt = ring_next(f"wmk{w2}")
        for jb in range(4):
            j = w2 * 4 + jb
            ps, bps = proj_F(wt, bwt, jb, lambda k: nT[:, k, 0:MEM], bn, MEM)
            S.op("act", lambda e, j=j: e.copy(KmT[:, j, :], ps[:, 0:MEM]), reads=[bps], writes=[bKm])
    for w2 in range(2):
        wt, bwt = ring_next(f"wmv{w2}")
        for mc in range(2):
            ps, bps = psum_next()
            for k in range(KC):
                S.op("pe", lambda e, k=k: e.matmul(ps[:, :], nT[:, k, mc * 128:(mc + 1) * 128], wt[:, k * 512:(k + 1) * 512],
                                                   start=(k == 0), stop=(k == KC - 1)),
                     reads=[bwt, bn[k]], writes=[bps], inc=(k == KC - 1))
            S.op("act", lambda e: e.copy(Vm[:, mc, w2 * 512:(w2 + 1) * 512], ps[:, :]), reads=[bps], writes=[bVm])
    prefetch_x(0)
    tap("KmT", KmT[:], [bKm], [128, 8, MEM])
    tap("Vm", Vm[:], [bVm], [128, 2, D])

    OG, QR, OA = 8, 12, 16
    final_toks = []
    hwin = lambda k, N=T: hT[:, k, H0 - 1:H0 - 1 + N]

    def ffn_finish(ti, N, c0, row0, skip_first, mid=None):
        g_ = rmsnorm_gen(lambda k: hT[:, k, c0:c0 + N], bh, "g_fin", N, lambda k: hT[:, k, c0:c0 + N], bh)
        next(g_, None)
        next(g_, None)
        next(g_, None)
        if mid is not None:
            mid()
        for _ in g_:
            pass
        nb = (N + 127) // 128
        for blk in range(nb):
            w = min(128, N - 128 * blk)
            sl = blk % 2
            r0 = row0 + 128 * blk
            for half in range(2):
                ps, bps = psum_next()
                for kk in range(4):
                    k = half * 4 + kk
                    S.op("pe", lambda e, kk=kk, k=k: e.transpose(ps[0:w, kk * 128:(kk + 1) * 128], hT[:, k, c0 + 128 * blk: c0 + 128 * blk + w], cs("ident")),
                         reads=[bh[k], bcon], writes=[bps], inc=(kk == 3))
                S.op("act" if half == 0 else "dve",
                     (lambda e: e.copy(ot[0:w, half, :], ps[0:w, :])) if half == 0 else
                     (lambda e: e.tensor_copy(ot[0:w, half, :], ps[0:w, :])),
                     reads=[bps], writes=[bot[half]])
                cols = slice(half * 512, (half + 1) * 512)
                if skip_first and blk == 0:
                    tok = S.dma("sp", out_d[r0 + 1:r0 + w, cols], ot[1:w, half, :], reads=[bot[half]])
                else:
                    tok = S.dma("sp", out_d[r0:r0 + w, cols], ot[0:w, half, :], reads=[bot[half]])
                final_toks.append(tok)

    for ti in range(NT):
        tok0 = ti * T
        if ti == 0:
            S.dma("sp", nT[:], nts_d[0], reads=[bnts[0]], writes=bn)
        load_x_tile(ti, nxt=(ti + 1 if ti + 1 < NT else None))
        S.dma("sp", posi[:], pos_d[:, tok0:tok0 + T].partition_broadcast(128), writes=[bposi])
        posf, bposf = tmpf()
        S.op("dve", lambda e: e.tensor_copy(posf, posi[:]), reads=[bposi], writes=[bposf])
        ct, bct = rot[:, 0, :], brot[0]
        sn, bsn = rot[:, 1, :], brot[1]
        for tb, btb, ph in ((ct, bct, "phase_c"), (sn, bsn, "phase_s")):
            S.op("dve", lambda e, tb=tb, ph=ph: e.tensor_scalar(tb, posf, cs("invf"), cs(ph), ALU.mult, ALU.add), reads=[bposf, bcon], writes=[btb])
            S.op("dve", lambda e, tb=tb: e.tensor_scalar(nint[:], tb, float(1.0 / (2 * np.pi)), None, ALU.mult), reads=[btb], writes=[bnint])
            nf_, bnf_ = tmpf()
            S.op("dve", lambda e: e.tensor_copy(nf_, nint[:]), reads=[bnint], writes=[bnf_])
            S.op("dve", lambda e, tb=tb: e.scalar_tensor_tensor(tb, nf_, float(-2 * np.pi), tb, ALU.mult, ALU.add), reads=[bnf_, btb], writes=[btb])
            S.op("dve", lambda e, tb=tb: e.tensor_scalar(tb, tb, -3.1415925, 3.1415925, ALU.max, ALU.min), reads=[btb], writes=[btb])
            S.op("act", lambda e, tb=tb: e.activation(tb, tb, AF.Sin), reads=[btb], writes=[btb])

        def attention_qb(qb):
            gb = ti * 4 + qb
            kbs = [kb for kb in (gb - 1, gb, gb + 1) if 0 <= kb < NBLK]
            s0 = kbs[0] - (gb - 1)
            ns = len(kbs)
            pso2 = [psum_hold(1)[0], psum_hold(1)[0]]
            def att_scores(h):
                kvh, j, p0 = h // 4, h // 2, 64 * (h % 2)
                ps, bps = psum_next()
                for kb in kbs:
                    s_ = kb - (gb - 1)
                    S.op("pe", lambda e, s_=s_, kb=kb: e.matmul(ps[:, s_ * 128:(s_ + 1) * 128], KT[p0:p0 + 64, kvh, kb * 128:(kb + 1) * 128],
                                                                X1[p0:p0 + 64, QR + j, qb * 128:(qb + 1) * 128], start=True, stop=True),
                         reads=[bKT[kb // 4], bX[QR + j]], writes=[bps], inc=(kb == kbs[-1]))
                pi_ = (qb * 8 + h) % NPT
                S.op("act", lambda e: e.activation(PT[:, pi_, s0:s0 + ns, :], ps[:, s0 * 128:(s0 + ns) * 128].rearrange("p (a b) -> p a b", a=ns),
                                                   AF.Exp, scale=0.125), reads=[bps], writes=[bPT[pi_]])
                S.op("dve", lambda e: e.tensor_tensor(PT[:, pi_, s0:s0 + ns, :], PT[:, pi_, s0:s0 + ns, :], bandb[:, s0:s0 + ns, :], ALU.mult),
                     reads=[bPT[pi_], bcst], writes=[bPT[pi_]])

            def att_pv(h):
                kvh = h // 4
                pi_ = (qb * 8 + h) % NPT
                po, bpo = PS[pso2[h // 4]], bPS[pso2[h // 4]]
                hh = h % 4
                for kb in kbs:
                    s_ = kb - (gb - 1)
                    S.op("pe", lambda e, s_=s_, kb=kb: e.matmul(po[:, hh * 65:hh * 65 + 65], PT[:, pi_, s_, :], Vst[:, kb, kvh, 0:65],
                                                                start=(kb == kbs[0]), stop=(kb == kbs[-1])),
                         reads=[bPT[pi_], bV[kb // 4]], writes=[bpo], inc=(kb == kbs[-1]))

            for h in range(8):
                att_scores(h)
                if h > 1:
                    att_pv(h - 2)
            att_pv(6)
            att_pv(7)
            osl = qb % 2
            for hb in range(2):
                po, bpo = PS[pso2[hb]], bPS[pso2[hb]]
                pv = po[:, 0:260].rearrange("p (h c) -> p h c", c=65)
                S.op("dve", lambda e: e.tensor_tensor(dsm[:, 0, hb * 4:hb * 4 + 4].unsqueeze(2), pv[:, :, 64:65], sm[:, 24 + hb * 4:28 + hb * 4].unsqueeze(2), ALU.add),
                     reads=[bpo, bsm], writes=[bdsm[0]])
                S.op("dve", lambda e: e.reciprocal(dsm[:, 1, hb * 4:hb * 4 + 4], dsm[:, 0, hb * 4:hb * 4 + 4]), reads=[bdsm[0]], writes=[bdsm[1]])
                S.op("dve", lambda e: e.tensor_tensor(oatok[:, osl, hb * 256:(hb + 1) * 256].rearrange("p (h c) -> p h c", c=64), pv[:, :, 0:64],
                                                      dsm[:, 1, hb * 4:hb * 4 + 4].unsqueeze(2).to_broadcast([128, 4, 64]), ALU.mult),
                     reads=[bpo, bdsm[1]], writes=[boatok[osl]])
            psum_release(pso2)
            ps, bps = psum_next()
            psb = ps[:, :].bitcast(BF16)
            for kc in range(4):
                S.op("pe", lambda e, kc=kc: e.transpose(psb[:, kc * 128:(kc + 1) * 128], oatok[:, osl, kc * 128:(kc + 1) * 128], identb[:]),
                     reads=[boatok[osl], bcst], writes=[bps], inc=(kc == 3))
            S.op("act", lambda e: e.copy(X1[:, OA:OA + 4, qb * 128:(qb + 1) * 128], psb[:, 0:512].rearrange("p (a b) -> p a b", a=4)),
                 reads=[bps], writes=bX[OA:OA + 4])

        def mix_hook():
            wt, bwt = ring_next("gr")
            for hd in range(4):
                psg, bpsg = proj_F(wt, bwt, hd, nTk, bn, T)
                S.op("act", lambda e, hd=hd: e.activation(B8[:, 4 + hd, :], psg[:, :], AF.Silu), reads=[bpsg], writes=[bB[4 + hd]])
            for jj in range(2):
                wt, bwt = ring_next(f"qq{jj}")
                for sub in range(2):
                    j = 2 * jj + sub
                    ps1, bps1 = proj_F(wt, bwt, 2 * sub, nTk, bn, T)
                    ps2, bps2 = proj_F(wt, bwt, 2 * sub + 1, nTk, bn, T)
                    t1, bt1 = tmpf()
                    t2, bt2 = tmpf()
                    S.op("dve", lambda e: e.tensor_tensor(t1, ps1[:, :], ct, ALU.mult), reads=[bps1, bct], writes=[bt1])
                    S.op("dve", lambda e: e.tensor_tensor(t2, ps2[:, :], sn, ALU.mult), reads=[bps2, bsn], writes=[bt2])
                    S.op("dve", lambda e, j=j: e.tensor_tensor(X1[:, QR + j, :], t1, t2, ALU.add), reads=[bt1, bt2], writes=[bX[QR + j]])

        def mix_hook2():
            mix_hook()
            attention_qb(0)
            attention_qb(1)
            attention_qb(2)
            attention_qb(3)

        held = hgrn_tile(ti, 0, ti == 0, mid_hook=mix_hook2, front_done=(ti > 0))
        S.dma("sp", HG[:, LF], ob_d[:, :, tok0:tok0 + T], reads=[bob[ti]], writes=bHG[LF])
        for hd in range(4):
            S.op("dve", lambda e, hd=hd: e.tensor_tensor(HG[:, KK, hd, :], PS[held[hd]][:, :], HG[:, LF, hd, :], ALU.add),
                 reads=[bPS[held[hd]], bHG[LF][hd]], writes=[bHG[KK][hd]])
        psum_release(held)
        if ti == 0:
            tap("osum", HG[:, KK], bHG[KK], [128, 4, T])
        def o_norm(hd):
            S.op("act", lambda e: e.activation(B8[:, hd, :], HG[:, KK, hd, :], AF.Square), reads=[bHG[KK][hd]], writes=[bB[hd]])
            ps, bps = psum_next()
            S.op("pe", lambda e: e.matmul(ps[:, :], onesV[:], B8[:, hd, :], start=True, stop=True), reads=[bB[hd], bcst], writes=[bps])
            rs, brs = tmpf()
            S.op("act", lambda e: e.activation(rs, ps[:, :], AF.Ln, bias=sm[:, 40:41]), reads=[bps, bsm], writes=[brs])
            S.op("act", lambda e: e.activation(rs, rs, AF.Exp, scale=-0.5), reads=[brs], writes=[brs])
            S.op("dve", lambda e: e.tensor_tensor(rs, rs, HG[:, KK, hd, :], ALU.mult), reads=[brs, bHG[KK][hd]], writes=[brs])
            S.op("dve", lambda e: e.scalar_tensor_tensor(X1[:, OG + hd, :], rs, pr("hgn", hd, hd + 1), B8[:, 4 + hd, :], ALU.mult, ALU.mult),
                 reads=[brs, bB[4 + hd], bpar], writes=[bX[OG + hd]])
        if ti == 0:
            tap("qrT", X1[:, QR:QR + 4, :], bX[QR:QR + 4], [128, 4, T])
        o_norm(0)
        o_norm(1)
        o_norm(2)
        o_norm(3)
        if ti == 0:
            tap("og", X1[:, OG:OG + 4, :], bX[OG:OG + 4], [128, 4, T])
        if ti == 0:
            tap("oaT", X1[:, OA:OA + 4, :], bX[OA:OA + 4], [128, 4, T])
        for j in range(8):
            wt, bwt = ring_next(f"mg{j}")
            psr, bpsr = proj_F(wt, bwt, 0, nTk, bn, T)
            psa, bpsa = proj_F(wt[:, 1024:], bwt, 0, nTk, bn, T)
            sr, bsr = tmpf()
            sa, bsa = tmpf()
            S.op("act", lambda e: e.activation(sr, psr[:, :], AF.Sigmoid), reads=[bpsr], writes=[bsr])
            S.op("act", lambda e: e.activation(sa, psa[:, :], AF.Sigmoid), reads=[bpsa], writes=[bsa])
            pyr, bpyr = proj_F(wt[:, 2048:], bwt, 0, lambda k: X1[:, OG + k, :], bX[OG:OG + 4], T, nk=4)
            pya, bpya = proj_F(wt[:, 2560:], bwt, 0, lambda k: X1[:, OA + k, :], bX[OA:OA + 4], T, nk=4)
            S.op("dve", lambda e: e.tensor_tensor(sr, pyr[:, :], sr, ALU.mult), reads=[bpyr, bsr], writes=[bsr])
            S.op("dve", lambda e: e.tensor_tensor(sa, pya[:, :], sa, ALU.mult), reads=[bpya, bsa], writes=[bsa])
            S.op("dve", lambda e, j=j: e.tensor_tensor(B8[:, j, :], sr, sa, ALU.add), reads=[bsr, bsa], writes=[bB[j]])
        if ti == 0:
            tap("merged", B8[:], bB, [128, 8, T])
        for w2 in range(2):
            wt, bwt = ring_next(f"wout{w2}")
            for jb in range(4):
                j = w2 * 4 + jb
                ps, bps = proj_F(wt, bwt, jb, lambda k: B8[:, k, :], bB, T)
                S.op("dve", lambda e, j=j: e.tensor_tensor(hT[:, j, H0:H0 + T], hT[:, j, H0:H0 + T], ps[:, :], ALU.add), reads=[bps, bh[j]], writes=[bh[j]])
        if ti == 0:
            tap("h1", hT[:], bh, [128, KC, HWID])
        rmsnorm(lambda k: hT[:, k, H0:H0 + T], bh, "g_mem", T, nTk, bn)
        QM, OM = 0, 8
        for w2 in range(2):
            wt, bwt = ring_next(f"wmq{w2}")
            for jb in range(4):
                j = w2 * 4 + jb
                ps, bps = proj_F(wt, bwt, jb, nTk, bn, T)
                S.op("act", lambda e, j=j: e.copy(X1[:, QM + j, :], ps[:, :]), reads=[bps], writes=[bX[QM + j]])
        def mem_scores(h):
            pb = 16 + 2 * (h % 2)
            for mc in range(2):
                ps, bps = psum_next()
                for dc in range(2):
                    S.op("pe", lambda e, dc=dc: e.matmul(ps[:, :], KmT[:, 2 * h + dc, mc * 128:(mc + 1) * 128], X1[:, QM + 2 * h + dc, :],
                                                         start=(dc == 0), stop=(dc == 1)),
                         reads=[bKm, bX[QM + 2 * h + dc]], writes=[bps], inc=(dc == 1))
                S.op("act", lambda e: e.activation(X1[:, pb + mc, :], ps[:, :], AF.Exp, scale=1.0 / 16), reads=[bps], writes=[bX[pb + mc]])

        def mem_pv(h):
            pb = 16 + 2 * (h % 2)
            psd, bpsd = psum_next()
            for mc in range(2):
                S.op("pe", lambda e: e.matmul(psd[:, :], ones1[:], X1[:, pb + mc, :], start=(mc == 0), stop=(mc == 1)),
                     reads=[bX[pb + mc], bcst], writes=[bpsd], inc=(mc == 1))
            rd, brd = tmpf()
            S.op("dve", lambda e: e.reciprocal(rd, psd[:, :]), reads=[bpsd], writes=[brd])
            for dc in range(2):
                ps, bps = psum_next()
                for mc in range(2):
                    S.op("pe", lambda e: e.matmul(ps[:, :], Vm[:, mc, h * 256 + dc * 128: h * 256 + (dc + 1) * 128], X1[:, pb + mc, :],
                                                  start=(mc == 0), stop=(mc == 1)),
                         reads=[bVm, bX[pb + mc]], writes=[bps], inc=(mc == 1))
                S.op("dve", lambda e: e.tensor_tensor(X1[:, OM + 2 * h + dc, :], ps[:, :], rd, ALU.mult), reads=[bps, brd], writes=[bX[OM + 2 * h + dc]])

        for h in range(4):
            mem_scores(h)
            if h > 0:
                mem_pv(h - 1)
        mem_pv(3)
        for w2 in range(2):
            wt, bwt = ring_next(f"wmo{w2}")
            for jb in range(4):
                j = w2 * 4 + jb
                ps, bps = proj_F(wt, bwt, jb, lambda k: X1[:, OM + k, :], bX[OM:OM + 8], T)
                S.op("dve", lambda e, j=j: e.tensor_tensor(hT[:, j, H0:H0 + T], hT[:, j, H0:H0 + T], ps[:, :], ALU.add), reads=[bps, bh[j]], writes=[bh[j]])
        if ti == 0:
            tap("h2", hT[:], bh, [128, KC, HWID])
        rmsnorm(lambda k: hT[:, k, H0:H0 + T], bh, "g_ffn", T, nTk, bn)
        cw = lambda j_, jf: pr("conv_w", j_ * JF + jf, j_ * JF + jf + 1)
        pend = []

        def ffn_B(jf, c_, bc_, psu, bpsu):
            S.op("act", lambda e: e.activation(c_, c_, AF.Silu), reads=[bc_], writes=[bc_])
            S.op("dve", lambda e: e.tensor_tensor(X1[:, jf, 1:T], c_[:, 1:T], psu[:, 0:T - 1], ALU.mult), reads=[bc_, bpsu], writes=[bX[jf]])
            S.op("dve", lambda e: e.tensor_tensor(X1[:, jf, 0:1], c_[:, 0:1], ucar[:, jf:jf + 1], ALU.mult), reads=[bc_, bucar[jf]], writes=[bX[jf]])
            S.op("dve", lambda e: e.tensor_copy(ucar[:, jf:jf + 1], psu[:, T - 1:T]), reads=[bpsu], writes=[bucar[jf]])

        for jj in range(JF // 2):
            wt, bwt = ring_next(f"up{jj}")
            for sub in range(2):
                jf = 2 * jj + sub
                psu, bpsu = proj_F(wt, bwt, 2 * sub, nTk, bn, T)
                psg, bpsg = proj_F(wt, bwt, 2 * sub + 1, nTk, bn, T)
                gi = jf % NGS
                g_, bg_ = gs[:, gi, :], bgs[gi]
                S.op("act", lambda e: e.copy(g_[:, 2:T + 2], psg[:, :]), reads=[bpsg], writes=[bg_])
                S.op("dve", lambda e: e.tensor_copy(g_[:, 0:2], gcar[:, jf, :]), reads=[bgcar[jf]], writes=[bg_])
                S.op("dve", lambda e: e.tensor_copy(gcar[:, jf, :], g_[:, T:T + 2]), reads=[bg_], writes=[bgcar[jf]])
                c_, bc_ = tmpf()
                S.op("dve", lambda e: e.tensor_scalar(c_, g_[:, 0:T], cw(0, jf), pr("conv_b", jf, jf + 1), ALU.mult, ALU.add), reads=[bg_, bpar], writes=[bc_])
                S.op("dve", lambda e: e.scalar_tensor_tensor(c_, g_[:, 1:T + 1], cw(1, jf), c_, ALU.mult, ALU.add), reads=[bg_, bpar, bc_], writes=[bc_])
                S.op("dve", lambda e: e.scalar_tensor_tensor(c_, g_[:, 2:T + 2], cw(2, jf), c_, ALU.mult, ALU.add), reads=[bg_, bpar, bc_], writes=[bc_])
                pend.append((jf, c_, bc_, psu, bpsu))
                if len(pend) > 1:
                    ffn_B(*pend.pop(0))
        while pend:
            ffn_B(*pend.pop(0))
        if ti + 1 < NT:
            S.dma("sp", nT[:], nts_d[ti + 1], reads=[bnts[ti + 1]], writes=bn)
        if ti == 0:
            tap("act", X1[:], bX, [128, JF, T])
        for j in range(8):
            wt, bwt = ring_next(f"dn{j}")
            ps, bps = proj_F(wt, bwt, 0, lambda k: X1[:, k, :], bX, T, nk=JF)
            S.op("dve", lambda e, j=j: e.tensor_tensor(hwin(j), hwin(j), ps[:, :], ALU.add), reads=[bps, bh[j]], writes=[bh[j]])
        ffn_finish(ti, T, H0 - 1, tok0 - 1, skip_first=(ti == 0), mid=((lambda: hg_front(0)) if ti + 1 < NT else None))
        S.op("pool", lambda e: e.tensor_copy(hT[:, :, H0 - 1:H0], hT[:, :, H0 + T - 1:H0 + T]), reads=bh, writes=bh)

    cwa = lambda j_: pr("conv_w", j_ * JF, (j_ + 1) * JF)
    S.op("dve", lambda e: e.tensor_tensor(small[:, 0, :], gcar[:, :, 0], cwa(0), ALU.mult), reads=bgcar + [bpar], writes=[bsmall[0]])
    S.op("dve", lambda e: e.tensor_tensor(small[:, 1, :], gcar[:, :, 1], cwa(1), ALU.mult), reads=bgcar + [bpar], writes=[bsmall[1]])
    S.op("dve", lambda e: e.tensor_tensor(small[:, 0, :], small[:, 0, :], small[:, 1, :], ALU.add), reads=[bsmall[0], bsmall[1]], writes=[bsmall[0]])
    S.op("dve", lambda e: e.tensor_tensor(small[:, 2, :], small[:, 0, :], pr("conv_b"), ALU.add), reads=[bsmall[0], bpar], writes=[bsmall[2]])
    S.op("act", lambda e: e.activation(small[:, 3, :], small[:, 2, :], AF.Silu), reads=[bsmall[2]], writes=[bsmall[3]])
    S.op("dve", lambda e: e.tensor_tensor(actl[:], small[:, 3, :], ucar[:], ALU.mult), reads=[bsmall[3]] + bucar, writes=[bactl])
    for j in range(8):
        wt, bwt = ring_next(f"dn{j}")
        ps, bps = psum_next()
        for kc in range(JF):
            S.op("pe", lambda e, kc=kc: e.matmul(ps[:, 0:1], wt[:, kc * 128:(kc + 1) * 128], actl[:, kc:kc + 1], start=(kc == 0), stop=(kc == JF - 1)),
                 reads=[bwt, bactl], writes=[bps], inc=(kc == JF - 1))
        S.op("dve", lambda e, j=j: e.tensor_tensor(hT[:, j, H0 - 1:H0], hT[:, j, H0 - 1:H0], ps[:, 0:1], ALU.add), reads=[bps, bh[j]], writes=[bh[j]])
    ffn_finish(NT, 1, H0 - 1, SEQ - 1, skip_first=False)
    return finish(nc, S, dbg_out, final_toks)


def finish(nc, S, dbg_out, final_toks=()):
    for tok in list(dbg_out.values()) + list(final_toks):
        S.wait_tok("sp", tok)
    return nc, S, dbg_out


_CACHE = {}


def _get_program(SEQ):
    if SEQ not in _CACHE:
        _CACHE[SEQ] = build_program(SEQ)
    return _CACHE[SEQ]


def kernel(**inputs):
    inp = {k: np.asarray(v) for k, v in inputs.items()}
    x = inp["x"].astype(np.float32, copy=False)
    B, SEQ, _ = x.shape
    wall, index, bounds = build_wall(inp)
    par, _ = build_params(inp)
    con, _ = build_consts()
    nc, S, _ = _get_program(SEQ)
    in_maps = []
    for b in range(B):
        in_maps.append({
            "x": np.ascontiguousarray(x[b]),
            "mem": np.ascontiguousarray(inp["mem"][b].astype(np.float32, copy=False)),
            "pos": np.ascontiguousarray(inp["positions"][b].astype(np.int32, copy=False).reshape(1, SEQ)),
            "wall": wall, "par": par, "con": con,
        })
    res = run_bass_kernel_spmd(nc, in_maps, core_ids=list(range(B)))
    return np.stack([np.asarray(r["out"], dtype=np.float32) for r in res.results], axis=0)
```
